# Optimizing a Trainium2 kernel written in Bass

```python
import math
import jax, jax.numpy as jnp
from jax import lax
import numpy as np

D_MODEL = 1024
BATCH = 16
SEQ = 4096
DEPTH = 1

CHUNK = 64
RET_HEADS = 4
RET_HEAD_DIM = 256
RET_WIDTH = RET_HEADS * RET_HEAD_DIM
ROPE_THETA = 10000.0
CONV_WIDTH = D_MODEL
CONV_TAPS = 31
N_BRANCHES = 2
IN_COLS = 4 * RET_WIDTH + 2 * CONV_WIDTH + N_BRANCHES * D_MODEL
PEER_HEADS = 8
PEER_QUERY_DIM = 256
PEER_HALF = PEER_QUERY_DIM // 2
N_KEYS = 128
N_EXPERTS = N_KEYS * N_KEYS
PEER_TOPK = 16
PEER_BLOCK = 128
DEEPNORM_ALPHA = (2.0 * DEPTH) ** 0.25
DEEPNORM_BETA = (8.0 * DEPTH) ** -0.25
LN_EPS = 1e-5

kernel_name = 'hybrid_retention_conformer_peer_block'


def _layer_norm(x, gain=None, bias=None):
    xf = x.astype(jnp.float32)
    mu = jnp.mean(xf, axis=-1, keepdims=True)
    var = jnp.mean(jnp.square(xf - mu), axis=-1, keepdims=True)
    y = (xf - mu) * lax.rsqrt(var + LN_EPS)
    if gain is not None:
        y = y * gain.astype(jnp.float32) + bias.astype(jnp.float32)
    return y.astype(x.dtype)


def _rotary(t, positions):
    half = t.shape[-1] // 2
    inv_freq = ROPE_THETA ** (-jnp.arange(half, dtype=jnp.float32) / half)
    ang = positions.astype(jnp.float32)[..., None] * inv_freq
    cos = jnp.cos(ang)[:, :, None, :]
    sin = jnp.sin(ang)[:, :, None, :]
    tf = t.astype(jnp.float32)
    t1, t2 = tf[..., :half], tf[..., half:]
    out = jnp.concatenate([t1 * cos - t2 * sin, t1 * sin + t2 * cos], axis=-1)
    return out.astype(t.dtype)


def _retention(q, k, v):
    B, S, H, Dh = q.shape
    n_chunks = S // CHUNK
    in_dtype = q.dtype
    q, k, v = (t.astype(jnp.float32) for t in (q, k, v))
    log_gamma = jnp.log(1.0 - 2.0 ** (-5.0 - jnp.arange(H, dtype=jnp.float32)))
    pos = jnp.arange(CHUNK, dtype=jnp.float32)
    rel = jnp.abs(pos[:, None] - pos[None, :])
    intra_decay = jnp.exp(log_gamma[:, None, None] * rel)
    q_decay = jnp.exp(log_gamma[None, :] * (pos[:, None] + 1.0))
    k_decay = jnp.exp(log_gamma[None, :] * (CHUNK - 1.0 - pos[:, None]))
    chunk_decay = jnp.exp(log_gamma * CHUNK)

    def step(state, inp):
        qi, ki, vi = inp
        s = jnp.einsum('bqhd,bkhd->bhqk', qi, ki) * intra_decay
        intra = jnp.einsum('bhqk,bkhd->bqhd', s, vi)
        cross = jnp.einsum('bqhd,bhde->bqhe', qi, state) * q_decay[None, :, :, None]
        new_state = state * chunk_decay[None, :, None, None] + jnp.einsum(
            'bkhd,kh,bkhe->bhde', ki, k_decay, vi)
        return new_state, intra + cross

    xs = tuple(t.reshape(B, n_chunks, CHUNK, H, Dh).swapaxes(0, 1) for t in (q, k, v))
    init = jnp.zeros((B, H, Dh, Dh), jnp.float32)
    _, out = lax.scan(step, init, xs)
    return out.swapaxes(0, 1).reshape(B, S, H, Dh).astype(in_dtype)


def _conv_module(a, dw, dw_b, ln_g, ln_b, w_pw2, b_pw2):
    u, g = jnp.split(a, 2, axis=-1)
    y = u * jax.nn.sigmoid(g)
    y = jnp.pad(y, ((0, 0), (CONV_TAPS - 1, 0), (0, 0)))
    y = lax.conv_general_dilated(
        y, dw[:, None, :].astype(y.dtype), window_strides=(1,), padding='VALID',
        dimension_numbers=('NWC', 'WIO', 'NWC'), feature_group_count=CONV_WIDTH) + dw_b
    y = jax.nn.silu(_layer_norm(y, ln_g, ln_b))
    return y @ w_pw2 + b_pw2


def _mixer(h, positions, w_in, conv_dw, conv_dw_b, conv_ln_g, conv_ln_b,
           w_conv_out, b_conv_out, w_out):
    B, S, _ = h.shape
    proj = h @ w_in
    cuts = [RET_WIDTH, 2 * RET_WIDTH, 3 * RET_WIDTH, 4 * RET_WIDTH,
            4 * RET_WIDTH + 2 * CONV_WIDTH, 4 * RET_WIDTH + 2 * CONV_WIDTH + D_MODEL]
    q, k, v, g_ret, conv_in, gate_a, gate_b = jnp.split(proj, cuts, axis=-1)
    hd = (B, S, RET_HEADS, RET_HEAD_DIM)
    q = _rotary(q.reshape(hd), positions)
    k = _rotary(k.reshape(hd), positions) * (RET_HEAD_DIM ** -0.5)
    ret = _layer_norm(_retention(q, k, v.reshape(hd)))
    ret = ret.reshape(B, S, RET_WIDTH) * jax.nn.silu(g_ret)
    conv = _conv_module(conv_in, conv_dw, conv_dw_b, conv_ln_g, conv_ln_b,
                        w_conv_out, b_conv_out)
    merged = jax.nn.sigmoid(gate_a) * ret + jax.nn.sigmoid(gate_b) * conv
    return merged @ w_out


def _peer(h, wq, subkeys, u_tab, v_tab):
    B, S, D = h.shape
    tokens = h.reshape(-1, PEER_BLOCK, D)

    def block(xb):
        blk = xb.shape[0]
        q = (xb @ wq).reshape(blk, PEER_HEADS, 2, PEER_HALF)
        s = jnp.einsum('thpd,hpkd->thpk', q, subkeys).astype(jnp.float32)
        top_s, top_i = lax.top_k(s, PEER_TOPK)
        cand_s = top_s[:, :, 0, :, None] + top_s[:, :, 1, None, :]
        cand_i = top_i[:, :, 0, :, None] * N_KEYS + top_i[:, :, 1, None, :]
        cand_s = cand_s.reshape(blk, PEER_HEADS, PEER_TOPK * PEER_TOPK)
        cand_i = cand_i.reshape(blk, PEER_HEADS, PEER_TOPK * PEER_TOPK)
        best_s, best_pos = lax.top_k(cand_s, PEER_TOPK)
        expert = jnp.take_along_axis(cand_i, best_pos, axis=-1)
        gate = jax.nn.softmax(best_s, axis=-1).astype(xb.dtype)
        u = u_tab[expert]
        act = jax.nn.gelu(jnp.einsum('td,thed->the', xb, u), approximate=False)
        v = v_tab[expert]
        return jnp.einsum('the,thed->td', gate * act, v)

    out = lax.map(block, tokens)
    return out.reshape(B, S, D)


def setup_inputs(seed: int = 0) -> dict:
    key = jax.random.key(seed)
    ks = jax.random.split(key, 24)
    f32 = jnp.float32

    def nrm(k, shape, scale):
        return jax.random.normal(k, shape, f32) * scale

    x = nrm(ks[0], (BATCH, SEQ, D_MODEL), 1.0)
    c = nrm(ks[1], (BATCH, D_MODEL), 1.0)
    offset = jax.random.randint(ks[2], (BATCH, 1), 0, 1024, dtype=jnp.int32)
    positions = offset + jnp.arange(SEQ, dtype=jnp.int32)[None, :]
    return {
        'x': x,
        'c': c,
        'positions': positions,
        'w_ada': nrm(ks[3], (DEPTH, D_MODEL, 6 * D_MODEL), 0.5 * D_MODEL ** -0.5),
        'b_ada': nrm(ks[4], (DEPTH, 6 * D_MODEL), 0.02),
        'w_in': nrm(ks[5], (DEPTH, D_MODEL, IN_COLS), D_MODEL ** -0.5),
        'conv_dw': nrm(ks[6], (DEPTH, CONV_TAPS, CONV_WIDTH), CONV_TAPS ** -0.5),
        'conv_dw_b': nrm(ks[7], (DEPTH, CONV_WIDTH), 0.02),
        'conv_ln_g': 1.0 + nrm(ks[8], (DEPTH, CONV_WIDTH), 0.02),
        'conv_ln_b': nrm(ks[9], (DEPTH, CONV_WIDTH), 0.02),
        'w_conv_out': nrm(ks[10], (DEPTH, CONV_WIDTH, D_MODEL), CONV_WIDTH ** -0.5),
        'b_conv_out': nrm(ks[11], (DEPTH, D_MODEL), 0.02),
        'w_out': nrm(ks[12], (DEPTH, D_MODEL, D_MODEL), DEEPNORM_BETA * D_MODEL ** -0.5),
        'ln1_g': 1.0 + nrm(ks[13], (DEPTH, D_MODEL), 0.02),
        'ln1_b': nrm(ks[14], (DEPTH, D_MODEL), 0.02),
        'peer_wq': nrm(ks[15], (DEPTH, D_MODEL, PEER_HEADS * PEER_QUERY_DIM), D_MODEL ** -0.5),
        'peer_subkeys': nrm(ks[16], (DEPTH, PEER_HEADS, 2, N_KEYS, PEER_HALF), PEER_HALF ** -0.5),
        'peer_u': nrm(ks[17], (DEPTH, N_EXPERTS, D_MODEL), D_MODEL ** -0.5),
        'peer_v': nrm(ks[18], (DEPTH, N_EXPERTS, D_MODEL), DEEPNORM_BETA * PEER_HEADS ** -0.5),
        'ln2_g': 1.0 + nrm(ks[19], (DEPTH, D_MODEL), 0.02),
        'ln2_b': nrm(ks[20], (DEPTH, D_MODEL), 0.02),
    }


def reference(x, c, positions, w_ada, b_ada, w_in, conv_dw, conv_dw_b, conv_ln_g,
              conv_ln_b, w_conv_out, b_conv_out, w_out, ln1_g, ln1_b, peer_wq,
              peer_subkeys, peer_u, peer_v, ln2_g, ln2_b):
    cond = jax.nn.silu(c)
    for l in range(DEPTH):
        mod = cond @ w_ada[l] + b_ada[l]
        sh1, sc1, g1, sh2, sc2, g2 = jnp.split(mod, 6, axis=-1)
        h1 = _layer_norm(x) * (1.0 + sc1[:, None, :]) + sh1[:, None, :]
        mix = _mixer(h1, positions, w_in[l], conv_dw[l], conv_dw_b[l], conv_ln_g[l],
                     conv_ln_b[l], w_conv_out[l], b_conv_out[l], w_out[l])
        x = _layer_norm(DEEPNORM_ALPHA * x + g1[:, None, :] * mix, ln1_g[l], ln1_b[l])
        h2 = _layer_norm(x) * (1.0 + sc2[:, None, :]) + sh2[:, None, :]
        ffn = _peer(h2, peer_wq[l], peer_subkeys[l], peer_u[l], peer_v[l])
        x = _layer_norm(DEEPNORM_ALPHA * x + g2[:, None, :] * ffn, ln2_g[l], ln2_b[l])
    return x
```

```python
import numpy as np
from contextlib import ExitStack
import concourse.bass as bass
import concourse.mybir as mybir
from concourse.bass_utils import run_bass_kernel_spmd

F32 = mybir.dt.float32; BF16 = mybir.dt.bfloat16; I32 = mybir.dt.int32
ALU = mybir.AluOpType; AF = mybir.ActivationFunctionType
ENGS = ['sync', 'scalar', 'vector', 'gpsimd', 'tensor']
D = 1024; KD = 8; NCOL = 8192
ALPHA = float(2.0 ** 0.25); EPS = 1e-5
NEG = -1e30


class Item:
    __slots__ = ('eng', 'fn', 'deps', 'needed', 'dma', 'semkey', 'count', 'sem')

    def __init__(s, eng, fn, dma, semkey):
        s.eng = eng; s.fn = fn; s.deps = []; s.needed = False; s.dma = dma
        s.semkey = semkey; s.count = 0; s.sem = None


class Rec:
    def __init__(s, nc):
        s.nc = nc; s.items = {e: [] for e in ENGS}; s.lastw = {}; s.readers = {}; s.all = []

    def op(s, eng, fn, reads=(), writes=(), dma=False, semkey=None):
        it = Item(eng, fn, dma, semkey)
        deps = {}
        for k in reads:
            w = s.lastw.get(k)
            if w is not None: deps[id(w)] = w
        for k in writes:
            w = s.lastw.get(k)
            if w is not None: deps[id(w)] = w
            for r in s.readers.get(k, ()): deps[id(r)] = r
        for d in deps.values():
            if d is it: continue
            if d.eng == eng and eng == 'tensor' and not d.dma: continue
            it.deps.append(d); d.needed = True
        for k in reads: s.readers.setdefault(k, []).append(it)
        for k in writes: s.lastw[k] = it; s.readers[k] = []
        s.items[eng].append(it); s.all.append(it)
        return it

    def barrier(s):
        lasts = [s.items[e][-1] for e in ENGS if s.items[e]]
        dmas = {}
        for it in s.all:
            if it.dma: dmas[it.semkey] = it
        for e in ENGS:
            it = Item(e, None, False, None)
            for d in lasts + list(dmas.values()):
                if d.fn is None: continue
                if d.eng == e and not d.dma: continue
                it.deps.append(d); d.needed = True
            s.items[e].append(it); s.all.append(it)
        s.lastw = {}; s.readers = {}

    def emit(s, stack, finals=()):
        nc = s.nc
        esem = {e: stack.enter_context(nc.semaphore('sem_' + e)) for e in ENGS}
        dsem = {}; dcnt = {}; cnt = {e: 0 for e in ENGS}
        for it in s.all:
            if it.dma:
                k = it.semkey
                if k not in dsem:
                    dsem[k] = stack.enter_context(nc.semaphore('dsem%d' % len(dsem))); dcnt[k] = 0
                dcnt[k] += 16; it.sem = dsem[k]; it.count = dcnt[k]; it.needed = True
            elif it.needed and it.fn is not None:
                cnt[it.eng] += 1; it.sem = esem[it.eng]; it.count = cnt[it.eng]
        block = stack.enter_context(nc.Block())

        def body(e):
            def run(eng):
                waited = {}
                for it in s.items[e]:
                    for d in it.deps:
                        if d.sem is None: continue
                        if waited.get(id(d.sem), 0) < d.count:
                            eng.wait_ge(d.sem, d.count); waited[id(d.sem)] = d.count
                    if it.fn is None: continue
                    ins = it.fn(eng)
                    if it.dma: ins.then_inc(it.sem, 16)
                    elif it.needed: ins.then_inc(it.sem, 1)
                if e == 'sync':
                    for d in finals:
                        if waited.get(id(d.sem), 0) < d.count:
                            eng.wait_ge(d.sem, d.count); waited[id(d.sem)] = d.count
            return run
        for e in ENGS:
            getattr(block, e)(body(e))


def host_consts():
    c = {}
    c['ident'] = np.eye(128, dtype=np.float32)
    c['iota'] = np.tile(np.arange(128, dtype=np.float32)[None, :], (128, 1))
    c['invf'] = (np.float32(10000.0) ** (-(np.arange(128, dtype=np.float32)) / np.float32(128))).astype(np.float32)[:, None]
    gam = [1.0 - 2.0 ** (-5.0 - h) for h in range(4)]
    idx = np.arange(128)
    mask = np.zeros((128, 4, 128), np.float32)
    qdec = np.zeros((128, 4, 2, 128), np.float32)
    kdec = np.zeros((128, 4), np.float32)
    for h in range(4):
        g = gam[h]
        m = (g ** np.abs(idx[:, None] - idx[None, :]).astype(np.float64)) * ((idx[:, None] // 64) <= (idx[None, :] // 64))
        mask[:, h, :] = (m / 16.0).astype(np.float32)
        qdec[:, h, :, :] = (g ** (idx + 1.0))[None, None, :]
        kdec[:, h] = (g ** (127.0 - idx)) / 16.0
    c['maskT'] = mask.reshape(128, 512)
    c['qdec'] = qdec.reshape(128, 1024)
    c['kdec'] = kdec
    c['gam128'] = [float(g ** 128) for g in gam]
    return c


def build(NB, S, debug=False):
    NT = S // 128
    NTOK = NB * S
    HC = host_consts()
    nc = bass.Bass('TRN2', target_bir_lowering=False)

    def din(name, shape, dt=F32): return nc.dram_tensor(name, shape, dt, kind='ExternalInput').ap()
    x_d = din('x', [NTOK, D]); cT_d = din('cT', [D, NB]); pos_d = din('pos', [NB, S], I32)
    wada_d = din('w_ada', [D, 6 * D]); bada_d = din('b_ada', [128, 48]); badar_d = din('b_ada_row', [1, 6 * D])
    win_d = din('w_in', [D, NCOL]); dwT_d = din('conv_dwT', [D, 31]); dwb_d = din('conv_dw_b', [1, D])
    clng_d = din('conv_ln_g', [128, KD]); clnb_d = din('conv_ln_b', [128, KD])
    wco_d = din('w_conv_out', [D, D]); bco_d = din('b_conv_out', [1, D]); wo_d = din('w_out', [D, D])
    ln1g_d = din('ln1_g', [D]); ln1b_d = din('ln1_b', [D]); ln2g_d = din('ln2_g', [D]); ln2b_d = din('ln2_b', [D])
    wq_d = din('peer_wq', [D, 2048]); skT_d = din('skT', [16, 128, 128])
    uT_d = din('peer_uT', [D, 16384]); v_d = din('peer_v', [16384, D])
    ident_d = din('ident', [128, 128]); iota_d = din('iota', [128, 128]); invf_d = din('invf', [128, 1])
    maskT_d = din('maskT', [128, 512]); qdec_d = din('qdec', [128, 1024]); kdec_d = din('kdec', [128, 4])
    y_d = nc.dram_tensor('y', [NTOK, D], F32, kind='ExternalOutput').ap()
    winbf_d = nc.dram_tensor('winbf', [128, KD, NCOL], BF16, kind='Internal').ap()
    ubf_d = nc.dram_tensor('ubf', [128, KD, 16384], BF16, kind='Internal').ap()
    vbf_d = nc.dram_tensor('vbf', [128, 128, D], BF16, kind='Internal').ap()
    mod_d = nc.dram_tensor('modd', [NB, 6 * D], F32, kind='Internal').ap()
    NTILE = NTOK // 128
    G_d = nc.dram_tensor('Gd', [NTILE * 2, 128, 128, 64], BF16, kind='Internal').ap()
    h2T_d = nc.dram_tensor('h2Td', [NTILE, 128, KD, 128], BF16, kind='Internal').ap()
    x1_d = nc.dram_tensor('x1d', [NTOK, D], F32, kind='ExternalOutput' if debug else 'Internal').ap()

    with ExitStack() as top:
        R = Rec(nc)
        PS = [top.enter_context(nc.psum_tensor('ps%d' % i, [128, 512], F32)) for i in range(8)]
        psc = [0]; pslo = [0]

        def nextps():
            i = pslo[0] + psc[0] % (8 - pslo[0]); psc[0] += 1
            return i

        def P(i): return ('ps', i)

        def dma(out, in_, reads=(), writes=(), semkey=None, eng='sync', **kw):
            return R.op(eng, lambda e: e.dma_start(out=out, in_=in_, **kw), reads=reads, writes=writes, dma=True, semkey=semkey)

        def V(fn, reads=(), writes=()): return R.op('vector', fn, reads, writes)
        def A(fn, reads=(), writes=()): return R.op('scalar', fn, reads, writes)
        def G(fn, reads=(), writes=()): return R.op('gpsimd', fn, reads, writes)
        def T(fn, reads=(), writes=()): return R.op('tensor', fn, reads, writes)

        def sbt(st, name, shape, dt): return st.enter_context(nc.sbuf_tensor('s_' + name, shape, dt))
        identf = sbt(top, 'identf', [128, 128], F32); identb = sbt(top, 'identb', [128, 128], BF16)
        ones = sbt(top, 'ones', [1, 128], BF16)
        modT = sbt(top, 'modT', [128, 48, NB], F32)
        sc1p = sbt(top, 'sc1p', [128, 8, NB], F32); sc2p = sbt(top, 'sc2p', [128, 8, NB], F32)
        dma(identf[:], ident_d, writes=['identf'], semkey='identf')
        V(lambda e: e.tensor_copy(out=identb[:], in_=identf[:]), ['identf'], ['identb'])
        V(lambda e: e.memset(ones[:], 1.0), [], ['ones'])
        epsc = sbt(top, 'epsc', [128, 1], F32)
        V(lambda e: e.memset(epsc[:], EPS), [], ['epsc'])

        with ExitStack() as p0:
            stgf = [sbt(p0, 'stgf%d' % i, [128, 4096], F32) for i in range(2)]
            stgb = [sbt(p0, 'stgb%d' % i, [128, 4096], BF16) for i in range(2)]
            cnt = [0]

            def conv_block(src_ap, dst_ap, shape3=None):
                i = cnt[0] % 2; cnt[0] += 1
                sf = stgf[i][:]; sbv = stgb[i][:]
                if shape3 is not None:
                    sf = sf.rearrange('p (a b) -> p a b', a=shape3); sbv = sbv.rearrange('p (a b) -> p a b', a=shape3)
                dma(sf, src_ap, writes=[('stgf', i)], semkey=('stgf', i))
                if i == 0:
                    V(lambda e: e.tensor_copy(out=stgb[i][:], in_=stgf[i][:]), [('stgf', i)], [('stgb', i)])
                else:
                    A(lambda e: e.copy(out=stgb[i][:], in_=stgf[i][:]), [('stgf', i)], [('stgb', i)])
                dma(dst_ap, sbv, reads=[('stgb', i)], semkey=('stgbo', i))
            for k in range(KD):
                for eb in range(4):
                    conv_block(uT_d[k * 128:(k + 1) * 128, eb * 4096:(eb + 1) * 4096], ubf_d[:, k, eb * 4096:(eb + 1) * 4096])
            vv = v_d.rearrange('(j p) d -> p j d', p=128)
            for jb in range(32):
                conv_block(vv[:, jb * 4:(jb + 1) * 4, :], vbf_d[:, jb * 4:(jb + 1) * 4, :], shape3=4)
            for k in range(KD):
                for cb in range(2):
                    conv_block(win_d[k * 128:(k + 1) * 128, cb * 4096:(cb + 1) * 4096], winbf_d[:, k, cb * 4096:(cb + 1) * 4096])

            cTs = sbt(p0, 'cTs', [128, KD, NB], F32); siluT = sbt(p0, 'siluT', [128, KD, NB], F32)
            badaT = sbt(p0, 'badaT', [128, 48], F32)
            wab = [sbt(p0, 'wab%d' % i, [128, KD, 512], F32) for i in range(2)]
            dma(cTs[:], cT_d.rearrange('(k p) b -> p k b', p=128), writes=['cTs'], semkey='cTs')
            dma(badaT[:], bada_d, writes=['badaT'], semkey='badaT')
            A(lambda e: e.activation(out=siluT[:], in_=cTs[:], func=AF.Silu), ['cTs'], ['siluT'])
            pm = nextps(); pm2 = [nextps(), nextps()]
            onesf = sbt(p0, 'onesf', [1, 8], F32); badar = sbt(p0, 'badar', [1, 6 * D], F32); modrow = sbt(p0, 'modrow', [NB, 6 * D], F32)
            V(lambda e: e.memset(onesf[:], 1.0), [], ['onesf'])
            dma(badar[:], badar_d, writes=['badar'], semkey='badar')
            for blk in range(12):
                wb_ = wab[blk % 2]
                dma(wb_[:], wada_d.rearrange('(k p) c -> p k c', p=128)[:, :, blk * 512:(blk + 1) * 512],
                    writes=[('wab', blk % 2)], semkey=('wab', blk % 2))
                q2 = pm2[blk % 2]
                for k in range(KD):
                    T(lambda e, wb_=wb_, k=k, q2=q2: e.matmul(PS[q2][0:NB, :], lhsT=siluT[:, k, :], rhs=wb_[:, k, :], start=(k == 0), stop=False),
                      [('wab', blk % 2), 'siluT'], [P(q2)])
                T(lambda e, q2=q2, blk=blk: e.matmul(PS[q2][0:NB, :], lhsT=onesf[0:1, 0:NB], rhs=badar[0:1, blk * 512:(blk + 1) * 512], start=False, stop=True),
                  ['onesf', 'badar'], [P(q2)])
                V(lambda e, q2=q2, blk=blk: e.tensor_copy(out=modrow[:, blk * 512:(blk + 1) * 512], in_=PS[q2][0:NB, :]), [P(q2)], ['modrow'])
                for c4 in range(4):
                    kk = blk * 4 + c4
                    for k in range(KD):
                        T(lambda e, wb_=wb_, k=k, c4=c4, kk=kk: e.matmul(
                            PS[pm][:, kk * NB:(kk + 1) * NB], lhsT=wb_[:, k, c4 * 128:(c4 + 1) * 128], rhs=siluT[:, k, :],
                            start=(k == 0), stop=(k == KD - 1)), [('wab', blk % 2), 'siluT'], [P(pm)])
            V(lambda e: e.tensor_tensor(out=modT[:], in0=PS[pm][:, 0:48 * NB].rearrange('p (k b) -> p k b', b=NB),
                                        in1=badaT[:].unsqueeze(2).to_broadcast([128, 48, NB]), op=ALU.add),
              [P(pm), 'badaT'], ['modT'])
            V(lambda e: e.tensor_scalar(out=sc1p[:], in0=modT[:, 8:16, :], scalar1=1.0, scalar2=None, op0=ALU.add), ['modT'], ['sc1p'])
            V(lambda e: e.tensor_scalar(out=sc2p[:], in0=modT[:, 32:40, :], scalar1=1.0, scalar2=None, op0=ALU.add), ['modT'], ['sc2p'])
            dma(mod_d, modrow[:], reads=['modrow'], semkey='modout')
            R.barrier()

        with ExitStack() as pa:
            def sa(name, shape, dt): return sbt(pa, name, shape, dt)
            XT = [sa('xt0', [128, D], F32)] * 2
            WB = [sa('wblk%d' % i, [128, KD, 512], BF16) for i in range(2)]
            DG = sa('dg', [128, KD, 31, 128], BF16)
            dwT = sa('dwT', [128, KD, 31], F32)
            WCO = sa('wco', [128, KD, D], BF16); WO = sa('wo', [128, KD, D], BF16)
            dwbr_f = sa('dwbr_f', [1, D], F32); dwbr = sa('dwbr', [1, D], BF16)
            bcor_f = sa('bcor_f', [1, D], F32); bcor = sa('bcor', [1, D], BF16)
            clng = sa('clng', [128, KD], F32); clnb = sa('clnb', [128, KD], F32)
            g1B = sa('g1B', [128, D], F32)
            invf = sa('invf', [128, 1], F32); maskT = sa('maskT', [128, 512], F32)
            qdecf = sa('qdecf', [128, 1024], F32) if False else None; qdec = sa('qdec', [128, 1024], BF16); kdec = sa('kdec', [128, 4], F32)
            st6 = sa('st6', [128, 2, 6], F32); mv = sa('mv', [128, 2], F32); rstd = sa('rstd', [128, 1], F32)
            st6h = sa('st6h', [128, 4, 6], F32); mvh = sa('mvh', [128, 4, 2], F32); rstdh = sa('rstdh', [128, 4], F32)
            xn = sa('xn', [128, D], F32); h1T = sa('h1T', [128, KD, 128], BF16)
            posi = sa('posi', [128, 128], I32); posf = sa('posf', [128, 128], F32); ang = sa('ang', [128, 128], F32)
            ki = sa('ki', [128, 128], I32); kf = sa('kf', [128, 128], F32); rr = sa('rr', [128, 128], F32)
            rc = sa('rc', [128, 128], F32); tmpa = sa('tmpa', [128, 128], F32)
            sinT = sa('sinT', [128, 128], F32); cosT = sa('cosT', [128, 128], F32)
            t1 = sa('t1', [128, 2, 128], F32); t2 = sa('t2', [128, 2, 128], F32)
            QT = sa('QT', [128, 4, 2, 128], BF16); KT = sa('KT', [128, 4, 2, 128], BF16); QTD = sa('QTD', [128, 4, 2, 128], BF16)
            Vb = sa('Vb', [128, D], BF16); SG = sa('SG', [128, D], F32); SA_ = sa('SA', [128, D], BF16); SBg = sa('SBg', [128, D], BF16)
            UT = sa('UT', [128, 1024], BF16); sgT = sa('sgT', [128, 512], BF16)
            ybuf = sa('ybuf', [128, KD, 158], BF16)
            SmT = sa('SmT', [128, 512], BF16); Kd = sa('Kd', [128, 4, 256], BF16)
            STF = sa('STF', [128, 4, 512], F32); STB = sa('STB', [128, 4, 2, 256], BF16)
            retn = sa('retn', [128, D], F32); MR = sa('MR', [128, D], F32)
            z = sa('z', [128, D], F32); stg = z; sT = sa('sT', [128, KD, 128], BF16)
            MG = z; MT = sa('MT', [128, KD, 128], BF16)
            x1p = retn
            tmpw = SG

            dma(invf[:], invf_d, writes=['invf'], semkey='c1'); dma(maskT[:], maskT_d, writes=['maskT'], semkey='c2')
            dma(z[:], qdec_d, writes=['z'], semkey='c3'); V(lambda e: e.tensor_copy(out=qdec[:], in_=z[:]), ['z'], ['qdec']); dma(kdec[:], kdec_d, writes=['kdec'], semkey='c4')
            dma(clng[:], clng_d, writes=['clng'], semkey='c5')
            dma(clnb[:], clnb_d, writes=['clnb'], semkey='c6')
            dma(dwbr_f[:], dwb_d, writes=['dwbr_f'], semkey='c9'); dma(bcor_f[:], bco_d, writes=['bcor_f'], semkey='c10')
            V(lambda e: e.tensor_copy(out=dwbr[:], in_=dwbr_f[:]), ['dwbr_f'], ['dwbr'])
            V(lambda e: e.tensor_copy(out=bcor[:], in_=bcor_f[:]), ['bcor_f'], ['bcor'])
            dma(dwT[:], dwT_d.rearrange('(k p) j -> p k j', p=128), writes=['dwT'], semkey='c11')
            for cc in range(KD):
                G(lambda e, cc=cc: e.tensor_tensor(out=DG[:, cc, :, :], in0=identb[:].unsqueeze(1).to_broadcast([128, 31, 128]),
                                                   in1=dwT[:, cc, :].unsqueeze(2).to_broadcast([128, 31, 128]), op=ALU.mult),
                  ['identb', 'dwT'], ['DG'])
            for k in range(KD):
                dma(stg[:], wco_d[k * 128:(k + 1) * 128, :], writes=['z'], semkey='stg')
                V(lambda e, k=k: e.tensor_copy(out=WCO[:, k, :], in_=stg[:]), ['z'], ['WCO'])
            for k in range(KD):
                dma(stg[:], wo_d[k * 128:(k + 1) * 128, :], writes=['z'], semkey='stg')
                V(lambda e, k=k: e.tensor_copy(out=WO[:, k, :], in_=stg[:]), ['z'], ['WO'])

            C1 = float(np.float32(6.28125)); C2 = float(np.float32(2 * np.pi - 6.28125)); PI = float(np.pi); TWO_PI = float(2 * np.pi)
            wcnt = [0]; itc = [0]
            for b in range(NB):
                dma(g1B[:], mod_d[b, 2048:3072].partition_broadcast(128), writes=['g1B'], semkey='g1B')
                G(lambda e: e.memset(STF[:], 0.0), [], ['STF']); G(lambda e: e.memset(STB[:], 0.0), [], ['STB'])
                G(lambda e: e.memset(ybuf[:, :, 0:30], 0.0), [], ['ybuf'])
                for t in range(NT):
                    g0 = b * S + t * 128
                    xi = 0; itc[0] += 1
                    xt = XT[xi]; XK = ('xt', xi)
                    dma(xt[:], x_d[g0:g0 + 128, :], writes=[XK], semkey=XK)
                    for hf in range(2):
                        V(lambda e, hf=hf: e.bn_stats(out=st6[:, hf, :], in_=xt[:, hf * 512:(hf + 1) * 512]), [XK], ['st6'])
                    V(lambda e: e.bn_aggr(out=mv[:], in_=st6[:]), ['st6'], ['mv'])
                    A(lambda e: e.activation(out=rstd[:], in_=mv[:, 1:2], func=AF.Sqrt, bias=epsc[:, 0:1], scale=1.0), ['mv', 'epsc'], ['rstd']); V(lambda e: e.reciprocal(out=rstd[:], in_=rstd[:]), ['rstd'], ['rstd'])
                    V(lambda e: e.tensor_scalar(out=xn[:], in0=xt[:], scalar1=mv[:, 0:1], scalar2=rstd[:, 0:1], op0=ALU.subtract, op1=ALU.mult),
                      [XK, 'mv', 'rstd'], ['xn'])
                    for hf in range(2):
                        pi_ = nextps()
                        for j in range(4):
                            k = hf * 4 + j
                            T(lambda e, pi_=pi_, j=j, k=k: e.transpose(out=PS[pi_][:, j * 128:(j + 1) * 128], in_=xn[:, k * 128:(k + 1) * 128], identity=identf[:]),
                              ['xn', 'identf'], [P(pi_)])
                        for j in range(4):
                            k = hf * 4 + j
                            V(lambda e, pi_=pi_, j=j, k=k, b=b: e.tensor_scalar(out=h1T[:, k, :], in0=PS[pi_][:, j * 128:(j + 1) * 128],
                                                                            scalar1=sc1p[:, k, b:b + 1], scalar2=modT[:, k, b:b + 1], op0=ALU.mult, op1=ALU.add),
                              [P(pi_), 'sc1p', 'modT'], ['h1T'])
                    dma(posi[:], pos_d[b, t * 128:(t + 1) * 128].partition_broadcast(128), writes=['posi'], semkey='posi')
                    V(lambda e: e.tensor_copy(out=posf[:], in_=posi[:]), ['posi'], ['posf'])
                    V(lambda e: e.tensor_scalar(out=ang[:], in0=posf[:], scalar1=invf[:, 0:1], scalar2=None, op0=ALU.mult), ['posf', 'invf'], ['ang'])
                    V(lambda e: e.tensor_scalar(out=ki[:], in0=ang[:], scalar1=float(1 / (2 * np.pi)), scalar2=None, op0=ALU.mult), ['ang'], ['ki'])
                    V(lambda e: e.tensor_copy(out=kf[:], in_=ki[:]), ['ki'], ['kf'])
                    V(lambda e: e.scalar_tensor_tensor(out=rr[:], in0=kf[:], scalar=-C1, in1=ang[:], op0=ALU.mult, op1=ALU.add), ['ang', 'kf'], ['rr'])
                    V(lambda e: e.scalar_tensor_tensor(out=rr[:], in0=kf[:], scalar=-C2, in1=rr[:], op0=ALU.mult, op1=ALU.add), ['rr', 'kf'], ['rr'])
                    V(lambda e: e.tensor_scalar(out=tmpa[:], in0=rr[:], scalar1=PI, scalar2=-TWO_PI, op0=ALU.is_gt, op1=ALU.mult), ['rr'], ['tmpa'])
                    V(lambda e: e.tensor_tensor(out=rr[:], in0=rr[:], in1=tmpa[:], op=ALU.add), ['rr', 'tmpa'], ['rr'])
                    V(lambda e: e.tensor_scalar(out=rc[:], in0=rr[:], scalar1=PI / 2, scalar2=None, op0=ALU.add), ['rr'], ['rc'])
                    V(lambda e: e.tensor_scalar(out=tmpa[:], in0=rc[:], scalar1=PI, scalar2=-TWO_PI, op0=ALU.is_gt, op1=ALU.mult), ['rc'], ['tmpa'])
                    V(lambda e: e.tensor_tensor(out=rc[:], in0=rc[:], in1=tmpa[:], op=ALU.add), ['rc', 'tmpa'], ['rc'])
                    V(lambda e: e.tensor_scalar(out=rr[:], in0=rr[:], scalar1=PI, scalar2=-PI, op0=ALU.min, op1=ALU.max), ['rr'], ['rr'])
                    V(lambda e: e.tensor_scalar(out=rc[:], in0=rc[:], scalar1=PI, scalar2=-PI, op0=ALU.min, op1=ALU.max), ['rc'], ['rc'])
                    A(lambda e: e.activation(out=sinT[:], in_=rr[:], func=AF.Sin), ['rr'], ['sinT'])
                    A(lambda e: e.activation(out=cosT[:], in_=rc[:], func=AF.Sin), ['rc'], ['cosT'])
                    for g in range(16):
                        wi = wcnt[0] % 2; wcnt[0] += 1
                        wb_ = WB[wi]; WK = ('wblk', wi)
                        dma(wb_[:], winbf_d[:, :, g * 512:(g + 1) * 512], writes=[WK], semkey=WK)
                        pi_ = nextps(); ps = PS[pi_]
                        if g in (0, 1, 2, 3, 8, 9, 10, 11):
                            for c4 in range(4):
                                for k in range(KD):
                                    T(lambda e, ps=ps, wb_=wb_, c4=c4, k=k: e.matmul(ps[:, c4 * 128:(c4 + 1) * 128], lhsT=wb_[:, k, c4 * 128:(c4 + 1) * 128],
                                                                                  rhs=h1T[:, k, :], start=(k == 0), stop=(k == KD - 1)), [WK, 'h1T'], [P(pi_)])
                        else:
                            for k in range(KD):
                                T(lambda e, ps=ps, wb_=wb_, k=k: e.matmul(ps[:, :], lhsT=h1T[:, k, :], rhs=wb_[:, k, :], start=(k == 0), stop=(k == KD - 1)),
                                  [WK, 'h1T'], [P(pi_)])
                        if g < 4:
                            dst = QT if g < 2 else KT; dk = 'QT' if g < 2 else 'KT'
                            h0 = (g % 2) * 2
                            psv = ps[:].rearrange('p (h a t) -> p h a t', h=2, a=2)
                            Av = psv[:, :, 0, :]; Bv = psv[:, :, 1, :]
                            cb = cosT[:].unsqueeze(1).to_broadcast([128, 2, 128]); sb_ = sinT[:].unsqueeze(1).to_broadcast([128, 2, 128])
                            V(lambda e, Av=Av, cb=cb: e.tensor_tensor(out=t1[:], in0=Av, in1=cb, op=ALU.mult), [P(pi_), 'cosT'], ['t1'])
                            V(lambda e, Bv=Bv, sb_=sb_: e.tensor_tensor(out=t2[:], in0=Bv, in1=sb_, op=ALU.mult), [P(pi_), 'sinT'], ['t2'])
                            V(lambda e, dst=dst, h0=h0: e.tensor_tensor(out=dst[:, h0:h0 + 2, 0, :], in0=t1[:], in1=t2[:], op=ALU.subtract), ['t1', 't2'], [dk])
                            V(lambda e, Av=Av, sb_=sb_: e.tensor_tensor(out=t1[:], in0=Av, in1=sb_, op=ALU.mult), [P(pi_), 'sinT'], ['t1'])
                            V(lambda e, Bv=Bv, cb=cb: e.tensor_tensor(out=t2[:], in0=Bv, in1=cb, op=ALU.mult), [P(pi_), 'cosT'], ['t2'])
                            V(lambda e, dst=dst, h0=h0: e.tensor_tensor(out=dst[:, h0:h0 + 2, 1, :], in0=t1[:], in1=t2[:], op=ALU.add), ['t1', 't2'], [dk])
                            if g < 2:
                                V(lambda e, h0=h0: e.tensor_tensor(out=QTD[:, h0:h0 + 2, :, :], in0=QT[:, h0:h0 + 2, :, :],
                                                                   in1=qdec[:, h0 * 256:(h0 + 2) * 256].rearrange('p (h a t) -> p h a t', h=2, a=2), op=ALU.mult),
                                  ['QT', 'qdec'], ['QTD'])
                        elif g in (4, 5):
                            A(lambda e, ps=ps, g=g: e.copy(out=Vb[:, (g - 4) * 512:(g - 3) * 512], in_=ps[:, :]), [P(pi_)], ['Vb'])
                        elif g in (6, 7):
                            A(lambda e, ps=ps, g=g: e.activation(out=SG[:, (g - 6) * 512:(g - 5) * 512], in_=ps[:, :], func=AF.Silu), [P(pi_)], ['SG'])
                        elif g in (8, 9):
                            A(lambda e, ps=ps, g=g: e.copy(out=UT[:, (g - 8) * 512:(g - 7) * 512], in_=ps[:, :]), [P(pi_)], ['UT'])
                        elif g in (10, 11):
                            hf = g - 10
                            A(lambda e, ps=ps: e.activation(out=sgT[:], in_=ps[:, :], func=AF.Sigmoid), [P(pi_)], ['sgT'])
                            V(lambda e, hf=hf: e.tensor_tensor(out=ybuf[:, hf * 4:(hf + 1) * 4, 30:158], in0=UT[:, hf * 512:(hf + 1) * 512].rearrange('p (c t) -> p c t', c=4),
                                                               in1=sgT[:].rearrange('p (c t) -> p c t', c=4), op=ALU.mult), ['UT', 'sgT'], ['ybuf'])
                        elif g in (12, 13):
                            A(lambda e, ps=ps, g=g: e.activation(out=SA_[:, (g - 12) * 512:(g - 11) * 512], in_=ps[:, :], func=AF.Sigmoid), [P(pi_)], ['SA'])
                        else:
                            A(lambda e, ps=ps, g=g: e.activation(out=SBg[:, (g - 14) * 512:(g - 13) * 512], in_=ps[:, :], func=AF.Sigmoid), [P(pi_)], ['SBg'])
                    pS = nextps()
                    for h in range(4):
                        for ab in range(2):
                            T(lambda e, h=h, ab=ab, pS=pS: e.matmul(PS[pS][:, h * 128:(h + 1) * 128], lhsT=KT[:, h, ab, :], rhs=QT[:, h, ab, :],
                                                                   start=(ab == 0), stop=(ab == 1)), ['KT', 'QT'], [P(pS)])
                    V(lambda e, pS=pS: e.tensor_tensor(out=SmT[:], in0=PS[pS][:, :], in1=maskT[:], op=ALU.mult), [P(pS), 'maskT'], ['SmT'])
                    pK = nextps(); psb = PS[pK][:].bitcast(BF16)
                    for h in range(4):
                        for ab in range(2):
                            c = h * 2 + ab
                            T(lambda e, h=h, ab=ab, c=c, psb=psb: e.transpose(out=psb[:, c * 128:(c + 1) * 128], in_=KT[:, h, ab, :], identity=identb[:]),
                              ['KT', 'identb'], [P(pK)])
                    V(lambda e, psb=psb: e.tensor_tensor(out=Kd[:], in0=psb.rearrange('p (h f) -> p h f', h=4),
                                                         in1=kdec[:].unsqueeze(2).to_broadcast([128, 4, 256]), op=ALU.mult), [P(pK), 'kdec'], ['Kd'])
                    for hp in range(2):
                        pO = nextps()
                        for hl in range(2):
                            h = hp * 2 + hl
                            reg = PS[pO][:, hl * 256:(hl + 1) * 256]
                            T(lambda e, reg=reg, h=h: e.matmul(reg, lhsT=SmT[:, h * 128:(h + 1) * 128], rhs=Vb[:, h * 256:(h + 1) * 256], start=True, stop=False),
                              ['SmT', 'Vb'], [P(pO)])
                            for ab in range(2):
                                T(lambda e, reg=reg, h=h, ab=ab: e.matmul(reg, lhsT=QTD[:, h, ab, :], rhs=STB[:, h, ab, :], start=False, stop=(ab == 1)),
                                  ['QTD', 'STB'], [P(pO)])
                        for hl in range(2):
                            h = hp * 2 + hl
                            reg = PS[pO][:, hl * 256:(hl + 1) * 256]
                            V(lambda e, reg=reg, h=h: e.bn_stats(out=st6h[:, h, :], in_=reg), [P(pO)], ['st6h'])
                            V(lambda e, h=h: e.bn_aggr(out=mvh[:, h, :], in_=st6h[:, h, :]), ['st6h'], ['mvh'])
                            A(lambda e, h=h: e.activation(out=rstdh[:, h:h + 1], in_=mvh[:, h, 1:2], func=AF.Sqrt, bias=epsc[:, 0:1], scale=1.0), ['mvh', 'epsc'], ['rstdh']); V(lambda e, h=h: e.reciprocal(out=rstdh[:, h:h + 1], in_=rstdh[:, h:h + 1]), ['rstdh'], ['rstdh'])
                            V(lambda e, reg=reg, h=h: e.tensor_scalar(out=retn[:, h * 256:(h + 1) * 256], in0=reg, scalar1=mvh[:, h, 0:1], scalar2=rstdh[:, h:h + 1],
                                                                       op0=ALU.subtract, op1=ALU.mult), [P(pO), 'mvh', 'rstdh'], ['retn'])
                    for h in range(4):
                        pU = nextps()
                        for ab in range(2):
                            T(lambda e, pU=pU, h=h, ab=ab: e.matmul(PS[pU][:, ab * 256:(ab + 1) * 256], lhsT=Kd[:, h, ab * 128:(ab + 1) * 128], rhs=Vb[:, h * 256:(h + 1) * 256],
                                                                   start=True, stop=True), ['Kd', 'Vb'], [P(pU)])
                        V(lambda e, pU=pU, h=h: e.scalar_tensor_tensor(out=STF[:, h, :], in0=STF[:, h, :], scalar=HC['gam128'][h], in1=PS[pU][:, :], op0=ALU.mult, op1=ALU.add),
                          [P(pU), 'STF'], ['STF'])
                        A(lambda e, h=h: e.copy(out=STB[:, h, :, :], in_=STF[:, h, :].rearrange('p (a f) -> p a f', a=2)), ['STF'], ['STB'])
                    G(lambda e: e.tensor_tensor(out=MR[:], in0=SG[:], in1=SA_[:], op=ALU.mult), ['SG', 'SA'], ['MR'])
                    V(lambda e: e.tensor_tensor(out=MR[:], in0=MR[:], in1=retn[:], op=ALU.mult), ['MR', 'retn'], ['MR'])
                    pcs = []
                    for hf in range(2):
                        pC = nextps(); pcs.append(pC)
                        for cl in range(4):
                            cc = hf * 4 + cl
                            reg = PS[pC][:, cl * 128:(cl + 1) * 128]
                            for j in range(31):
                                T(lambda e, reg=reg, cc=cc, j=j: e.matmul(reg, lhsT=ybuf[:, cc, j:j + 128], rhs=DG[:, cc, j, :], start=(j == 0), stop=False),
                                  ['ybuf', 'DG'], [P(pC)])
                            T(lambda e, reg=reg, cc=cc: e.matmul(reg, lhsT=ones[0:1, :], rhs=dwbr[0:1, cc * 128:(cc + 1) * 128], start=False, stop=True),
                              ['ones', 'dwbr'], [P(pC)])
                        V(lambda e, pC=pC, hf=hf: e.bn_stats(out=st6[:, hf, :], in_=PS[pC][:, :]), [P(pC)], ['st6'])
                    G(lambda e: e.tensor_copy(out=ybuf[:, :, 0:30], in_=ybuf[:, :, 128:158]), ['ybuf'], ['ybuf'])
                    V(lambda e: e.bn_aggr(out=mv[:], in_=st6[:]), ['st6'], ['mv'])
                    A(lambda e: e.activation(out=rstd[:], in_=mv[:, 1:2], func=AF.Sqrt, bias=epsc[:, 0:1], scale=1.0), ['mv', 'epsc'], ['rstd']); V(lambda e: e.reciprocal(out=rstd[:], in_=rstd[:]), ['rstd'], ['rstd'])
                    for hf in range(2):
                        V(lambda e, hf=hf, pcs=pcs: e.tensor_scalar(out=z[:, hf * 512:(hf + 1) * 512], in0=PS[pcs[hf]][:, :], scalar1=mv[:, 0:1], scalar2=rstd[:, 0:1],
                                                           op0=ALU.subtract, op1=ALU.mult), [P(pcs[hf]), 'mv', 'rstd'], ['z'])
                    for hf in range(2):
                        pZ = nextps()
                        for j in range(4):
                            k = hf * 4 + j
                            T(lambda e, pZ=pZ, j=j, k=k: e.transpose(out=PS[pZ][:, j * 128:(j + 1) * 128], in_=z[:, k * 128:(k + 1) * 128], identity=identf[:]),
                              ['z', 'identf'], [P(pZ)])
                        for j in range(4):
                            k = hf * 4 + j
                            V(lambda e, pZ=pZ, j=j, k=k: e.tensor_scalar(out=xn[:, k * 128:(k + 1) * 128], in0=PS[pZ][:, j * 128:(j + 1) * 128], scalar1=clng[:, k:k + 1], scalar2=clnb[:, k:k + 1],
                                                                          op0=ALU.mult, op1=ALU.add), [P(pZ), 'clng', 'clnb'], ['xn'])
                    A(lambda e: e.activation(out=sT[:], in_=xn[:].rearrange('p (k t) -> p k t', k=KD), func=AF.Silu), ['xn'], ['sT'])
                    for hf in range(2):
                        pD = nextps()
                        for cc in range(KD):
                            T(lambda e, pD=pD, cc=cc, hf=hf: e.matmul(PS[pD][:, :], lhsT=sT[:, cc, :], rhs=WCO[:, cc, hf * 512:(hf + 1) * 512], start=(cc == 0), stop=False),
                              ['sT', 'WCO'], [P(pD)])
                        T(lambda e, pD=pD, hf=hf: e.matmul(PS[pD][:, :], lhsT=ones[0:1, :], rhs=bcor[0:1, hf * 512:(hf + 1) * 512], start=False, stop=True),
                          ['ones', 'bcor'], [P(pD)])
                        V(lambda e, pD=pD, hf=hf: e.tensor_tensor(out=tmpw[:, hf * 512:(hf + 1) * 512], in0=PS[pD][:, :], in1=SBg[:, hf * 512:(hf + 1) * 512], op=ALU.mult),
                          [P(pD), 'SBg'], ['SG'])
                    G(lambda e: e.tensor_tensor(out=MG[:], in0=tmpw[:], in1=MR[:], op=ALU.add), ['SG', 'MR'], ['z'])
                    for hf in range(2):
                        pM = nextps()
                        for j in range(4):
                            k = hf * 4 + j
                            T(lambda e, pM=pM, j=j, k=k: e.transpose(out=PS[pM][:, j * 128:(j + 1) * 128], in_=MG[:, k * 128:(k + 1) * 128], identity=identf[:]),
                              ['z', 'identf'], [P(pM)])
                        A(lambda e, pM=pM, hf=hf: e.copy(out=MT[:, hf * 4:(hf + 1) * 4, :], in_=PS[pM][:, :].rearrange('p (c t) -> p c t', c=4)), [P(pM)], ['MT'])
                    for hf in range(2):
                        pX = nextps()
                        for k in range(KD):
                            T(lambda e, pX=pX, k=k, hf=hf: e.matmul(PS[pX][:, :], lhsT=MT[:, k, :], rhs=WO[:, k, hf * 512:(hf + 1) * 512], start=(k == 0), stop=(k == KD - 1)),
                              ['MT', 'WO'], [P(pX)])
                        V(lambda e, pX=pX, hf=hf: e.tensor_tensor(out=tmpw[:, hf * 512:(hf + 1) * 512], in0=PS[pX][:, :], in1=g1B[:, hf * 512:(hf + 1) * 512], op=ALU.mult),
                          [P(pX), 'g1B'], ['SG'])
                    V(lambda e: e.scalar_tensor_tensor(out=x1p[:], in0=xt[:], scalar=ALPHA, in1=tmpw[:], op0=ALU.mult, op1=ALU.add), [XK, 'SG'], ['retn'])
                    for hf in range(2):
                        V(lambda e, hf=hf: e.bn_stats(out=st6[:, hf, :], in_=x1p[:, hf * 512:(hf + 1) * 512]), ['retn'], ['st6'])
                    V(lambda e: e.bn_aggr(out=mv[:], in_=st6[:]), ['st6'], ['mv'])
                    A(lambda e: e.activation(out=rstd[:], in_=mv[:, 1:2], func=AF.Sqrt, bias=epsc[:, 0:1], scale=1.0), ['mv', 'epsc'], ['rstd']); V(lambda e: e.reciprocal(out=rstd[:], in_=rstd[:]), ['rstd'], ['rstd'])
                    V(lambda e: e.tensor_scalar(out=x1p[:], in0=x1p[:], scalar1=mv[:, 0:1], scalar2=rstd[:, 0:1], op0=ALU.subtract, op1=ALU.mult),
                      ['retn', 'mv', 'rstd'], ['retn'])
                    dma(x1_d[g0:g0 + 128, :], x1p[:], reads=['retn'], semkey='x1o')
            R.barrier()
        def tt(eng, out, in0, in1, op, reads, writes):
            return R.op(eng, lambda e: e.tensor_tensor(out=out, in0=in0, in1=in1, op=op), reads, writes)

        def ts(eng, out, in0, s1, s2, op0, op1, reads, writes):
            if op1 is None:
                return R.op(eng, lambda e: e.tensor_scalar(out=out, in0=in0, scalar1=s1, scalar2=None, op0=op0), reads, writes)
            return R.op(eng, lambda e: e.tensor_scalar(out=out, in0=in0, scalar1=s1, scalar2=s2, op0=op0, op1=op1), reads, writes)

        def stt(out, in0, sc, in1, op0, op1, reads, writes):
            return R.op('vector', lambda e: e.scalar_tensor_tensor(out=out, in0=in0, scalar=sc, in1=in1, op0=op0, op1=op1), reads, writes)

        def act(out, in_, func, reads, writes, bias=None):
            if bias is None:
                return R.op('scalar', lambda e: e.activation(out=out, in_=in_, func=func), reads, writes)
            return R.op('scalar', lambda e: e.activation(out=out, in_=in_, func=func, bias=bias, scale=1.0), reads, writes)

        def cp(eng, out, in_, reads, writes):
            if eng == 'scalar':
                return R.op('scalar', lambda e: e.copy(out=out, in_=in_), reads, writes)
            return R.op(eng, lambda e: e.tensor_copy(out=out, in_=in_), reads, writes)

        def mm(out, lhsT, rhs, start, stop, reads, writes):
            return R.op('tensor', lambda e: e.matmul(out, lhsT=lhsT, rhs=rhs, start=start, stop=stop), reads, writes)

        def tr(out, in_, ident, reads, writes):
            return R.op('tensor', lambda e: e.transpose(out=out, in_=in_, identity=ident), reads, writes)

        def ln_stats(src, skey, st6_, mv_, rstd_, pfx):
            for hf in range(2):
                R.op('vector', (lambda hf: lambda e: e.bn_stats(out=st6_[:, hf, :], in_=src[:, hf * 512:(hf + 1) * 512]))(hf), [skey], [pfx + 'st6'])
            R.op('vector', lambda e: e.bn_aggr(out=mv_[:], in_=st6_[:]), [pfx + 'st6'], [pfx + 'mv'])
            act(rstd_[:], mv_[:, 1:2], AF.Sqrt, [pfx + 'mv', 'epsc'], [pfx + 'rstd'], bias=epsc[:, 0:1])
            R.op('vector', lambda e: e.reciprocal(out=rstd_[:], in_=rstd_[:]), [pfx + 'rstd'], [pfx + 'rstd'])

        with ExitStack() as pb:
            def sB(name, shape, dt): return sbt(pb, 'b1_' + name, shape, dt)
            WQ = sB('WQ', [128, KD, 2048], BF16); SKT = sB('SKT', [128, 16, 128], BF16)
            ln1gB = sB('ln1gB', [128, D], F32); ln1bB = sB('ln1bB', [128, D], F32)
            x1t = sB('x1t', [128, D], F32); h2T = sB('h2T', [128, KD, 128], BF16)
            sc_ = sB('sc', [128, 16, 128], F32)
            top_ = sB('top', [128, 16, 16], F32)
            cand = sB('cand', [128, 8, 16, 16], F32); best = sB('best', [128, 8, 16], F32)
            db = sB('db', [128, 8, 16], F32); Zs = sB('Zs', [128, 8], F32); rZ = sB('rZ', [128, 8], F32); nb0 = sB('nb0', [128, 8], F32)
            Lb = [sB('Lb%d' % i, [128, 16, 128], F32) for i in range(2)]; Eb1 = sB('Eb1', [128, 16, 128], F32)
            xn2 = Eb1[:].rearrange('p a b -> p (a b)')[:, 0:D]
            qTs = Lb[1][:].rearrange('p a b -> p (a b)').bitcast(BF16)[:, 0:2048].rearrange('p (c t) -> p c t', c=16)
            Ebs = [cand[:].rearrange('p h a b -> p (h a) b').rearrange('p (x y) b -> p x (y b)', x=16), Eb1[:]]
            Wa = sB('Wa', [128, 8, 16, 128], BF16); Wb = sB('Wb', [128, 8, 16, 128], BF16)
            AT = sB('AT', [128, 128, 64], BF16); BT = sB('BT', [128, 128, 64], BF16)
            Gs = sB('Gs', [128, 128, 64], BF16)
            st6b = sB('st6', [128, 2, 6], F32); mvb = sB('mv', [128, 2], F32); rstdb = sB('rstd', [128, 1], F32)
            stgq = Lb[0][:].rearrange('p a b -> p (a b)')
            for k in range(KD):
                dma(stgq, wq_d[k * 128:(k + 1) * 128, :], writes=[('Lb', 0)], semkey='stgq')
                cp('vector', WQ[:, k, :], stgq, [('Lb', 0)], ['WQ'])
            for c2 in range(16):
                dma(stgq[:, 0:128], skT_d[c2, :, :], writes=[('Lb', 0)], semkey='stgq')
                cp('vector', SKT[:, c2, :], stgq[:, 0:128], [('Lb', 0)], ['SKT'])
            dma(ln1gB[:], ln1g_d.partition_broadcast(128), writes=['ln1gB'], semkey='b1c1')
            dma(ln1bB[:], ln1b_d.partition_broadcast(128), writes=['ln1bB'], semkey='b1c2')
            for b in range(NB):
                for t in range(NT):
                    g0 = b * S + t * 128; tile_i = b * NT + t
                    dma(x1t[:], x1_d[g0:g0 + 128, :], writes=['x1t'], semkey='x1t')
                    tt('gpsimd', x1t[:], x1t[:], ln1gB[:], ALU.mult, ['x1t', 'ln1gB'], ['x1t'])
                    tt('gpsimd', x1t[:], x1t[:], ln1bB[:], ALU.add, ['x1t', 'ln1bB'], ['x1t'])
                    dma(x1_d[g0:g0 + 128, :], x1t[:], reads=['x1t'], semkey='x1tw')
                    ln_stats(x1t, 'x1t', st6b, mvb, rstdb, 'b1')
                    ts('vector', xn2, x1t[:], mvb[:, 0:1], rstdb[:, 0:1], ALU.subtract, ALU.mult, ['x1t', 'b1mv', 'b1rstd'], [('Eb', 1)])
                    for hf in range(2):
                        pi_ = nextps()
                        for j in range(4):
                            k = hf * 4 + j
                            tr(PS[pi_][:, j * 128:(j + 1) * 128], xn2[:, k * 128:(k + 1) * 128], identf[:], [('Eb', 1), 'identf'], [P(pi_)])
                        for j in range(4):
                            k = hf * 4 + j
                            ts('vector', h2T[:, k, :], PS[pi_][:, j * 128:(j + 1) * 128], sc2p[:, k, b:b + 1], modT[:, 24 + k, b:b + 1], ALU.mult, ALU.add,
                               [P(pi_), 'sc2p', 'modT'], ['h2T'])
                    dma(h2T_d[tile_i], h2T[:], reads=['h2T'], semkey='h2Tw')
                    for c0 in range(0, 16, 4):
                        pi_ = nextps()
                        for cl in range(4):
                            c = c0 + cl
                            for k in range(KD):
                                mm(PS[pi_][:, cl * 128:(cl + 1) * 128], WQ[:, k, c * 128:(c + 1) * 128], h2T[:, k, :], k == 0, k == KD - 1, ['WQ', 'h2T'], [P(pi_)])
                        cp('scalar', qTs[:, c0:c0 + 4, :], PS[pi_][:, :].rearrange('p (c t) -> p c t', c=4), [P(pi_)], [('Lb', 1)])
                    for c0 in range(0, 16, 4):
                        pi_ = nextps()
                        for cl in range(4):
                            c = c0 + cl
                            mm(PS[pi_][:, cl * 128:(cl + 1) * 128], qTs[:, c, :], SKT[:, c, :], True, True, [('Lb', 1), 'SKT'], [P(pi_)])
                        cp('scalar', sc_[:, c0:c0 + 4, :], PS[pi_][:, :].rearrange('p (c t) -> p c t', c=4), [P(pi_)], ['sc'])
                    for c in range(16):
                        R.op('vector', (lambda c: lambda e: e.max(out=top_[:, c, 0:8], in_=sc_[:, c, :]))(c), ['sc'], [('top', c)])
                    for c in range(16):
                        R.op('vector', (lambda c: lambda e: e.match_replace(out=Lb[0][:, c, :], in_to_replace=top_[:, c, 0:8], in_values=sc_[:, c, :], imm_value=NEG))(c),
                             ['sc', ('top', c)], [('wk', c), ('Lb', 0)])
                    for c in range(16):
                        R.op('vector', (lambda c: lambda e: e.max(out=top_[:, c, 8:16], in_=Lb[0][:, c, :]))(c), [('wk', c)], [('top', c), 'top'])
                    topv = top_[:].rearrange('p (h s) i -> p h s i', s=2)
                    tt('vector', cand[:], topv[:, :, 0, :].unsqueeze(3).to_broadcast([128, 8, 16, 16]),
                       topv[:, :, 1, :].unsqueeze(2).to_broadcast([128, 8, 16, 16]), ALU.add, ['top'], ['cand', ('Eb', 0)])
                    Lw = Lb[1][:].rearrange('p a b -> p (a b)').rearrange('p (h x) -> p h x', h=8)
                    for h in range(8):
                        cv = cand[:, h, :, :].rearrange('p a b -> p (a b)')
                        R.op('vector', (lambda h, cv: lambda e: e.max(out=best[:, h, 0:8], in_=cv))(h, cv), ['cand'], [('best', h)])
                    for h in range(8):
                        cv = cand[:, h, :, :].rearrange('p a b -> p (a b)')
                        R.op('vector', (lambda h, cv: lambda e: e.match_replace(out=Lw[:, h, :], in_to_replace=best[:, h, 0:8], in_values=cv, imm_value=NEG))(h, cv),
                             ['cand', ('best', h)], [('wk2', h), ('Lb', 1)])
                    for h in range(8):
                        R.op('vector', (lambda h: lambda e: e.max(out=best[:, h, 8:16], in_=Lw[:, h, :]))(h), [('wk2', h)], [('best', h), 'best'])
                    tt('vector', db[:], best[:], best[:, :, 0:1].to_broadcast([128, 8, 16]), ALU.subtract, ['best'], ['db'])
                    act(db[:], db[:], AF.Exp, ['db'], ['db'])
                    R.op('vector', lambda e: e.tensor_reduce(out=Zs[:], in_=db[:], axis=mybir.AxisListType.X, op=ALU.add), ['db'], ['Zs'])
                    R.op('vector', lambda e: e.reciprocal(out=rZ[:], in_=Zs[:]), ['Zs'], ['rZ'])
                    ts('vector', nb0[:], best[:, :, 0], -1.0, None, ALU.mult, None, ['best'], ['nb0'])
                    sv = sc_[:].rearrange('p (h s) k -> p h s k', s=2)
                    tt('vector', Wb[:], sv[:, :, 1, :].unsqueeze(2).to_broadcast([128, 8, 16, 128]),
                       topv[:, :, 1, :].unsqueeze(3).to_broadcast([128, 8, 16, 128]), ALU.is_equal, ['sc', 'top'], ['Wb'])
                    for h in range(8):
                        q_ = h % 2
                        tt('vector', Lb[q_][:], sc_[:, 2 * h, :].unsqueeze(1).to_broadcast([128, 16, 128]),
                           top_[:, 2 * h + 1, :].unsqueeze(2).to_broadcast([128, 16, 128]), ALU.add, ['sc', 'top'], [('Lb', q_)])
                        act(Ebs[q_], Lb[q_][:], AF.Exp, [('Lb', q_), 'nb0'], [('Eb', q_)] + (['cand'] if q_ == 0 else []), bias=nb0[:, h:h + 1])
                        ts('vector', Lb[q_][:], Lb[q_][:], best[:, h, 15:16], rZ[:, h:h + 1], ALU.is_ge, ALU.mult, [('Lb', q_), 'best', 'rZ', ('Eb', q_)], [('Lb', q_)])
                        tt('gpsimd', Wa[:, h, :, :], Ebs[q_], Lb[q_][:], ALU.mult, [('Eb', q_), ('Lb', q_)], ['Wa'])
                    Wav = Wa[:].rearrange('p h i k -> p (h i) k'); Wbv = Wb[:].rearrange('p h i k -> p (h i) k')
                    for half in range(2):
                        p0_, p1_ = half * 64, (half + 1) * 64
                        for (Wv, dstT, wk, dk) in ((Wbv, BT, 'Wb', 'BT'), (Wav, AT, 'Wa', 'AT')):
                            for kb in range(8):
                                pi_ = nextps(); psb = PS[pi_][:].bitcast(BF16)
                                for kl in range(16):
                                    kk_ = kb * 16 + kl
                                    tr(psb[:, kl * 64:(kl + 1) * 64], Wv[p0_:p1_, :, kk_], identb[p0_:p1_, p0_:p1_], [wk, 'identb'], [P(pi_)])
                                cp('scalar' if kb % 2 == 0 else 'vector', dstT[:, kb * 16:(kb + 1) * 16, :], psb.rearrange('p (k t) -> p k t', k=16), [P(pi_)], [dk])
                        for t0 in range(0, 64, 4):
                            pi_ = nextps()
                            for tl in range(4):
                                tk = t0 + tl
                                mm(PS[pi_][:, tl * 128:(tl + 1) * 128], BT[:, :, tk], AT[:, :, tk], True, True, ['BT', 'AT'], [P(pi_)])
                            cp('vector' if (t0 // 4) % 2 == 0 else 'scalar', Gs[:, :, t0:t0 + 4], PS[pi_][:, :].rearrange('p (t k) -> p k t', t=4), [P(pi_)], ['Gs'])
                        dma(G_d[tile_i * 2 + half], Gs[:], reads=['Gs'], semkey='Gw')
            R.barrier()

        finals = []
        GT = min(8, NT)
        with ExitStack() as pc:
            def sC(name, shape, dt): return sbt(pc, 'b2_' + name, shape, dt)
            Ub = [sC('Ub%d' % i, [128, KD, 1024], BF16) for i in range(2)]
            Vbk = [sC('Vb%d' % i, [128, 8, 1024], BF16) for i in range(2)]
            Gb = [sC('Gb%d' % i, [128, 2 * GT, 8, 64], BF16) for i in range(2)]
            H2 = sC('H2', [128, KD, GT * 128], BF16)
            acc = [sC('acc%d' % i, [128, D], F32) for i in range(GT)]
            ln2gB = sC('ln2gB', [128, D], F32); ln2bB = sC('ln2bB', [128, D], F32); g2B = sC('g2B', [128, D], F32)
            gl = [sC('gl%d' % i, [128, 128], BF16) for i in range(4)]; PT = [sC('PT%d' % i, [128, 128], BF16) for i in range(4)]
            xr = sC('xr', [128, D], F32); yp = sC('yp', [128, D], F32)
            st6c = sC('st6', [128, 2, 6], F32); mvc = sC('mv', [128, 2], F32); rstdc = sC('rstd', [128, 1], F32)
            dma(ln2gB[:], ln2g_d.partition_broadcast(128), writes=['ln2gB'], semkey='b2c1')
            dma(ln2bB[:], ln2b_d.partition_broadcast(128), writes=['ln2bB'], semkey='b2c2')
            pslo[0] = 4; psc[0] = 0
            bc = [0]; cc_ = [0]; pc_ = [0]
            for b in range(NB):
                dma(g2B[:], mod_d[b, 5120:6144].partition_broadcast(128), writes=['g2B'], semkey='g2B')
                for gi in range(NT // GT):
                    tile0 = b * NT + gi * GT
                    for tl in range(GT):
                        dma(H2[:, :, tl * 128:(tl + 1) * 128], h2T_d[tile0 + tl], writes=[('H2', tl)], semkey=('H2', tl))
                    pend = []

                    def vstage(ci, bi, j, pp, tl, blk):
                        for dh in range(2):
                            mm(PS[pp * 2 + dh][:, :], PT[ci][:, :], Vbk[bi][:, j, dh * 512:(dh + 1) * 512], j == 0, j == 7,
                               [('PT', ci), ('Vbk', bi)], [P(pp * 2 + dh)])
                        if j == 7:
                            for dh in range(2):
                                if blk == 0:
                                    cp('vector', acc[tl][:, dh * 512:(dh + 1) * 512], PS[pp * 2 + dh][:, :], [P(pp * 2 + dh)], [('acc', tl)])
                                else:
                                    tt('vector', acc[tl][:, dh * 512:(dh + 1) * 512], PS[pp * 2 + dh][:, :], acc[tl][:, dh * 512:(dh + 1) * 512], ALU.add,
                                       [P(pp * 2 + dh), ('acc', tl)], [('acc', tl)])
                    for blk in range(16):
                        bi = bc[0] % 2; bc[0] += 1
                        dma(Ub[bi][:], ubf_d[:, :, blk * 1024:(blk + 1) * 1024], writes=[('Ub', bi)], semkey=('Ub', bi))
                        dma(Vbk[bi][:], vbf_d[:, blk * 8:(blk + 1) * 8, :], writes=[('Vbk', bi)], semkey=('Vbk', bi))
                        dma(Gb[bi][:].rearrange('p h k t -> p h (k t)'),
                            G_d[tile0 * 2:(tile0 + GT) * 2, :, blk * 8:(blk + 1) * 8, :].rearrange('h p k t -> p h (k t)'),
                            writes=[('Gb', bi)], semkey=('Gb', bi))
                        for tl in range(GT):
                            pp = pc_[0] % 2; pc_[0] += 1
                            for j in range(8):
                                ci = cc_[0] % 4; cc_[0] += 1
                                pa = nextps()
                                for k in range(KD):
                                    mm(PS[pa][:, 0:128], Ub[bi][:, k, j * 128:(j + 1) * 128], H2[:, k, tl * 128:(tl + 1) * 128], k == 0, k == KD - 1,
                                       [('Ub', bi), ('H2', tl)], [P(pa)])
                                act(gl[ci][:], PS[pa][:, 0:128], AF.Gelu, [P(pa)], [('gl', ci)])
                                tt('vector', PT[ci][:].rearrange('p (q t) -> p q t', q=2), gl[ci][:].rearrange('p (q t) -> p q t', q=2),
                                   Gb[bi][:, 2 * tl:2 * tl + 2, j, :], ALU.mult, [('gl', ci), ('Gb', bi)], [('PT', ci)])
                                pend.append((ci, bi, j, pp, tl, blk))
                                if len(pend) > 2:
                                    vstage(*pend.pop(0))
                    while pend:
                        vstage(*pend.pop(0))
                    for tl in range(GT):
                        g0 = (tile0 + tl) * 128
                        dma(xr[:], x1_d[g0:g0 + 128, :], writes=['xr'], semkey='xr')
                        tt('gpsimd', acc[tl][:], acc[tl][:], g2B[:], ALU.mult, [('acc', tl), 'g2B'], [('acc', tl)])
                        stt(yp[:], xr[:], ALPHA, acc[tl][:], ALU.mult, ALU.add, ['xr', ('acc', tl)], ['yp'])
                        ln_stats(yp, 'yp', st6c, mvc, rstdc, 'b2')
                        ts('vector', yp[:], yp[:], mvc[:, 0:1], rstdc[:, 0:1], ALU.subtract, ALU.mult, ['yp', 'b2mv', 'b2rstd'], ['yp'])
                        tt('gpsimd', yp[:], yp[:], ln2gB[:], ALU.mult, ['yp', 'ln2gB'], ['yp'])
                        tt('gpsimd', yp[:], yp[:], ln2bB[:], ALU.add, ['yp', 'ln2bB'], ['yp'])
                        finals.append(dma(y_d[g0:g0 + 128, :], yp[:], reads=['yp'], semkey='yout'))
            finals = finals[-1:]
        R.emit(top, finals)
    return nc


def make_in_maps(inputs, n_cores, NB):
    HC = host_consts()
    f = lambda a: np.ascontiguousarray(np.asarray(a))
    shared = {
        'w_ada': f(inputs['w_ada'][0]), 'b_ada': f(np.asarray(inputs['b_ada'][0]).reshape(48, 128).T), 'b_ada_row': f(np.asarray(inputs['b_ada'][0])[None, :]), 'w_in': f(inputs['w_in'][0]),
        'conv_dwT': f(np.asarray(inputs['conv_dw'][0]).T), 'conv_dw_b': f(np.asarray(inputs['conv_dw_b'][0])[None, :]),
        'conv_ln_g': f(np.asarray(inputs['conv_ln_g'][0]).reshape(KD, 128).T), 'conv_ln_b': f(np.asarray(inputs['conv_ln_b'][0]).reshape(KD, 128).T),
        'w_conv_out': f(inputs['w_conv_out'][0]), 'b_conv_out': f(np.asarray(inputs['b_conv_out'][0])[None, :]),
        'w_out': f(inputs['w_out'][0]), 'ln1_g': f(inputs['ln1_g'][0]), 'ln1_b': f(inputs['ln1_b'][0]),
        'ln2_g': f(inputs['ln2_g'][0]), 'ln2_b': f(inputs['ln2_b'][0]), 'peer_wq': f(inputs['peer_wq'][0]),
        'skT': f(np.asarray(inputs['peer_subkeys'][0]).reshape(16, 128, 128).transpose(0, 2, 1)),
        'peer_uT': f(np.asarray(inputs['peer_u'][0]).T), 'peer_v': f(inputs['peer_v'][0]),
        'ident': HC['ident'], 'iota': HC['iota'], 'invf': HC['invf'], 'maskT': HC['maskT'], 'qdec': HC['qdec'], 'kdec': HC['kdec'],
    }
    x = np.asarray(inputs['x']); c = np.asarray(inputs['c']); pos = np.asarray(inputs['positions'])
    S = x.shape[1]
    maps = []
    for i in range(n_cores):
        m = dict(shared)
        m['x'] = f(x[i * NB:(i + 1) * NB].reshape(NB * S, D))
        m['cT'] = f(c[i * NB:(i + 1) * NB].T)
        m['pos'] = f(pos[i * NB:(i + 1) * NB].astype(np.int32))
        maps.append(m)
    return maps


def kernel(**inputs):
    n_cores = 8; NB = 2
    S = np.asarray(inputs['x']).shape[1]
    nc = build(NB, S)
    maps = make_in_maps(inputs, n_cores, NB)
    res = run_bass_kernel_spmd(nc, maps, core_ids=list(range(n_cores)))
    out = np.concatenate([np.asarray(r['y']).reshape(NB, S, D) for r in res.results], axis=0)
    return out.astype(np.float32)
```

```python
import numpy as np
from contextlib import ExitStack
import concourse.bass as bass
import concourse.mybir as mybir
from concourse.bass_utils import run_bass_kernel_spmd

F32 = mybir.dt.float32; BF16 = mybir.dt.bfloat16; I32 = mybir.dt.int32
ALU = mybir.AluOpType; AF = mybir.ActivationFunctionType
ENGS = ['sync', 'scalar', 'vector', 'gpsimd', 'tensor']
D = 1024; KD = 8; NCOL = 8192
ALPHA = float(2.0 ** 0.25); EPS = 1e-5
NEG = -1e30


class Item:
    __slots__ = ('eng', 'fn', 'deps', 'needed', 'dma', 'semkey', 'count', 'sem')

    def __init__(s, eng, fn, dma, semkey):
        s.eng = eng; s.fn = fn; s.deps = []; s.needed = False; s.dma = dma
        s.semkey = semkey; s.count = 0; s.sem = None


class Rec:
    def __init__(s, nc):
        s.nc = nc; s.items = {e: [] for e in ENGS}; s.lastw = {}; s.readers = {}; s.all = []

    def op(s, eng, fn, reads=(), writes=(), dma=False, semkey=None):
        it = Item(eng, fn, dma, semkey)
        deps = {}
        for k in reads:
            w = s.lastw.get(k)
            if w is not None: deps[id(w)] = w
        for k in writes:
            w = s.lastw.get(k)
            if w is not None: deps[id(w)] = w
            for r in s.readers.get(k, ()): deps[id(r)] = r
        for d in deps.values():
            if d is it: continue
            if d.eng == eng and eng == 'tensor' and not d.dma: continue
            it.deps.append(d); d.needed = True
        for k in reads: s.readers.setdefault(k, []).append(it)
        for k in writes: s.lastw[k] = it; s.readers[k] = []
        s.items[eng].append(it); s.all.append(it)
        return it

    def barrier(s):
        lasts = [s.items[e][-1] for e in ENGS if s.items[e]]
        dmas = {}
        for it in s.all:
            if it.dma: dmas[it.semkey] = it
        for e in ENGS:
            it = Item(e, None, False, None)
            for d in lasts + list(dmas.values()):
                if d.fn is None: continue
                if d.eng == e and not d.dma: continue
                it.deps.append(d); d.needed = True
            s.items[e].append(it); s.all.append(it)
        s.lastw = {}; s.readers = {}

    def emit(s, stack, finals=()):
        nc = s.nc
        esem = {e: stack.enter_context(nc.semaphore('sem_' + e)) for e in ENGS}
        dsem = {}; dcnt = {}; cnt = {e: 0 for e in ENGS}
        for it in s.all:
            if it.dma:
                k = it.semkey
                if k not in dsem:
                    dsem[k] = stack.enter_context(nc.semaphore('dsem%d' % len(dsem))); dcnt[k] = 0
                dcnt[k] += 16; it.sem = dsem[k]; it.count = dcnt[k]; it.needed = True
            elif it.needed and it.fn is not None:
                cnt[it.eng] += 1; it.sem = esem[it.eng]; it.count = cnt[it.eng]
        block = stack.enter_context(nc.Block())

        def body(e):
            def run(eng):
                waited = {}
                for it in s.items[e]:
                    for d in it.deps:
                        if d.sem is None: continue
                        if waited.get(id(d.sem), 0) < d.count:
                            eng.wait_ge(d.sem, d.count); waited[id(d.sem)] = d.count
                    if it.fn is None: continue
                    ins = it.fn(eng)
                    if it.dma: ins.then_inc(it.sem, 16)
                    elif it.needed: ins.then_inc(it.sem, 1)
                if e == 'sync':
                    for d in finals:
                        if waited.get(id(d.sem), 0) < d.count:
                            eng.wait_ge(d.sem, d.count); waited[id(d.sem)] = d.count
            return run
        for e in ENGS:
            getattr(block, e)(body(e))


def host_consts():
    c = {}
    c['ident'] = np.eye(128, dtype=np.float32)
    c['iota'] = np.tile(np.arange(128, dtype=np.float32)[None, :], (128, 1))
    c['invf'] = (np.float32(10000.0) ** (-(np.arange(128, dtype=np.float32)) / np.float32(128))).astype(np.float32)[:, None]
    gam = [1.0 - 2.0 ** (-5.0 - h) for h in range(4)]
    idx = np.arange(128)
    mask = np.zeros((128, 4, 128), np.float32)
    qdec = np.zeros((128, 4, 2, 128), np.float32)
    kdec = np.zeros((128, 4), np.float32)
    for h in range(4):
        g = gam[h]
        m = (g ** np.abs(idx[:, None] - idx[None, :]).astype(np.float64)) * ((idx[:, None] // 64) <= (idx[None, :] // 64))
        mask[:, h, :] = (m / 16.0).astype(np.float32)
        qdec[:, h, :, :] = (g ** (idx + 1.0))[None, None, :]
        kdec[:, h] = (g ** (127.0 - idx)) / 16.0
    c['maskT'] = mask.reshape(128, 512)
    c['qdec'] = qdec.reshape(128, 1024)
    c['kdec'] = kdec
    c['gam128'] = [float(g ** 128) for g in gam]
    return c


def build(NB, S, debug=False):
    NT = S // 128
    NTOK = NB * S
    HC = host_consts()
    nc = bass.Bass('TRN2', target_bir_lowering=False)

    def din(name, shape, dt=F32): return nc.dram_tensor(name, shape, dt, kind='ExternalInput').ap()
    x_d = din('x', [NTOK, D]); cT_d = din('cT', [D, NB]); pos_d = din('pos', [NB, S], I32)
    wada_d = din('w_ada', [D, 6 * D]); bada_d = din('b_ada', [128, 48]); badar_d = din('b_ada_row', [1, 6 * D])
    win_d = din('w_in', [D, NCOL]); dwT_d = din('conv_dwT', [D, 31]); dwb_d = din('conv_dw_b', [1, D])
    clng_d = din('conv_ln_g', [128, KD]); clnb_d = din('conv_ln_b', [128, KD])
    wco_d = din('w_conv_out', [D, D]); bco_d = din('b_conv_out', [1, D]); wo_d = din('w_out', [D, D])
    ln1g_d = din('ln1_g', [D]); ln1b_d = din('ln1_b', [D]); ln2g_d = din('ln2_g', [D]); ln2b_d = din('ln2_b', [D])
    wq_d = din('peer_wq', [D, 2048]); skT_d = din('skT', [16, 128, 128])
    uT_d = din('peer_uT', [D, 16384]); v_d = din('peer_v', [16384, D])
    ident_d = din('ident', [128, 128]); iota_d = din('iota', [128, 128]); invf_d = din('invf', [128, 1])
    maskT_d = din('maskT', [128, 512]); qdec_d = din('qdec', [128, 1024]); kdec_d = din('kdec', [128, 4])
    y_d = nc.dram_tensor('y', [NTOK, D], F32, kind='ExternalOutput').ap()
    winbf_d = nc.dram_tensor('winbf', [128, KD, NCOL], BF16, kind='Internal').ap()
    ubf_d = nc.dram_tensor('ubf', [128, KD, 16384], BF16, kind='Internal').ap()
    vbf_d = nc.dram_tensor('vbf', [128, 128, D], BF16, kind='Internal').ap()
    mod_d = nc.dram_tensor('modd', [NB, 6 * D], F32, kind='Internal').ap()
    NTILE = NTOK // 128
    G_d = nc.dram_tensor('Gd', [NTILE * 2, 128, 128, 64], BF16, kind='Internal').ap()
    h2T_d = nc.dram_tensor('h2Td', [NTILE, 128, KD, 128], BF16, kind='Internal').ap()
    x1_d = nc.dram_tensor('x1d', [NTOK, D], F32, kind='ExternalOutput' if debug else 'Internal').ap()

    with ExitStack() as top:
        R = Rec(nc)
        PS = [top.enter_context(nc.psum_tensor('ps%d' % i, [128, 512], F32)) for i in range(8)]
        psc = [0]; pslo = [0]

        def nextps():
            i = pslo[0] + psc[0] % (8 - pslo[0]); psc[0] += 1
            return i

        def P(i): return ('ps', i)

        def dma(out, in_, reads=(), writes=(), semkey=None, eng='sync', **kw):
            return R.op(eng, lambda e: e.dma_start(out=out, in_=in_, **kw), reads=reads, writes=writes, dma=True, semkey=semkey)

        def V(fn, reads=(), writes=()): return R.op('vector', fn, reads, writes)
        def A(fn, reads=(), writes=()): return R.op('scalar', fn, reads, writes)
        def G(fn, reads=(), writes=()): return R.op('gpsimd', fn, reads, writes)
        def T(fn, reads=(), writes=()): return R.op('tensor', fn, reads, writes)

        def sbt(st, name, shape, dt): return st.enter_context(nc.sbuf_tensor('s_' + name, shape, dt))
        identf = sbt(top, 'identf', [128, 128], F32); identb = sbt(top, 'identb', [128, 128], BF16)
        ones = sbt(top, 'ones', [1, 128], BF16)
        modT = sbt(top, 'modT', [128, 48, NB], F32)
        sc1p = sbt(top, 'sc1p', [128, 8, NB], F32); sc2p = sbt(top, 'sc2p', [128, 8, NB], F32)
        dma(identf[:], ident_d, writes=['identf'], semkey='identf')
        V(lambda e: e.tensor_copy(out=identb[:], in_=identf[:]), ['identf'], ['identb'])
        V(lambda e: e.memset(ones[:], 1.0), [], ['ones'])
        epsc = sbt(top, 'epsc', [128, 1], F32)
        V(lambda e: e.memset(epsc[:], EPS), [], ['epsc'])

        with ExitStack() as p0:
            stgf = [sbt(p0, 'stgf%d' % i, [128, 4096], F32) for i in range(2)]
            stgb = [sbt(p0, 'stgb%d' % i, [128, 4096], BF16) for i in range(2)]
            cnt = [0]

            def conv_block(src_ap, dst_ap, shape3=None):
                i = cnt[0] % 2; cnt[0] += 1
                sf = stgf[i][:]; sbv = stgb[i][:]
                if shape3 is not None:
                    sf = sf.rearrange('p (a b) -> p a b', a=shape3); sbv = sbv.rearrange('p (a b) -> p a b', a=shape3)
                dma(sf, src_ap, writes=[('stgf', i)], semkey=('stgf', i))
                if i == 0:
                    V(lambda e: e.tensor_copy(out=stgb[i][:], in_=stgf[i][:]), [('stgf', i)], [('stgb', i)])
                else:
                    A(lambda e: e.copy(out=stgb[i][:], in_=stgf[i][:]), [('stgf', i)], [('stgb', i)])
                dma(dst_ap, sbv, reads=[('stgb', i)], semkey=('stgbo', i))
            for k in range(KD):
                for eb in range(4):
                    conv_block(uT_d[k * 128:(k + 1) * 128, eb * 4096:(eb + 1) * 4096], ubf_d[:, k, eb * 4096:(eb + 1) * 4096])
            vv = v_d.rearrange('(j p) d -> p j d', p=128)
            for jb in range(32):
                conv_block(vv[:, jb * 4:(jb + 1) * 4, :], vbf_d[:, jb * 4:(jb + 1) * 4, :], shape3=4)
            for k in range(KD):
                for cb in range(2):
                    conv_block(win_d[k * 128:(k + 1) * 128, cb * 4096:(cb + 1) * 4096], winbf_d[:, k, cb * 4096:(cb + 1) * 4096])

            cTs = sbt(p0, 'cTs', [128, KD, NB], F32); siluT = sbt(p0, 'siluT', [128, KD, NB], F32)
            badaT = sbt(p0, 'badaT', [128, 48], F32)
            wab = [sbt(p0, 'wab%d' % i, [128, KD, 512], F32) for i in range(2)]
            dma(cTs[:], cT_d.rearrange('(k p) b -> p k b', p=128), writes=['cTs'], semkey='cTs')
            dma(badaT[:], bada_d, writes=['badaT'], semkey='badaT')
            A(lambda e: e.activation(out=siluT[:], in_=cTs[:], func=AF.Silu), ['cTs'], ['siluT'])
            pm = nextps(); pm2 = [nextps(), nextps()]
            onesf = sbt(p0, 'onesf', [1, 8], F32); badar = sbt(p0, 'badar', [1, 6 * D], F32); modrow = sbt(p0, 'modrow', [NB, 6 * D], F32)
            V(lambda e: e.memset(onesf[:], 1.0), [], ['onesf'])
            dma(badar[:], badar_d, writes=['badar'], semkey='badar')
            for blk in range(12):
                wb_ = wab[blk % 2]
                dma(wb_[:], wada_d.rearrange('(k p) c -> p k c', p=128)[:, :, blk * 512:(blk + 1) * 512],
                    writes=[('wab', blk % 2)], semkey=('wab', blk % 2))
                q2 = pm2[blk % 2]
                for k in range(KD):
                    T(lambda e, wb_=wb_, k=k, q2=q2: e.matmul(PS[q2][0:NB, :], lhsT=siluT[:, k, :], rhs=wb_[:, k, :], start=(k == 0), stop=False),
                      [('wab', blk % 2), 'siluT'], [P(q2)])
                T(lambda e, q2=q2, blk=blk: e.matmul(PS[q2][0:NB, :], lhsT=onesf[0:1, 0:NB], rhs=badar[0:1, blk * 512:(blk + 1) * 512], start=False, stop=True),
                  ['onesf', 'badar'], [P(q2)])
                V(lambda e, q2=q2, blk=blk: e.tensor_copy(out=modrow[:, blk * 512:(blk + 1) * 512], in_=PS[q2][0:NB, :]), [P(q2)], ['modrow'])
                for c4 in range(4):
                    kk = blk * 4 + c4
                    for k in range(KD):
                        T(lambda e, wb_=wb_, k=k, c4=c4, kk=kk: e.matmul(
                            PS[pm][:, kk * NB:(kk + 1) * NB], lhsT=wb_[:, k, c4 * 128:(c4 + 1) * 128], rhs=siluT[:, k, :],
                            start=(k == 0), stop=(k == KD - 1)), [('wab', blk % 2), 'siluT'], [P(pm)])
            V(lambda e: e.tensor_tensor(out=modT[:], in0=PS[pm][:, 0:48 * NB].rearrange('p (k b) -> p k b', b=NB),
                                        in1=badaT[:].unsqueeze(2).to_broadcast([128, 48, NB]), op=ALU.add),
              [P(pm), 'badaT'], ['modT'])
            V(lambda e: e.tensor_scalar(out=sc1p[:], in0=modT[:, 8:16, :], scalar1=1.0, scalar2=None, op0=ALU.add), ['modT'], ['sc1p'])
            V(lambda e: e.tensor_scalar(out=sc2p[:], in0=modT[:, 32:40, :], scalar1=1.0, scalar2=None, op0=ALU.add), ['modT'], ['sc2p'])
            dma(mod_d, modrow[:], reads=['modrow'], semkey='modout')
            R.barrier()

        with ExitStack() as pa:
            def sa(name, shape, dt): return sbt(pa, name, shape, dt)
            XT = [sa('xt0', [128, D], F32)] * 2
            WB = [sa('wblk%d' % i, [128, KD, 512], BF16) for i in range(2)]
            DG = sa('dg', [128, KD, 31, 128], BF16)
            dwT = sa('dwT', [128, KD, 31], F32)
            WCO = sa('wco', [128, KD, D], BF16); WO = sa('wo', [128, KD, D], BF16)
            dwbr_f = sa('dwbr_f', [1, D], F32); dwbr = sa('dwbr', [1, D], BF16)
            bcor_f = sa('bcor_f', [1, D], F32); bcor = sa('bcor', [1, D], BF16)
            clng = sa('clng', [128, KD], F32); clnb = sa('clnb', [128, KD], F32)
            g1B = sa('g1B', [128, D], F32)
            invf = sa('invf', [128, 1], F32); maskT = sa('maskT', [128, 512], F32)
            qdecf = sa('qdecf', [128, 1024], F32) if False else None; qdec = sa('qdec', [128, 1024], BF16); kdec = sa('kdec', [128, 4], F32)
            st6 = sa('st6', [128, 2, 6], F32); mv = sa('mv', [128, 2], F32); rstd = sa('rstd', [128, 1], F32)
            st6h = sa('st6h', [128, 4, 6], F32); mvh = sa('mvh', [128, 4, 2], F32); rstdh = sa('rstdh', [128, 4], F32)
            xn = sa('xn', [128, D], F32); h1T = sa('h1T', [128, KD, 128], BF16)
            posi = sa('posi', [128, 128], I32); posf = sa('posf', [128, 128], F32); ang = sa('ang', [128, 128], F32)
            ki = sa('ki', [128, 128], I32); kf = sa('kf', [128, 128], F32); rr = sa('rr', [128, 128], F32)
            rc = sa('rc', [128, 128], F32); tmpa = sa('tmpa', [128, 128], F32)
            sinT = sa('sinT', [128, 128], F32); cosT = sa('cosT', [128, 128], F32)
            t1 = sa('t1', [128, 2, 128], F32); t2 = sa('t2', [128, 2, 128], F32)
            QT = sa('QT', [128, 4, 2, 128], BF16); KT = sa('KT', [128, 4, 2, 128], BF16); QTD = sa('QTD', [128, 4, 2, 128], BF16)
            Vb = sa('Vb', [128, D], BF16); SG = sa('SG', [128, D], F32); SA_ = sa('SA', [128, D], BF16); SBg = sa('SBg', [128, D], BF16)
            UT = sa('UT', [128, 1024], BF16); sgT = sa('sgT', [128, 512], BF16)
            ybuf = sa('ybuf', [128, KD, 158], BF16)
            SmT = sa('SmT', [128, 512], BF16); Kd = sa('Kd', [128, 4, 256], BF16)
            STF = sa('STF', [128, 4, 512], F32); STB = sa('STB', [128, 4, 2, 256], BF16)
            retn = sa('retn', [128, D], F32); MR = sa('MR', [128, D], F32)
            z = sa('z', [128, D], F32); stg = z; sT = sa('sT', [128, KD, 128], BF16)
            MG = z; MT = sa('MT', [128, KD, 128], BF16)
            x1p = retn
            tmpw = SG

            dma(invf[:], invf_d, writes=['invf'], semkey='c1'); dma(maskT[:], maskT_d, writes=['maskT'], semkey='c2')
            dma(z[:], qdec_d, writes=['z'], semkey='c3'); V(lambda e: e.tensor_copy(out=qdec[:], in_=z[:]), ['z'], ['qdec']); dma(kdec[:], kdec_d, writes=['kdec'], semkey='c4')
            dma(clng[:], clng_d, writes=['clng'], semkey='c5')
            dma(clnb[:], clnb_d, writes=['clnb'], semkey='c6')
            dma(dwbr_f[:], dwb_d, writes=['dwbr_f'], semkey='c9'); dma(bcor_f[:], bco_d, writes=['bcor_f'], semkey='c10')
            V(lambda e: e.tensor_copy(out=dwbr[:], in_=dwbr_f[:]), ['dwbr_f'], ['dwbr'])
            V(lambda e: e.tensor_copy(out=bcor[:], in_=bcor_f[:]), ['bcor_f'], ['bcor'])
            dma(dwT[:], dwT_d.rearrange('(k p) j -> p k j', p=128), writes=['dwT'], semkey='c11')
            for cc in range(KD):
                G(lambda e, cc=cc: e.tensor_tensor(out=DG[:, cc, :, :], in0=identb[:].unsqueeze(1).to_broadcast([128, 31, 128]),
                                                   in1=dwT[:, cc, :].unsqueeze(2).to_broadcast([128, 31, 128]), op=ALU.mult),
                  ['identb', 'dwT'], ['DG'])
            for k in range(KD):
                dma(stg[:], wco_d[k * 128:(k + 1) * 128, :], writes=['z'], semkey='stg')
                V(lambda e, k=k: e.tensor_copy(out=WCO[:, k, :], in_=stg[:]), ['z'], ['WCO'])
            for k in range(KD):
                dma(stg[:], wo_d[k * 128:(k + 1) * 128, :], writes=['z'], semkey='stg')
                V(lambda e, k=k: e.tensor_copy(out=WO[:, k, :], in_=stg[:]), ['z'], ['WO'])

            C1 = float(np.float32(6.28125)); C2 = float(np.float32(2 * np.pi - 6.28125)); PI = float(np.pi); TWO_PI = float(2 * np.pi)
            wcnt = [0]; itc = [0]
            for b in range(NB):
                dma(g1B[:], mod_d[b, 2048:3072].partition_broadcast(128), writes=['g1B'], semkey='g1B')
                G(lambda e: e.memset(STF[:], 0.0), [], ['STF']); G(lambda e: e.memset(STB[:], 0.0), [], ['STB'])
                G(lambda e: e.memset(ybuf[:, :, 0:30], 0.0), [], ['ybuf'])
                for t in range(NT):
                    g0 = b * S + t * 128
                    xi = 0; itc[0] += 1
                    xt = XT[xi]; XK = ('xt', xi)
                    dma(xt[:], x_d[g0:g0 + 128, :], writes=[XK], semkey=XK)
                    for hf in range(2):
                        V(lambda e, hf=hf: e.bn_stats(out=st6[:, hf, :], in_=xt[:, hf * 512:(hf + 1) * 512]), [XK], ['st6'])
                    V(lambda e: e.bn_aggr(out=mv[:], in_=st6[:]), ['st6'], ['mv'])
                    A(lambda e: e.activation(out=rstd[:], in_=mv[:, 1:2], func=AF.Sqrt, bias=epsc[:, 0:1], scale=1.0), ['mv', 'epsc'], ['rstd']); V(lambda e: e.reciprocal(out=rstd[:], in_=rstd[:]), ['rstd'], ['rstd'])
                    V(lambda e: e.tensor_scalar(out=xn[:], in0=xt[:], scalar1=mv[:, 0:1], scalar2=rstd[:, 0:1], op0=ALU.subtract, op1=ALU.mult),
                      [XK, 'mv', 'rstd'], ['xn'])
                    for hf in range(2):
                        pi_ = nextps()
                        for j in range(4):
                            k = hf * 4 + j
                            T(lambda e, pi_=pi_, j=j, k=k: e.transpose(out=PS[pi_][:, j * 128:(j + 1) * 128], in_=xn[:, k * 128:(k + 1) * 128], identity=identf[:]),
                              ['xn', 'identf'], [P(pi_)])
                        for j in range(4):
                            k = hf * 4 + j
                            V(lambda e, pi_=pi_, j=j, k=k, b=b: e.tensor_scalar(out=h1T[:, k, :], in0=PS[pi_][:, j * 128:(j + 1) * 128],
                                                                            scalar1=sc1p[:, k, b:b + 1], scalar2=modT[:, k, b:b + 1], op0=ALU.mult, op1=ALU.add),
                              [P(pi_), 'sc1p', 'modT'], ['h1T'])
                    dma(posi[:], pos_d[b, t * 128:(t + 1) * 128].partition_broadcast(128), writes=['posi'], semkey='posi')
                    V(lambda e: e.tensor_copy(out=posf[:], in_=posi[:]), ['posi'], ['posf'])
                    V(lambda e: e.tensor_scalar(out=ang[:], in0=posf[:], scalar1=invf[:, 0:1], scalar2=None, op0=ALU.mult), ['posf', 'invf'], ['ang'])
                    V(lambda e: e.tensor_scalar(out=ki[:], in0=ang[:], scalar1=float(1 / (2 * np.pi)), scalar2=None, op0=ALU.mult), ['ang'], ['ki'])
                    V(lambda e: e.tensor_copy(out=kf[:], in_=ki[:]), ['ki'], ['kf'])
                    V(lambda e: e.scalar_tensor_tensor(out=rr[:], in0=kf[:], scalar=-C1, in1=ang[:], op0=ALU.mult, op1=ALU.add), ['ang', 'kf'], ['rr'])
                    V(lambda e: e.scalar_tensor_tensor(out=rr[:], in0=kf[:], scalar=-C2, in1=rr[:], op0=ALU.mult, op1=ALU.add), ['rr', 'kf'], ['rr'])
                    V(lambda e: e.tensor_scalar(out=tmpa[:], in0=rr[:], scalar1=PI, scalar2=-TWO_PI, op0=ALU.is_gt, op1=ALU.mult), ['rr'], ['tmpa'])
                    V(lambda e: e.tensor_tensor(out=rr[:], in0=rr[:], in1=tmpa[:], op=ALU.add), ['rr', 'tmpa'], ['rr'])
                    V(lambda e: e.tensor_scalar(out=rc[:], in0=rr[:], scalar1=PI / 2, scalar2=None, op0=ALU.add), ['rr'], ['rc'])
                    V(lambda e: e.tensor_scalar(out=tmpa[:], in0=rc[:], scalar1=PI, scalar2=-TWO_PI, op0=ALU.is_gt, op1=ALU.mult), ['rc'], ['tmpa'])
                    V(lambda e: e.tensor_tensor(out=rc[:], in0=rc[:], in1=tmpa[:], op=ALU.add), ['rc', 'tmpa'], ['rc'])
                    V(lambda e: e.tensor_scalar(out=rr[:], in0=rr[:], scalar1=PI, scalar2=-PI, op0=ALU.min, op1=ALU.max), ['rr'], ['rr'])
                    V(lambda e: e.tensor_scalar(out=rc[:], in0=rc[:], scalar1=PI, scalar2=-PI, op0=ALU.min, op1=ALU.max), ['rc'], ['rc'])
                    A(lambda e: e.activation(out=sinT[:], in_=rr[:], func=AF.Sin), ['rr'], ['sinT'])
                    A(lambda e: e.activation(out=cosT[:], in_=rc[:], func=AF.Sin), ['rc'], ['cosT'])
                    for g in range(16):
                        wi = wcnt[0] % 2; wcnt[0] += 1
                        wb_ = WB[wi]; WK = ('wblk', wi)
                        dma(wb_[:], winbf_d[:, :, g * 512:(g + 1) * 512], writes=[WK], semkey=WK)
                        pi_ = nextps(); ps = PS[pi_]
                        if g in (0, 1, 2, 3, 8, 9, 10, 11):
                            for c4 in range(4):
                                for k in range(KD):
                                    T(lambda e, ps=ps, wb_=wb_, c4=c4, k=k: e.matmul(ps[:, c4 * 128:(c4 + 1) * 128], lhsT=wb_[:, k, c4 * 128:(c4 + 1) * 128],
                                                                                  rhs=h1T[:, k, :], start=(k == 0), stop=(k == KD - 1)), [WK, 'h1T'], [P(pi_)])
                        else:
                            for k in range(KD):
                                T(lambda e, ps=ps, wb_=wb_, k=k: e.matmul(ps[:, :], lhsT=h1T[:, k, :], rhs=wb_[:, k, :], start=(k == 0), stop=(k == KD - 1)),
                                  [WK, 'h1T'], [P(pi_)])
                        if g < 4:
                            dst = QT if g < 2 else KT; dk = 'QT' if g < 2 else 'KT'
                            h0 = (g % 2) * 2
                            psv = ps[:].rearrange('p (h a t) -> p h a t', h=2, a=2)
                            Av = psv[:, :, 0, :]; Bv = psv[:, :, 1, :]
                            cb = cosT[:].unsqueeze(1).to_broadcast([128, 2, 128]); sb_ = sinT[:].unsqueeze(1).to_broadcast([128, 2, 128])
                            V(lambda e, Av=Av, cb=cb: e.tensor_tensor(out=t1[:], in0=Av, in1=cb, op=ALU.mult), [P(pi_), 'cosT'], ['t1'])
                            V(lambda e, Bv=Bv, sb_=sb_: e.tensor_tensor(out=t2[:], in0=Bv, in1=sb_, op=ALU.mult), [P(pi_), 'sinT'], ['t2'])
                            V(lambda e, dst=dst, h0=h0: e.tensor_tensor(out=dst[:, h0:h0 + 2, 0, :], in0=t1[:], in1=t2[:], op=ALU.subtract), ['t1', 't2'], [dk])
                            V(lambda e, Av=Av, sb_=sb_: e.tensor_tensor(out=t1[:], in0=Av, in1=sb_, op=ALU.mult), [P(pi_), 'sinT'], ['t1'])
                            V(lambda e, Bv=Bv, cb=cb: e.tensor_tensor(out=t2[:], in0=Bv, in1=cb, op=ALU.mult), [P(pi_), 'cosT'], ['t2'])
                            V(lambda e, dst=dst, h0=h0: e.tensor_tensor(out=dst[:, h0:h0 + 2, 1, :], in0=t1[:], in1=t2[:], op=ALU.add), ['t1', 't2'], [dk])
                            if g < 2:
                                V(lambda e, h0=h0: e.tensor_tensor(out=QTD[:, h0:h0 + 2, :, :], in0=QT[:, h0:h0 + 2, :, :],
                                                                   in1=qdec[:, h0 * 256:(h0 + 2) * 256].rearrange('p (h a t) -> p h a t', h=2, a=2), op=ALU.mult),
                                  ['QT', 'qdec'], ['QTD'])
                        elif g in (4, 5):
                            A(lambda e, ps=ps, g=g: e.copy(out=Vb[:, (g - 4) * 512:(g - 3) * 512], in_=ps[:, :]), [P(pi_)], ['Vb'])
                        elif g in (6, 7):
                            A(lambda e, ps=ps, g=g: e.activation(out=SG[:, (g - 6) * 512:(g - 5) * 512], in_=ps[:, :], func=AF.Silu), [P(pi_)], ['SG'])
                        elif g in (8, 9):
                            A(lambda e, ps=ps, g=g: e.copy(out=UT[:, (g - 8) * 512:(g - 7) * 512], in_=ps[:, :]), [P(pi_)], ['UT'])
                        elif g in (10, 11):
                            hf = g - 10
                            A(lambda e, ps=ps: e.activation(out=sgT[:], in_=ps[:, :], func=AF.Sigmoid), [P(pi_)], ['sgT'])
                            V(lambda e, hf=hf: e.tensor_tensor(out=ybuf[:, hf * 4:(hf + 1) * 4, 30:158], in0=UT[:, hf * 512:(hf + 1) * 512].rearrange('p (c t) -> p c t', c=4),
                                                               in1=sgT[:].rearrange('p (c t) -> p c t', c=4), op=ALU.mult), ['UT', 'sgT'], ['ybuf'])
                        elif g in (12, 13):
                            A(lambda e, ps=ps, g=g: e.activation(out=SA_[:, (g - 12) * 512:(g - 11) * 512], in_=ps[:, :], func=AF.Sigmoid), [P(pi_)], ['SA'])
                        else:
                            A(lambda e, ps=ps, g=g: e.activation(out=SBg[:, (g - 14) * 512:(g - 13) * 512], in_=ps[:, :], func=AF.Sigmoid), [P(pi_)], ['SBg'])
                    pS = nextps()
                    for h in range(4):
                        for ab in range(2):
                            T(lambda e, h=h, ab=ab, pS=pS: e.matmul(PS[pS][:, h * 128:(h + 1) * 128], lhsT=KT[:, h, ab, :], rhs=QT[:, h, ab, :],
                                                                   start=(ab == 0), stop=(ab == 1)), ['KT', 'QT'], [P(pS)])
                    V(lambda e, pS=pS: e.tensor_tensor(out=SmT[:], in0=PS[pS][:, :], in1=maskT[:], op=ALU.mult), [P(pS), 'maskT'], ['SmT'])
                    pK = nextps(); psb = PS[pK][:].bitcast(BF16)
                    for h in range(4):
                        for ab in range(2):
                            c = h * 2 + ab
                            T(lambda e, h=h, ab=ab, c=c, psb=psb: e.transpose(out=psb[:, c * 128:(c + 1) * 128], in_=KT[:, h, ab, :], identity=identb[:]),
                              ['KT', 'identb'], [P(pK)])
                    V(lambda e, psb=psb: e.tensor_tensor(out=Kd[:], in0=psb.rearrange('p (h f) -> p h f', h=4),
                                                         in1=kdec[:].unsqueeze(2).to_broadcast([128, 4, 256]), op=ALU.mult), [P(pK), 'kdec'], ['Kd'])
                    for hp in range(2):
                        pO = nextps()
                        for hl in range(2):
                            h = hp * 2 + hl
                            reg = PS[pO][:, hl * 256:(hl + 1) * 256]
                            T(lambda e, reg=reg, h=h: e.matmul(reg, lhsT=SmT[:, h * 128:(h + 1) * 128], rhs=Vb[:, h * 256:(h + 1) * 256], start=True, stop=False),
                              ['SmT', 'Vb'], [P(pO)])
                            for ab in range(2):
                                T(lambda e, reg=reg, h=h, ab=ab: e.matmul(reg, lhsT=QTD[:, h, ab, :], rhs=STB[:, h, ab, :], start=False, stop=(ab == 1)),
                                  ['QTD', 'STB'], [P(pO)])
                        for hl in range(2):
                            h = hp * 2 + hl
                            reg = PS[pO][:, hl * 256:(hl + 1) * 256]
                            V(lambda e, reg=reg, h=h: e.bn_stats(out=st6h[:, h, :], in_=reg), [P(pO)], ['st6h'])
                            V(lambda e, h=h: e.bn_aggr(out=mvh[:, h, :], in_=st6h[:, h, :]), ['st6h'], ['mvh'])
                            A(lambda e, h=h: e.activation(out=rstdh[:, h:h + 1], in_=mvh[:, h, 1:2], func=AF.Sqrt, bias=epsc[:, 0:1], scale=1.0), ['mvh', 'epsc'], ['rstdh']); V(lambda e, h=h: e.reciprocal(out=rstdh[:, h:h + 1], in_=rstdh[:, h:h + 1]), ['rstdh'], ['rstdh'])
                            V(lambda e, reg=reg, h=h: e.tensor_scalar(out=retn[:, h * 256:(h + 1) * 256], in0=reg, scalar1=mvh[:, h, 0:1], scalar2=rstdh[:, h:h + 1],
                                                                       op0=ALU.subtract, op1=ALU.mult), [P(pO), 'mvh', 'rstdh'], ['retn'])
                    for h in range(4):
                        pU = nextps()
                        for ab in range(2):
                            T(lambda e, pU=pU, h=h, ab=ab: e.matmul(PS[pU][:, ab * 256:(ab + 1) * 256], lhsT=Kd[:, h, ab * 128:(ab + 1) * 128], rhs=Vb[:, h * 256:(h + 1) * 256],
                                                                   start=True, stop=True), ['Kd', 'Vb'], [P(pU)])
                        V(lambda e, pU=pU, h=h: e.scalar_tensor_tensor(out=STF[:, h, :], in0=STF[:, h, :], scalar=HC['gam128'][h], in1=PS[pU][:, :], op0=ALU.mult, op1=ALU.add),
                          [P(pU), 'STF'], ['STF'])
                        A(lambda e, h=h: e.copy(out=STB[:, h, :, :], in_=STF[:, h, :].rearrange('p (a f) -> p a f', a=2)), ['STF'], ['STB'])
                    G(lambda e: e.tensor_tensor(out=MR[:], in0=SG[:], in1=SA_[:], op=ALU.mult), ['SG', 'SA'], ['MR'])
                    V(lambda e: e.tensor_tensor(out=MR[:], in0=MR[:], in1=retn[:], op=ALU.mult), ['MR', 'retn'], ['MR'])
                    pcs = []
                    for hf in range(2):
                        pC = nextps(); pcs.append(pC)
                        for cl in range(4):
                            cc = hf * 4 + cl
                            reg = PS[pC][:, cl * 128:(cl + 1) * 128]
                            for j in range(31):
                                T(lambda e, reg=reg, cc=cc, j=j: e.matmul(reg, lhsT=ybuf[:, cc, j:j + 128], rhs=DG[:, cc, j, :], start=(j == 0), stop=False),
                                  ['ybuf', 'DG'], [P(pC)])
                            T(lambda e, reg=reg, cc=cc: e.matmul(reg, lhsT=ones[0:1, :], rhs=dwbr[0:1, cc * 128:(cc + 1) * 128], start=False, stop=True),
                              ['ones', 'dwbr'], [P(pC)])
                        V(lambda e, pC=pC, hf=hf: e.bn_stats(out=st6[:, hf, :], in_=PS[pC][:, :]), [P(pC)], ['st6'])
                    G(lambda e: e.tensor_copy(out=ybuf[:, :, 0:30], in_=ybuf[:, :, 128:158]), ['ybuf'], ['ybuf'])
                    V(lambda e: e.bn_aggr(out=mv[:], in_=st6[:]), ['st6'], ['mv'])
                    A(lambda e: e.activation(out=rstd[:], in_=mv[:, 1:2], func=AF.Sqrt, bias=epsc[:, 0:1], scale=1.0), ['mv', 'epsc'], ['rstd']); V(lambda e: e.reciprocal(out=rstd[:], in_=rstd[:]), ['rstd'], ['rstd'])
                    for hf in range(2):
                        V(lambda e, hf=hf, pcs=pcs: e.tensor_scalar(out=z[:, hf * 512:(hf + 1) * 512], in0=PS[pcs[hf]][:, :], scalar1=mv[:, 0:1], scalar2=rstd[:, 0:1],
                                                           op0=ALU.subtract, op1=ALU.mult), [P(pcs[hf]), 'mv', 'rstd'], ['z'])
                    for hf in range(2):
                        pZ = nextps()
                        for j in range(4):
                            k = hf * 4 + j
                            T(lambda e, pZ=pZ, j=j, k=k: e.transpose(out=PS[pZ][:, j * 128:(j + 1) * 128], in_=z[:, k * 128:(k + 1) * 128], identity=identf[:]),
                              ['z', 'identf'], [P(pZ)])
                        for j in range(4):
                            k = hf * 4 + j
                            V(lambda e, pZ=pZ, j=j, k=k: e.tensor_scalar(out=xn[:, k * 128:(k + 1) * 128], in0=PS[pZ][:, j * 128:(j + 1) * 128], scalar1=clng[:, k:k + 1], scalar2=clnb[:, k:k + 1],
                                                                          op0=ALU.mult, op1=ALU.add), [P(pZ), 'clng', 'clnb'], ['xn'])
                    A(lambda e: e.activation(out=sT[:], in_=xn[:].rearrange('p (k t) -> p k t', k=KD), func=AF.Silu), ['xn'], ['sT'])
                    for hf in range(2):
                        pD = nextps()
                        for cc in range(KD):
                            T(lambda e, pD=pD, cc=cc, hf=hf: e.matmul(PS[pD][:, :], lhsT=sT[:, cc, :], rhs=WCO[:, cc, hf * 512:(hf + 1) * 512], start=(cc == 0), stop=False),
                              ['sT', 'WCO'], [P(pD)])
                        T(lambda e, pD=pD, hf=hf: e.matmul(PS[pD][:, :], lhsT=ones[0:1, :], rhs=bcor[0:1, hf * 512:(hf + 1) * 512], start=False, stop=True),
                          ['ones', 'bcor'], [P(pD)])
                        V(lambda e, pD=pD, hf=hf: e.tensor_tensor(out=tmpw[:, hf * 512:(hf + 1) * 512], in0=PS[pD][:, :], in1=SBg[:, hf * 512:(hf + 1) * 512], op=ALU.mult),
                          [P(pD), 'SBg'], ['SG'])
                    G(lambda e: e.tensor_tensor(out=MG[:], in0=tmpw[:], in1=MR[:], op=ALU.add), ['SG', 'MR'], ['z'])
                    for hf in range(2):
                        pM = nextps()
                        for j in range(4):
                            k = hf * 4 + j
                            T(lambda e, pM=pM, j=j, k=k: e.transpose(out=PS[pM][:, j * 128:(j + 1) * 128], in_=MG[:, k * 128:(k + 1) * 128], identity=identf[:]),
                              ['z', 'identf'], [P(pM)])
                        A(lambda e, pM=pM, hf=hf: e.copy(out=MT[:, hf * 4:(hf + 1) * 4, :], in_=PS[pM][:, :].rearrange('p (c t) -> p c t', c=4)), [P(pM)], ['MT'])
                    for hf in range(2):
                        pX = nextps()
                        for k in range(KD):
                            T(lambda e, pX=pX, k=k, hf=hf: e.matmul(PS[pX][:, :], lhsT=MT[:, k, :], rhs=WO[:, k, hf * 512:(hf + 1) * 512], start=(k == 0), stop=(k == KD - 1)),
                              ['MT', 'WO'], [P(pX)])
                        V(lambda e, pX=pX, hf=hf: e.tensor_tensor(out=tmpw[:, hf * 512:(hf + 1) * 512], in0=PS[pX][:, :], in1=g1B[:, hf * 512:(hf + 1) * 512], op=ALU.mult),
                          [P(pX), 'g1B'], ['SG'])
                    V(lambda e: e.scalar_tensor_tensor(out=x1p[:], in0=xt[:], scalar=ALPHA, in1=tmpw[:], op0=ALU.mult, op1=ALU.add), [XK, 'SG'], ['retn'])
                    for hf in range(2):
                        V(lambda e, hf=hf: e.bn_stats(out=st6[:, hf, :], in_=x1p[:, hf * 512:(hf + 1) * 512]), ['retn'], ['st6'])
                    V(lambda e: e.bn_aggr(out=mv[:], in_=st6[:]), ['st6'], ['mv'])
                    A(lambda e: e.activation(out=rstd[:], in_=mv[:, 1:2], func=AF.Sqrt, bias=epsc[:, 0:1], scale=1.0), ['mv', 'epsc'], ['rstd']); V(lambda e: e.reciprocal(out=rstd[:], in_=rstd[:]), ['rstd'], ['rstd'])
                    V(lambda e: e.tensor_scalar(out=x1p[:], in0=x1p[:], scalar1=mv[:, 0:1], scalar2=rstd[:, 0:1], op0=ALU.subtract, op1=ALU.mult),
                      ['retn', 'mv', 'rstd'], ['retn'])
                    dma(x1_d[g0:g0 + 128, :], x1p[:], reads=['retn'], semkey='x1o')
            R.barrier()
        def tt(eng, out, in0, in1, op, reads, writes):
            return R.op(eng, lambda e: e.tensor_tensor(out=out, in0=in0, in1=in1, op=op), reads, writes)

        def ts(eng, out, in0, s1, s2, op0, op1, reads, writes):
            if op1 is None:
                return R.op(eng, lambda e: e.tensor_scalar(out=out, in0=in0, scalar1=s1, scalar2=None, op0=op0), reads, writes)
            return R.op(eng, lambda e: e.tensor_scalar(out=out, in0=in0, scalar1=s1, scalar2=s2, op0=op0, op1=op1), reads, writes)

        def stt(out, in0, sc, in1, op0, op1, reads, writes):
            return R.op('vector', lambda e: e.scalar_tensor_tensor(out=out, in0=in0, scalar=sc, in1=in1, op0=op0, op1=op1), reads, writes)

        def act(out, in_, func, reads, writes, bias=None):
            if bias is None:
                return R.op('scalar', lambda e: e.activation(out=out, in_=in_, func=func), reads, writes)
            return R.op('scalar', lambda e: e.activation(out=out, in_=in_, func=func, bias=bias, scale=1.0), reads, writes)

        def cp(eng, out, in_, reads, writes):
            if eng == 'scalar':
                return R.op('scalar', lambda e: e.copy(out=out, in_=in_), reads, writes)
            return R.op(eng, lambda e: e.tensor_copy(out=out, in_=in_), reads, writes)

        def mm(out, lhsT, rhs, start, stop, reads, writes):
            return R.op('tensor', lambda e: e.matmul(out, lhsT=lhsT, rhs=rhs, start=start, stop=stop), reads, writes)

        def tr(out, in_, ident, reads, writes):
            return R.op('tensor', lambda e: e.transpose(out=out, in_=in_, identity=ident), reads, writes)

        def ln_stats(src, skey, st6_, mv_, rstd_, pfx):
            for hf in range(2):
                R.op('vector', (lambda hf: lambda e: e.bn_stats(out=st6_[:, hf, :], in_=src[:, hf * 512:(hf + 1) * 512]))(hf), [skey], [pfx + 'st6'])
            R.op('vector', lambda e: e.bn_aggr(out=mv_[:], in_=st6_[:]), [pfx + 'st6'], [pfx + 'mv'])
            act(rstd_[:], mv_[:, 1:2], AF.Sqrt, [pfx + 'mv', 'epsc'], [pfx + 'rstd'], bias=epsc[:, 0:1])
            R.op('vector', lambda e: e.reciprocal(out=rstd_[:], in_=rstd_[:]), [pfx + 'rstd'], [pfx + 'rstd'])

        with ExitStack() as pb:
            def sB(name, shape, dt): return sbt(pb, 'b1_' + name, shape, dt)
            WQ = sB('WQ', [128, KD, 2048], BF16); SKT = sB('SKT', [128, 16, 128], BF16)
            ln1gB = sB('ln1gB', [128, D], F32); ln1bB = sB('ln1bB', [128, D], F32)
            x1t = sB('x1t', [128, D], F32); h2T = sB('h2T', [128, KD, 128], BF16)
            sc_ = sB('sc', [128, 16, 128], F32)
            top_ = sB('top', [128, 16, 16], F32)
            cand = sB('cand', [128, 8, 16, 16], F32); best = sB('best', [128, 8, 16], F32)
            db = sB('db', [128, 8, 16], F32); Zs = sB('Zs', [128, 8], F32); rZ = sB('rZ', [128, 8], F32); nb0 = sB('nb0', [128, 8], F32)
            Lb = [sB('Lb%d' % i, [128, 16, 128], F32) for i in range(2)]; Eb1 = sB('Eb1', [128, 16, 128], F32)
            xn2 = Eb1[:].rearrange('p a b -> p (a b)')[:, 0:D]
            qTs = Lb[1][:].rearrange('p a b -> p (a b)').bitcast(BF16)[:, 0:2048].rearrange('p (c t) -> p c t', c=16)
            Ebs = [cand[:].rearrange('p h a b -> p (h a) b').rearrange('p (x y) b -> p x (y b)', x=16), Eb1[:]]
            Wa = sB('Wa', [128, 8, 16, 128], BF16); Wb = sB('Wb', [128, 8, 16, 128], BF16)
            AT = sB('AT', [128, 128, 64], BF16); BT = sB('BT', [128, 128, 64], BF16)
            Gs = sB('Gs', [128, 128, 64], BF16)
            st6b = sB('st6', [128, 2, 6], F32); mvb = sB('mv', [128, 2], F32); rstdb = sB('rstd', [128, 1], F32)
            stgq = Lb[0][:].rearrange('p a b -> p (a b)')
            for k in range(KD):
                dma(stgq, wq_d[k * 128:(k + 1) * 128, :], writes=[('Lb', 0)], semkey='stgq')
                cp('vector', WQ[:, k, :], stgq, [('Lb', 0)], ['WQ'])
            for c2 in range(16):
                dma(stgq[:, 0:128], skT_d[c2, :, :], writes=[('Lb', 0)], semkey='stgq')
                cp('vector', SKT[:, c2, :], stgq[:, 0:128], [('Lb', 0)], ['SKT'])
            dma(ln1gB[:], ln1g_d.partition_broadcast(128), writes=['ln1gB'], semkey='b1c1')
            dma(ln1bB[:], ln1b_d.partition_broadcast(128), writes=['ln1bB'], semkey='b1c2')
            def b1_front(b, t):
                g0 = b * S + t * 128; tile_i = b * NT + t
                dma(x1t[:], x1_d[g0:g0 + 128, :], writes=['x1t'], semkey='x1t')
                tt('gpsimd', x1t[:], x1t[:], ln1gB[:], ALU.mult, ['x1t', 'ln1gB'], ['x1t'])
                tt('gpsimd', x1t[:], x1t[:], ln1bB[:], ALU.add, ['x1t', 'ln1bB'], ['x1t'])
                dma(x1_d[g0:g0 + 128, :], x1t[:], reads=['x1t'], semkey='x1tw')
                ln_stats(x1t, 'x1t', st6b, mvb, rstdb, 'b1')
                ts('vector', xn2, x1t[:], mvb[:, 0:1], rstdb[:, 0:1], ALU.subtract, ALU.mult, ['x1t', 'b1mv', 'b1rstd'], [('Eb', 1)])
                for hf in range(2):
                    pi_ = nextps()
                    for j in range(4):
                        k = hf * 4 + j
                        tr(PS[pi_][:, j * 128:(j + 1) * 128], xn2[:, k * 128:(k + 1) * 128], identf[:], [('Eb', 1), 'identf'], [P(pi_)])
                    for j in range(4):
                        k = hf * 4 + j
                        ts('vector', h2T[:, k, :], PS[pi_][:, j * 128:(j + 1) * 128], sc2p[:, k, b:b + 1], modT[:, 24 + k, b:b + 1], ALU.mult, ALU.add,
                           [P(pi_), 'sc2p', 'modT'], ['h2T'])
                dma(h2T_d[tile_i], h2T[:], reads=['h2T'], semkey='h2Tw')
                for c0 in range(0, 16, 4):
                    pi_ = nextps()
                    for cl in range(4):
                        c = c0 + cl
                        for k in range(KD):
                            mm(PS[pi_][:, cl * 128:(cl + 1) * 128], WQ[:, k, c * 128:(c + 1) * 128], h2T[:, k, :], k == 0, k == KD - 1, ['WQ', 'h2T'], [P(pi_)])
                    cp('scalar', qTs[:, c0:c0 + 4, :], PS[pi_][:, :].rearrange('p (c t) -> p c t', c=4), [P(pi_)], [('Lb', 1)])
                for c0 in range(0, 16, 4):
                    pi_ = nextps()
                    for cl in range(4):
                        c = c0 + cl
                        mm(PS[pi_][:, cl * 128:(cl + 1) * 128], qTs[:, c, :], SKT[:, c, :], True, True, [('Lb', 1), 'SKT'], [P(pi_)])
                    cp('scalar', sc_[:, c0:c0 + 4, :], PS[pi_][:, :].rearrange('p (c t) -> p c t', c=4), [P(pi_)], ['sc'])
                for c in range(16):
                    R.op('vector', (lambda c: lambda e: e.max(out=top_[:, c, 0:8], in_=sc_[:, c, :]))(c), ['sc'], [('top', c)])
                for c in range(16):
                    R.op('vector', (lambda c: lambda e: e.match_replace(out=Lb[0][:, c, :], in_to_replace=top_[:, c, 0:8], in_values=sc_[:, c, :], imm_value=NEG))(c),
                         ['sc', ('top', c)], [('wk', c), ('Lb', 0)])
                for c in range(16):
                    R.op('vector', (lambda c: lambda e: e.max(out=top_[:, c, 8:16], in_=Lb[0][:, c, :]))(c), [('wk', c)], [('top', c), 'top'])
                topv = top_[:].rearrange('p (h s) i -> p h s i', s=2)
                tt('vector', cand[:], topv[:, :, 0, :].unsqueeze(3).to_broadcast([128, 8, 16, 16]),
                   topv[:, :, 1, :].unsqueeze(2).to_broadcast([128, 8, 16, 16]), ALU.add, ['top'], ['cand', ('Eb', 0)])
                Lw = Lb[1][:].rearrange('p a b -> p (a b)').rearrange('p (h x) -> p h x', h=8)
                for h in range(8):
                    cv = cand[:, h, :, :].rearrange('p a b -> p (a b)')
                    R.op('vector', (lambda h, cv: lambda e: e.max(out=best[:, h, 0:8], in_=cv))(h, cv), ['cand'], [('best', h)])
                for h in range(8):
                    cv = cand[:, h, :, :].rearrange('p a b -> p (a b)')
                    R.op('vector', (lambda h, cv: lambda e: e.match_replace(out=Lw[:, h, :], in_to_replace=best[:, h, 0:8], in_values=cv, imm_value=NEG))(h, cv),
                         ['cand', ('best', h)], [('wk2', h), ('Lb', 1)])
                for h in range(8):
                    R.op('vector', (lambda h: lambda e: e.max(out=best[:, h, 8:16], in_=Lw[:, h, :]))(h), [('wk2', h)], [('best', h), 'best'])
            def b1_s4(b, t):
                topv = top_[:].rearrange('p (h s) i -> p h s i', s=2)
                tt('vector', db[:], best[:], best[:, :, 0:1].to_broadcast([128, 8, 16]), ALU.subtract, ['best'], ['db'])
                act(db[:], db[:], AF.Exp, ['db'], ['db'])
                R.op('vector', lambda e: e.tensor_reduce(out=Zs[:], in_=db[:], axis=mybir.AxisListType.X, op=ALU.add), ['db'], ['Zs'])
                act(rZ[:], Zs[:], AF.Ln, ['Zs'], ['rZ'])
                stt(nb0[:], best[:, :, 0], -1.0, rZ[:], ALU.mult, ALU.subtract, ['best', 'rZ'], ['nb0'])
                sv = sc_[:].rearrange('p (h s) k -> p h s k', s=2)
                tt('vector', Wb[:], sv[:, :, 1, :].unsqueeze(2).to_broadcast([128, 8, 16, 128]),
                   topv[:, :, 1, :].unsqueeze(3).to_broadcast([128, 8, 16, 128]), ALU.is_equal, ['sc', 'top'], ['Wb'])
                for h in range(8):
                    q_ = h % 2
                    tt('vector', Lb[q_][:], sc_[:, 2 * h, :].unsqueeze(1).to_broadcast([128, 16, 128]),
                       top_[:, 2 * h + 1, :].unsqueeze(2).to_broadcast([128, 16, 128]), ALU.add, ['sc', 'top'], [('Lb', q_)])
                    act(Ebs[q_], Lb[q_][:], AF.Exp, [('Lb', q_), 'nb0'], [('Eb', q_)] + (['cand'] if q_ == 0 else []), bias=nb0[:, h:h + 1])
                    stt(Wa[:, h, :, :], Lb[q_][:], best[:, h, 15:16], Ebs[q_], ALU.is_ge, ALU.mult, [('Lb', q_), 'best', ('Eb', q_)], ['Wa'])
            def b1_s5(b, t):
                tile_i = b * NT + t
                Wav = Wa[:].rearrange('p h i k -> p (h i) k'); Wbv = Wb[:].rearrange('p h i k -> p (h i) k')
                for half in range(2):
                    p0_, p1_ = half * 64, (half + 1) * 64
                    for (Wv, dstT, wk, dk) in ((Wbv, BT, 'Wb', 'BT'), (Wav, AT, 'Wa', 'AT')):
                        for kb in range(8):
                            pi_ = nextps(); psb = PS[pi_][:].bitcast(BF16)
                            for kl in range(16):
                                kk_ = kb * 16 + kl
                                tr(psb[:, kl * 64:(kl + 1) * 64], Wv[p0_:p1_, :, kk_], identb[p0_:p1_, p0_:p1_], [wk, 'identb'], [P(pi_)])
                            cp('scalar', dstT[:, kb * 16:(kb + 1) * 16, :], psb.rearrange('p (k t) -> p k t', k=16), [P(pi_)], [dk])
                    for t0 in range(0, 64, 4):
                        pi_ = nextps()
                        for tl in range(4):
                            tk = t0 + tl
                            mm(PS[pi_][:, tl * 128:(tl + 1) * 128], BT[:, :, tk], AT[:, :, tk], True, True, ['BT', 'AT'], [P(pi_)])
                        cp('scalar', Gs[:, :, t0:t0 + 4], PS[pi_][:, :].rearrange('p (t k) -> p k t', t=4), [P(pi_)], ['Gs'])
                    dma(G_d[tile_i * 2 + half], Gs[:], reads=['Gs'], semkey='Gw')
            tiles_ = [(b, t) for b in range(NB) for t in range(NT)]
            b1_front(*tiles_[0])
            for ii, bt in enumerate(tiles_):
                b1_s4(*bt)
                if ii + 1 < len(tiles_):
                    b1_front(*tiles_[ii + 1])
                b1_s5(*bt)
            R.barrier()

        finals = []
        GT = min(8, NT)
        with ExitStack() as pc:
            def sC(name, shape, dt): return sbt(pc, 'b2_' + name, shape, dt)
            Ub = [sC('Ub%d' % i, [128, KD, 1024], BF16) for i in range(2)]
            Vbk = [sC('Vb%d' % i, [128, 8, 1024], BF16) for i in range(2)]
            Gb = [sC('Gb%d' % i, [128, 2 * GT, 8, 64], BF16) for i in range(2)]
            H2 = sC('H2', [128, KD, GT * 128], BF16)
            acc = [sC('acc%d' % i, [128, D], F32) for i in range(GT)]
            ln2gB = sC('ln2gB', [128, D], F32); ln2bB = sC('ln2bB', [128, D], F32); g2B = sC('g2B', [128, D], F32)
            gl = [sC('gl%d' % i, [128, 128], BF16) for i in range(4)]; PT = [sC('PT%d' % i, [128, 128], BF16) for i in range(4)]
            xr = sC('xr', [128, D], F32); yp = sC('yp', [128, D], F32)
            st6c = sC('st6', [128, 2, 6], F32); mvc = sC('mv', [128, 2], F32); rstdc = sC('rstd', [128, 1], F32)
            dma(ln2gB[:], ln2g_d.partition_broadcast(128), writes=['ln2gB'], semkey='b2c1')
            dma(ln2bB[:], ln2b_d.partition_broadcast(128), writes=['ln2bB'], semkey='b2c2')
            pslo[0] = 4; psc[0] = 0
            bc = [0]; cc_ = [0]; pc_ = [0]
            for b in range(NB):
                dma(g2B[:], mod_d[b, 5120:6144].partition_broadcast(128), writes=['g2B'], semkey='g2B')
                for gi in range(NT // GT):
                    tile0 = b * NT + gi * GT
                    for tl in range(GT):
                        dma(H2[:, :, tl * 128:(tl + 1) * 128], h2T_d[tile0 + tl], writes=[('H2', tl)], semkey=('H2', tl))
                    pend = []

                    def vstage(ci, bi, j, pp, tl, blk):
                        for dh in range(2):
                            mm(PS[pp * 2 + dh][:, :], PT[ci][:, :], Vbk[bi][:, j, dh * 512:(dh + 1) * 512], j == 0, j == 7,
                               [('PT', ci), ('Vbk', bi)], [P(pp * 2 + dh)])
                        if j == 7:
                            for dh in range(2):
                                if blk == 0:
                                    cp('vector', acc[tl][:, dh * 512:(dh + 1) * 512], PS[pp * 2 + dh][:, :], [P(pp * 2 + dh)], [('acc', tl)])
                                else:
                                    tt('vector', acc[tl][:, dh * 512:(dh + 1) * 512], PS[pp * 2 + dh][:, :], acc[tl][:, dh * 512:(dh + 1) * 512], ALU.add,
                                       [P(pp * 2 + dh), ('acc', tl)], [('acc', tl)])
                    for blk in range(16):
                        bi = bc[0] % 2; bc[0] += 1
                        dma(Ub[bi][:], ubf_d[:, :, blk * 1024:(blk + 1) * 1024], writes=[('Ub', bi)], semkey=('Ub', bi))
                        dma(Vbk[bi][:], vbf_d[:, blk * 8:(blk + 1) * 8, :], writes=[('Vbk', bi)], semkey=('Vbk', bi))
                        dma(Gb[bi][:].rearrange('p h k t -> p h (k t)'),
                            G_d[tile0 * 2:(tile0 + GT) * 2, :, blk * 8:(blk + 1) * 8, :].rearrange('h p k t -> p h (k t)'),
                            writes=[('Gb', bi)], semkey=('Gb', bi))
                        for tl in range(GT):
                            pp = pc_[0] % 2; pc_[0] += 1
                            for j in range(8):
                                ci = cc_[0] % 4; cc_[0] += 1
                                pa = nextps()
                                for k in range(KD):
                                    mm(PS[pa][:, 0:128], Ub[bi][:, k, j * 128:(j + 1) * 128], H2[:, k, tl * 128:(tl + 1) * 128], k == 0, k == KD - 1,
                                       [('Ub', bi), ('H2', tl)], [P(pa)])
                                act(gl[ci][:], PS[pa][:, 0:128], AF.Gelu, [P(pa)], [('gl', ci)])
                                tt('vector', PT[ci][:].rearrange('p (q t) -> p q t', q=2), gl[ci][:].rearrange('p (q t) -> p q t', q=2),
                                   Gb[bi][:, 2 * tl:2 * tl + 2, j, :], ALU.mult, [('gl', ci), ('Gb', bi)], [('PT', ci)])
                                pend.append((ci, bi, j, pp, tl, blk))
                                if len(pend) > 2:
                                    vstage(*pend.pop(0))
                    while pend:
                        vstage(*pend.pop(0))
                    for tl in range(GT):
                        g0 = (tile0 + tl) * 128
                        dma(xr[:], x1_d[g0:g0 + 128, :], writes=['xr'], semkey='xr')
                        tt('gpsimd', acc[tl][:], acc[tl][:], g2B[:], ALU.mult, [('acc', tl), 'g2B'], [('acc', tl)])
                        stt(yp[:], xr[:], ALPHA, acc[tl][:], ALU.mult, ALU.add, ['xr', ('acc', tl)], ['yp'])
                        ln_stats(yp, 'yp', st6c, mvc, rstdc, 'b2')
                        ts('vector', yp[:], yp[:], mvc[:, 0:1], rstdc[:, 0:1], ALU.subtract, ALU.mult, ['yp', 'b2mv', 'b2rstd'], ['yp'])
                        tt('gpsimd', yp[:], yp[:], ln2gB[:], ALU.mult, ['yp', 'ln2gB'], ['yp'])
                        tt('gpsimd', yp[:], yp[:], ln2bB[:], ALU.add, ['yp', 'ln2bB'], ['yp'])
                        finals.append(dma(y_d[g0:g0 + 128, :], yp[:], reads=['yp'], semkey='yout'))
            finals = finals[-1:]
        R.emit(top, finals)
    return nc


def make_in_maps(inputs, n_cores, NB):
    HC = host_consts()
    f = lambda a: np.ascontiguousarray(np.asarray(a))
    shared = {
        'w_ada': f(inputs['w_ada'][0]), 'b_ada': f(np.asarray(inputs['b_ada'][0]).reshape(48, 128).T), 'b_ada_row': f(np.asarray(inputs['b_ada'][0])[None, :]), 'w_in': f(inputs['w_in'][0]),
        'conv_dwT': f(np.asarray(inputs['conv_dw'][0]).T), 'conv_dw_b': f(np.asarray(inputs['conv_dw_b'][0])[None, :]),
        'conv_ln_g': f(np.asarray(inputs['conv_ln_g'][0]).reshape(KD, 128).T), 'conv_ln_b': f(np.asarray(inputs['conv_ln_b'][0]).reshape(KD, 128).T),
        'w_conv_out': f(inputs['w_conv_out'][0]), 'b_conv_out': f(np.asarray(inputs['b_conv_out'][0])[None, :]),
        'w_out': f(inputs['w_out'][0]), 'ln1_g': f(inputs['ln1_g'][0]), 'ln1_b': f(inputs['ln1_b'][0]),
        'ln2_g': f(inputs['ln2_g'][0]), 'ln2_b': f(inputs['ln2_b'][0]), 'peer_wq': f(inputs['peer_wq'][0]),
        'skT': f(np.asarray(inputs['peer_subkeys'][0]).reshape(16, 128, 128).transpose(0, 2, 1)),
        'peer_uT': f(np.asarray(inputs['peer_u'][0]).T), 'peer_v': f(inputs['peer_v'][0]),
        'ident': HC['ident'], 'iota': HC['iota'], 'invf': HC['invf'], 'maskT': HC['maskT'], 'qdec': HC['qdec'], 'kdec': HC['kdec'],
    }
    x = np.asarray(inputs['x']); c = np.asarray(inputs['c']); pos = np.asarray(inputs['positions'])
    S = x.shape[1]
    maps = []
    for i in range(n_cores):
        m = dict(shared)
        m['x'] = f(x[i * NB:(i + 1) * NB].reshape(NB * S, D))
        m['cT'] = f(c[i * NB:(i + 1) * NB].T)
        m['pos'] = f(pos[i * NB:(i + 1) * NB].astype(np.int32))
        maps.append(m)
    return maps


def kernel(**inputs):
    n_cores = 8; NB = 2
    S = np.asarray(inputs['x']).shape[1]
    nc = build(NB, S)
    maps = make_in_maps(inputs, n_cores, NB)
    res = run_bass_kernel_spmd(nc, maps, core_ids=list(range(n_cores)))
    out = np.concatenate([np.asarray(r['y']).reshape(NB, S, D) for r in res.results], axis=0)
    return out.astype(np.float32)
```

```python
import numpy as np
from contextlib import ExitStack
import concourse.bass as bass
import concourse.mybir as mybir
from concourse.bass_utils import run_bass_kernel_spmd

F32 = mybir.dt.float32; BF16 = mybir.dt.bfloat16; I32 = mybir.dt.int32
ALU = mybir.AluOpType; AF = mybir.ActivationFunctionType
ENGS = ['sync', 'scalar', 'vector', 'gpsimd', 'tensor']
D = 1024; KD = 8; NCOL = 8192
ALPHA = float(2.0 ** 0.25); EPS = 1e-5
NEG = -1e30


class Item:
    __slots__ = ('eng', 'fn', 'deps', 'needed', 'dma', 'semkey', 'count', 'sem')

    def __init__(s, eng, fn, dma, semkey):
        s.eng = eng; s.fn = fn; s.deps = []; s.needed = False; s.dma = dma
        s.semkey = semkey; s.count = 0; s.sem = None


class Rec:
    def __init__(s, nc):
        s.nc = nc; s.items = {e: [] for e in ENGS}; s.lastw = {}; s.readers = {}; s.all = []

    def op(s, eng, fn, reads=(), writes=(), dma=False, semkey=None):
        it = Item(eng, fn, dma, semkey)
        deps = {}
        for k in reads:
            w = s.lastw.get(k)
            if w is not None: deps[id(w)] = w
        for k in writes:
            w = s.lastw.get(k)
            if w is not None: deps[id(w)] = w
            for r in s.readers.get(k, ()): deps[id(r)] = r
        for d in deps.values():
            if d is it: continue
            if d.eng == eng and eng == 'tensor' and not d.dma: continue
            it.deps.append(d); d.needed = True
        for k in reads: s.readers.setdefault(k, []).append(it)
        for k in writes: s.lastw[k] = it; s.readers[k] = []
        s.items[eng].append(it); s.all.append(it)
        return it

    def barrier(s):
        lasts = [s.items[e][-1] for e in ENGS if s.items[e]]
        dmas = {}
        for it in s.all:
            if it.dma: dmas[it.semkey] = it
        for e in ENGS:
            it = Item(e, None, False, None)
            for d in lasts + list(dmas.values()):
                if d.fn is None: continue
                if d.eng == e and not d.dma: continue
                it.deps.append(d); d.needed = True
            s.items[e].append(it); s.all.append(it)
        s.lastw = {}; s.readers = {}

    def emit(s, stack, finals=()):
        nc = s.nc
        esem = {e: stack.enter_context(nc.semaphore('sem_' + e)) for e in ENGS}
        dsem = {}; dcnt = {}; cnt = {e: 0 for e in ENGS}
        for it in s.all:
            if it.dma:
                k = it.semkey
                if k not in dsem:
                    dsem[k] = stack.enter_context(nc.semaphore('dsem%d' % len(dsem))); dcnt[k] = 0
                dcnt[k] += 16; it.sem = dsem[k]; it.count = dcnt[k]; it.needed = True
            elif it.needed and it.fn is not None:
                cnt[it.eng] += 1; it.sem = esem[it.eng]; it.count = cnt[it.eng]
        block = stack.enter_context(nc.Block())

        def body(e):
            def run(eng):
                waited = {}
                for it in s.items[e]:
                    for d in it.deps:
                        if d.sem is None: continue
                        if waited.get(id(d.sem), 0) < d.count:
                            eng.wait_ge(d.sem, d.count); waited[id(d.sem)] = d.count
                    if it.fn is None: continue
                    ins = it.fn(eng)
                    if it.dma: ins.then_inc(it.sem, 16)
                    elif it.needed: ins.then_inc(it.sem, 1)
                if e == 'sync':
                    for d in finals:
                        if waited.get(id(d.sem), 0) < d.count:
                            eng.wait_ge(d.sem, d.count); waited[id(d.sem)] = d.count
            return run
        for e in ENGS:
            getattr(block, e)(body(e))


def host_consts():
    c = {}
    c['ident'] = np.eye(128, dtype=np.float32)
    c['iota'] = np.tile(np.arange(128, dtype=np.float32)[None, :], (128, 1))
    c['invf'] = (np.float32(10000.0) ** (-(np.arange(128, dtype=np.float32)) / np.float32(128))).astype(np.float32)[:, None]
    gam = [1.0 - 2.0 ** (-5.0 - h) for h in range(4)]
    idx = np.arange(128)
    mask = np.zeros((128, 4, 128), np.float32)
    qdec = np.zeros((128, 4, 2, 128), np.float32)
    kdec = np.zeros((128, 4), np.float32)
    for h in range(4):
        g = gam[h]
        m = (g ** np.abs(idx[:, None] - idx[None, :]).astype(np.float64)) * ((idx[:, None] // 64) <= (idx[None, :] // 64))
        mask[:, h, :] = (m / 16.0).astype(np.float32)
        qdec[:, h, :, :] = (g ** (idx + 1.0))[None, None, :]
        kdec[:, h] = (g ** (127.0 - idx)) / 16.0
    c['maskT'] = mask.reshape(128, 512)
    c['qdec'] = qdec.reshape(128, 1024)
    c['kdec'] = kdec
    c['gam128'] = [float(g ** 128) for g in gam]
    return c


def build(NB, S, debug=False):
    NT = S // 128
    NTOK = NB * S
    HC = host_consts()
    nc = bass.Bass('TRN2', target_bir_lowering=False)

    def din(name, shape, dt=F32): return nc.dram_tensor(name, shape, dt, kind='ExternalInput').ap()
    x_d = din('x', [NTOK, D]); cT_d = din('cT', [D, NB]); pos_d = din('pos', [NB, S], I32)
    wada_d = din('w_ada', [D, 6 * D]); bada_d = din('b_ada', [128, 48]); badar_d = din('b_ada_row', [1, 6 * D])
    win_d = din('w_in', [D, NCOL]); dwT_d = din('conv_dwT', [D, 31]); dwb_d = din('conv_dw_b', [1, D])
    clng_d = din('conv_ln_g', [128, KD]); clnb_d = din('conv_ln_b', [128, KD])
    wco_d = din('w_conv_out', [D, D]); bco_d = din('b_conv_out', [1, D]); wo_d = din('w_out', [D, D])
    ln1g_d = din('ln1_g', [D]); ln1b_d = din('ln1_b', [D]); ln2g_d = din('ln2_g', [D]); ln2b_d = din('ln2_b', [D])
    wq_d = din('peer_wq', [D, 2048]); skT_d = din('skT', [16, 128, 128])
    uT_d = din('peer_uT', [D, 16384]); v_d = din('peer_v', [16384, D])
    ident_d = din('ident', [128, 128]); iota_d = din('iota', [128, 128]); invf_d = din('invf', [128, 1])
    maskT_d = din('maskT', [128, 512]); qdec_d = din('qdec', [128, 1024]); kdec_d = din('kdec', [128, 4])
    y_d = nc.dram_tensor('y', [NTOK, D], F32, kind='ExternalOutput').ap()
    winbf_d = nc.dram_tensor('winbf', [128, KD, NCOL], BF16, kind='Internal').ap()
    ubf_d = nc.dram_tensor('ubf', [128, KD, 16384], BF16, kind='Internal').ap()
    vbf_d = nc.dram_tensor('vbf', [128, 128, D], BF16, kind='Internal').ap()
    mod_d = nc.dram_tensor('modd', [NB, 6 * D], F32, kind='Internal').ap()
    NTILE = NTOK // 128
    G_d = nc.dram_tensor('Gd', [NTILE * 2, 128, 128, 64], BF16, kind='Internal').ap()
    h2T_d = nc.dram_tensor('h2Td', [NTILE, 128, KD, 128], BF16, kind='Internal').ap()
    x1_d = nc.dram_tensor('x1d', [NTOK, D], F32, kind='ExternalOutput' if debug else 'Internal').ap()

    with ExitStack() as top:
        R = Rec(nc)
        PS = [top.enter_context(nc.psum_tensor('ps%d' % i, [128, 512], F32)) for i in range(8)]
        psc = [0]; pslo = [0]

        def nextps():
            i = pslo[0] + psc[0] % (8 - pslo[0]); psc[0] += 1
            return i

        def P(i): return ('ps', i)

        def dma(out, in_, reads=(), writes=(), semkey=None, eng='sync', **kw):
            return R.op(eng, lambda e: e.dma_start(out=out, in_=in_, **kw), reads=reads, writes=writes, dma=True, semkey=semkey)

        def V(fn, reads=(), writes=()): return R.op('vector', fn, reads, writes)
        def A(fn, reads=(), writes=()): return R.op('scalar', fn, reads, writes)
        def G(fn, reads=(), writes=()): return R.op('gpsimd', fn, reads, writes)
        def T(fn, reads=(), writes=()): return R.op('tensor', fn, reads, writes)

        def sbt(st, name, shape, dt): return st.enter_context(nc.sbuf_tensor('s_' + name, shape, dt))
        identf = sbt(top, 'identf', [128, 128], F32); identb = sbt(top, 'identb', [128, 128], BF16)
        ones = sbt(top, 'ones', [1, 128], BF16)
        modT = sbt(top, 'modT', [128, 48, NB], F32)
        sc1p = sbt(top, 'sc1p', [128, 8, NB], F32); sc2p = sbt(top, 'sc2p', [128, 8, NB], F32)
        dma(identf[:], ident_d, writes=['identf'], semkey='identf')
        V(lambda e: e.tensor_copy(out=identb[:], in_=identf[:]), ['identf'], ['identb'])
        V(lambda e: e.memset(ones[:], 1.0), [], ['ones'])
        epsc = sbt(top, 'epsc', [128, 1], F32)
        V(lambda e: e.memset(epsc[:], EPS), [], ['epsc'])

        with ExitStack() as p0:
            stgf = [sbt(p0, 'stgf%d' % i, [128, 4096], F32) for i in range(2)]
            stgb = [sbt(p0, 'stgb%d' % i, [128, 4096], BF16) for i in range(2)]
            cnt = [0]

            def conv_block(src_ap, dst_ap, shape3=None):
                i = cnt[0] % 2; cnt[0] += 1
                sf = stgf[i][:]; sbv = stgb[i][:]
                if shape3 is not None:
                    sf = sf.rearrange('p (a b) -> p a b', a=shape3); sbv = sbv.rearrange('p (a b) -> p a b', a=shape3)
                dma(sf, src_ap, writes=[('stgf', i)], semkey=('stgf', i))
                if i == 0:
                    V(lambda e: e.tensor_copy(out=stgb[i][:], in_=stgf[i][:]), [('stgf', i)], [('stgb', i)])
                else:
                    A(lambda e: e.copy(out=stgb[i][:], in_=stgf[i][:]), [('stgf', i)], [('stgb', i)])
                dma(dst_ap, sbv, reads=[('stgb', i)], semkey=('stgbo', i))
            for k in range(KD):
                for eb in range(4):
                    conv_block(uT_d[k * 128:(k + 1) * 128, eb * 4096:(eb + 1) * 4096], ubf_d[:, k, eb * 4096:(eb + 1) * 4096])
            vv = v_d.rearrange('(j p) d -> p j d', p=128)
            for jb in range(32):
                conv_block(vv[:, jb * 4:(jb + 1) * 4, :], vbf_d[:, jb * 4:(jb + 1) * 4, :], shape3=4)
            for k in range(KD):
                for cb in range(2):
                    conv_block(win_d[k * 128:(k + 1) * 128, cb * 4096:(cb + 1) * 4096], winbf_d[:, k, cb * 4096:(cb + 1) * 4096])

            cTs = sbt(p0, 'cTs', [128, KD, NB], F32); siluT = sbt(p0, 'siluT', [128, KD, NB], F32)
            badaT = sbt(p0, 'badaT', [128, 48], F32)
            wab = [sbt(p0, 'wab%d' % i, [128, KD, 512], F32) for i in range(2)]
            dma(cTs[:], cT_d.rearrange('(k p) b -> p k b', p=128), writes=['cTs'], semkey='cTs')
            dma(badaT[:], bada_d, writes=['badaT'], semkey='badaT')
            A(lambda e: e.activation(out=siluT[:], in_=cTs[:], func=AF.Silu), ['cTs'], ['siluT'])
            pm = nextps(); pm2 = [nextps(), nextps()]
            onesf = sbt(p0, 'onesf', [1, 8], F32); badar = sbt(p0, 'badar', [1, 6 * D], F32); modrow = sbt(p0, 'modrow', [NB, 6 * D], F32)
            V(lambda e: e.memset(onesf[:], 1.0), [], ['onesf'])
            dma(badar[:], badar_d, writes=['badar'], semkey='badar')
            for blk in range(12):
                wb_ = wab[blk % 2]
                dma(wb_[:], wada_d.rearrange('(k p) c -> p k c', p=128)[:, :, blk * 512:(blk + 1) * 512],
                    writes=[('wab', blk % 2)], semkey=('wab', blk % 2))
                q2 = pm2[blk % 2]
                for k in range(KD):
                    T(lambda e, wb_=wb_, k=k, q2=q2: e.matmul(PS[q2][0:NB, :], lhsT=siluT[:, k, :], rhs=wb_[:, k, :], start=(k == 0), stop=False),
                      [('wab', blk % 2), 'siluT'], [P(q2)])
                T(lambda e, q2=q2, blk=blk: e.matmul(PS[q2][0:NB, :], lhsT=onesf[0:1, 0:NB], rhs=badar[0:1, blk * 512:(blk + 1) * 512], start=False, stop=True),
                  ['onesf', 'badar'], [P(q2)])
                V(lambda e, q2=q2, blk=blk: e.tensor_copy(out=modrow[:, blk * 512:(blk + 1) * 512], in_=PS[q2][0:NB, :]), [P(q2)], ['modrow'])
                for c4 in range(4):
                    kk = blk * 4 + c4
                    for k in range(KD):
                        T(lambda e, wb_=wb_, k=k, c4=c4, kk=kk: e.matmul(
                            PS[pm][:, kk * NB:(kk + 1) * NB], lhsT=wb_[:, k, c4 * 128:(c4 + 1) * 128], rhs=siluT[:, k, :],
                            start=(k == 0), stop=(k == KD - 1)), [('wab', blk % 2), 'siluT'], [P(pm)])
            V(lambda e: e.tensor_tensor(out=modT[:], in0=PS[pm][:, 0:48 * NB].rearrange('p (k b) -> p k b', b=NB),
                                        in1=badaT[:].unsqueeze(2).to_broadcast([128, 48, NB]), op=ALU.add),
              [P(pm), 'badaT'], ['modT'])
            V(lambda e: e.tensor_scalar(out=sc1p[:], in0=modT[:, 8:16, :], scalar1=1.0, scalar2=None, op0=ALU.add), ['modT'], ['sc1p'])
            V(lambda e: e.tensor_scalar(out=sc2p[:], in0=modT[:, 32:40, :], scalar1=1.0, scalar2=None, op0=ALU.add), ['modT'], ['sc2p'])
            dma(mod_d, modrow[:], reads=['modrow'], semkey='modout')
            R.barrier()

        with ExitStack() as pa:
            def sa(name, shape, dt): return sbt(pa, name, shape, dt)
            XT = [sa('xt0', [128, D], F32)] * 2
            WB = [sa('wblk%d' % i, [128, KD, 512], BF16) for i in range(2)]
            DG = sa('dg', [128, KD, 31, 128], BF16)
            dwT = sa('dwT', [128, KD, 31], F32)
            WCO = sa('wco', [128, KD, D], BF16); WO = sa('wo', [128, KD, D], BF16)
            dwbr_f = sa('dwbr_f', [1, D], F32); dwbr = sa('dwbr', [1, D], BF16)
            bcor_f = sa('bcor_f', [1, D], F32); bcor = sa('bcor', [1, D], BF16)
            clng = sa('clng', [128, KD], F32); clnb = sa('clnb', [128, KD], F32)
            g1B = sa('g1B', [128, D], F32)
            invf = sa('invf', [128, 1], F32); maskT = sa('maskT', [128, 512], F32)
            qdecf = sa('qdecf', [128, 1024], F32) if False else None; qdec = sa('qdec', [128, 1024], BF16); kdec = sa('kdec', [128, 4], F32)
            st6 = sa('st6', [128, 2, 6], F32); mv = sa('mv', [128, 2], F32); rstd = sa('rstd', [128, 1], F32)
            st6h = sa('st6h', [128, 4, 6], F32); mvh = sa('mvh', [128, 4, 2], F32); rstdh = sa('rstdh', [128, 4], F32)
            xn = sa('xn', [128, D], F32); h1T = sa('h1T', [128, KD, 128], BF16)
            posi = sa('posi', [128, 128], I32); posf = sa('posf', [128, 128], F32); ang = sa('ang', [128, 128], F32)
            ki = sa('ki', [128, 128], I32); kf = sa('kf', [128, 128], F32); rr = sa('rr', [128, 128], F32)
            rc = sa('rc', [128, 128], F32); tmpa = sa('tmpa', [128, 128], F32)
            sinT = sa('sinT', [128, 128], F32); cosT = sa('cosT', [128, 128], F32)
            t1 = sa('t1', [128, 2, 128], F32); t2 = sa('t2', [128, 2, 128], F32)
            QT = sa('QT', [128, 4, 2, 128], BF16); KT = sa('KT', [128, 4, 2, 128], BF16); QTD = sa('QTD', [128, 4, 2, 128], BF16)
            Vb = sa('Vb', [128, D], BF16); SG = sa('SG', [128, D], F32); SA_ = sa('SA', [128, D], BF16); SBg = sa('SBg', [128, D], BF16)
            UT = sa('UT', [128, 1024], BF16); sgT = sa('sgT', [128, 512], BF16)
            ybuf = sa('ybuf', [128, KD, 158], BF16)
            SmT = sa('SmT', [128, 512], BF16); Kd = sa('Kd', [128, 4, 256], BF16)
            STF = sa('STF', [128, 4, 512], F32); STB = sa('STB', [128, 4, 2, 256], BF16)
            retn = sa('retn', [128, D], F32); MR = sa('MR', [128, D], F32)
            z = sa('z', [128, D], F32); stg = z; sT = sa('sT', [128, KD, 128], BF16)
            MG = z; MT = sa('MT', [128, KD, 128], BF16)
            x1p = retn
            tmpw = SG

            dma(invf[:], invf_d, writes=['invf'], semkey='c1'); dma(maskT[:], maskT_d, writes=['maskT'], semkey='c2')
            dma(z[:], qdec_d, writes=['z'], semkey='c3'); V(lambda e: e.tensor_copy(out=qdec[:], in_=z[:]), ['z'], ['qdec']); dma(kdec[:], kdec_d, writes=['kdec'], semkey='c4')
            dma(clng[:], clng_d, writes=['clng'], semkey='c5')
            dma(clnb[:], clnb_d, writes=['clnb'], semkey='c6')
            dma(dwbr_f[:], dwb_d, writes=['dwbr_f'], semkey='c9'); dma(bcor_f[:], bco_d, writes=['bcor_f'], semkey='c10')
            V(lambda e: e.tensor_copy(out=dwbr[:], in_=dwbr_f[:]), ['dwbr_f'], ['dwbr'])
            V(lambda e: e.tensor_copy(out=bcor[:], in_=bcor_f[:]), ['bcor_f'], ['bcor'])
            dma(dwT[:], dwT_d.rearrange('(k p) j -> p k j', p=128), writes=['dwT'], semkey='c11')
            for cc in range(KD):
                G(lambda e, cc=cc: e.tensor_tensor(out=DG[:, cc, :, :], in0=identb[:].unsqueeze(1).to_broadcast([128, 31, 128]),
                                                   in1=dwT[:, cc, :].unsqueeze(2).to_broadcast([128, 31, 128]), op=ALU.mult),
                  ['identb', 'dwT'], ['DG'])
            for k in range(KD):
                dma(stg[:], wco_d[k * 128:(k + 1) * 128, :], writes=['z'], semkey='stg')
                V(lambda e, k=k: e.tensor_copy(out=WCO[:, k, :], in_=stg[:]), ['z'], ['WCO'])
            for k in range(KD):
                dma(stg[:], wo_d[k * 128:(k + 1) * 128, :], writes=['z'], semkey='stg')
                V(lambda e, k=k: e.tensor_copy(out=WO[:, k, :], in_=stg[:]), ['z'], ['WO'])

            C1 = float(np.float32(6.28125)); C2 = float(np.float32(2 * np.pi - 6.28125)); PI = float(np.pi); TWO_PI = float(2 * np.pi)
            wcnt = [0]; itc = [0]
            for b in range(NB):
                dma(g1B[:], mod_d[b, 2048:3072].partition_broadcast(128), writes=['g1B'], semkey='g1B')
                G(lambda e: e.memset(STF[:], 0.0), [], ['STF']); G(lambda e: e.memset(STB[:], 0.0), [], ['STB'])
                G(lambda e: e.memset(ybuf[:, :, 0:30], 0.0), [], ['ybuf'])
                for t in range(NT):
                    g0 = b * S + t * 128
                    xi = 0; itc[0] += 1
                    xt = XT[xi]; XK = ('xt', xi)
                    dma(xt[:], x_d[g0:g0 + 128, :], writes=[XK], semkey=XK)
                    for hf in range(2):
                        V(lambda e, hf=hf: e.bn_stats(out=st6[:, hf, :], in_=xt[:, hf * 512:(hf + 1) * 512]), [XK], ['st6'])
                    V(lambda e: e.bn_aggr(out=mv[:], in_=st6[:]), ['st6'], ['mv'])
                    A(lambda e: e.activation(out=rstd[:], in_=mv[:, 1:2], func=AF.Sqrt, bias=epsc[:, 0:1], scale=1.0), ['mv', 'epsc'], ['rstd']); V(lambda e: e.reciprocal(out=rstd[:], in_=rstd[:]), ['rstd'], ['rstd'])
                    V(lambda e: e.tensor_scalar(out=xn[:], in0=xt[:], scalar1=mv[:, 0:1], scalar2=rstd[:, 0:1], op0=ALU.subtract, op1=ALU.mult),
                      [XK, 'mv', 'rstd'], ['xn'])
                    for hf in range(2):
                        pi_ = nextps()
                        for j in range(4):
                            k = hf * 4 + j
                            T(lambda e, pi_=pi_, j=j, k=k: e.transpose(out=PS[pi_][:, j * 128:(j + 1) * 128], in_=xn[:, k * 128:(k + 1) * 128], identity=identf[:]),
                              ['xn', 'identf'], [P(pi_)])
                        for j in range(4):
                            k = hf * 4 + j
                            V(lambda e, pi_=pi_, j=j, k=k, b=b: e.tensor_scalar(out=h1T[:, k, :], in0=PS[pi_][:, j * 128:(j + 1) * 128],
                                                                            scalar1=sc1p[:, k, b:b + 1], scalar2=modT[:, k, b:b + 1], op0=ALU.mult, op1=ALU.add),
                              [P(pi_), 'sc1p', 'modT'], ['h1T'])
                    dma(posi[:], pos_d[b, t * 128:(t + 1) * 128].partition_broadcast(128), writes=['posi'], semkey='posi')
                    V(lambda e: e.tensor_copy(out=posf[:], in_=posi[:]), ['posi'], ['posf'])
                    V(lambda e: e.tensor_scalar(out=ang[:], in0=posf[:], scalar1=invf[:, 0:1], scalar2=None, op0=ALU.mult), ['posf', 'invf'], ['ang'])
                    V(lambda e: e.tensor_scalar(out=ki[:], in0=ang[:], scalar1=float(1 / (2 * np.pi)), scalar2=None, op0=ALU.mult), ['ang'], ['ki'])
                    V(lambda e: e.tensor_copy(out=kf[:], in_=ki[:]), ['ki'], ['kf'])
                    V(lambda e: e.scalar_tensor_tensor(out=rr[:], in0=kf[:], scalar=-C1, in1=ang[:], op0=ALU.mult, op1=ALU.add), ['ang', 'kf'], ['rr'])
                    V(lambda e: e.scalar_tensor_tensor(out=rr[:], in0=kf[:], scalar=-C2, in1=rr[:], op0=ALU.mult, op1=ALU.add), ['rr', 'kf'], ['rr'])
                    V(lambda e: e.tensor_scalar(out=tmpa[:], in0=rr[:], scalar1=PI, scalar2=-TWO_PI, op0=ALU.is_gt, op1=ALU.mult), ['rr'], ['tmpa'])
                    V(lambda e: e.tensor_tensor(out=rr[:], in0=rr[:], in1=tmpa[:], op=ALU.add), ['rr', 'tmpa'], ['rr'])
                    V(lambda e: e.tensor_scalar(out=rc[:], in0=rr[:], scalar1=PI / 2, scalar2=None, op0=ALU.add), ['rr'], ['rc'])
                    V(lambda e: e.tensor_scalar(out=tmpa[:], in0=rc[:], scalar1=PI, scalar2=-TWO_PI, op0=ALU.is_gt, op1=ALU.mult), ['rc'], ['tmpa'])
                    V(lambda e: e.tensor_tensor(out=rc[:], in0=rc[:], in1=tmpa[:], op=ALU.add), ['rc', 'tmpa'], ['rc'])
                    V(lambda e: e.tensor_scalar(out=rr[:], in0=rr[:], scalar1=PI, scalar2=-PI, op0=ALU.min, op1=ALU.max), ['rr'], ['rr'])
                    V(lambda e: e.tensor_scalar(out=rc[:], in0=rc[:], scalar1=PI, scalar2=-PI, op0=ALU.min, op1=ALU.max), ['rc'], ['rc'])
                    A(lambda e: e.activation(out=sinT[:], in_=rr[:], func=AF.Sin), ['rr'], ['sinT'])
                    A(lambda e: e.activation(out=cosT[:], in_=rc[:], func=AF.Sin), ['rc'], ['cosT'])
                    for g in range(16):
                        wi = wcnt[0] % 2; wcnt[0] += 1
                        wb_ = WB[wi]; WK = ('wblk', wi)
                        dma(wb_[:], winbf_d[:, :, g * 512:(g + 1) * 512], writes=[WK], semkey=WK)
                        pi_ = nextps(); ps = PS[pi_]
                        if g in (0, 1, 2, 3, 8, 9, 10, 11):
                            for c4 in range(4):
                                for k in range(KD):
                                    T(lambda e, ps=ps, wb_=wb_, c4=c4, k=k: e.matmul(ps[:, c4 * 128:(c4 + 1) * 128], lhsT=wb_[:, k, c4 * 128:(c4 + 1) * 128],
                                                                                  rhs=h1T[:, k, :], start=(k == 0), stop=(k == KD - 1)), [WK, 'h1T'], [P(pi_)])
                        else:
                            for k in range(KD):
                                T(lambda e, ps=ps, wb_=wb_, k=k: e.matmul(ps[:, :], lhsT=h1T[:, k, :], rhs=wb_[:, k, :], start=(k == 0), stop=(k == KD - 1)),
                                  [WK, 'h1T'], [P(pi_)])
                        if g < 4:
                            dst = QT if g < 2 else KT; dk = 'QT' if g < 2 else 'KT'
                            h0 = (g % 2) * 2
                            psv = ps[:].rearrange('p (h a t) -> p h a t', h=2, a=2)
                            Av = psv[:, :, 0, :]; Bv = psv[:, :, 1, :]
                            cb = cosT[:].unsqueeze(1).to_broadcast([128, 2, 128]); sb_ = sinT[:].unsqueeze(1).to_broadcast([128, 2, 128])
                            V(lambda e, Av=Av, cb=cb: e.tensor_tensor(out=t1[:], in0=Av, in1=cb, op=ALU.mult), [P(pi_), 'cosT'], ['t1'])
                            V(lambda e, Bv=Bv, sb_=sb_: e.tensor_tensor(out=t2[:], in0=Bv, in1=sb_, op=ALU.mult), [P(pi_), 'sinT'], ['t2'])
                            V(lambda e, dst=dst, h0=h0: e.tensor_tensor(out=dst[:, h0:h0 + 2, 0, :], in0=t1[:], in1=t2[:], op=ALU.subtract), ['t1', 't2'], [dk])
                            V(lambda e, Av=Av, sb_=sb_: e.tensor_tensor(out=t1[:], in0=Av, in1=sb_, op=ALU.mult), [P(pi_), 'sinT'], ['t1'])
                            V(lambda e, Bv=Bv, cb=cb: e.tensor_tensor(out=t2[:], in0=Bv, in1=cb, op=ALU.mult), [P(pi_), 'cosT'], ['t2'])
                            V(lambda e, dst=dst, h0=h0: e.tensor_tensor(out=dst[:, h0:h0 + 2, 1, :], in0=t1[:], in1=t2[:], op=ALU.add), ['t1', 't2'], [dk])
                            if g < 2:
                                V(lambda e, h0=h0: e.tensor_tensor(out=QTD[:, h0:h0 + 2, :, :], in0=QT[:, h0:h0 + 2, :, :],
                                                                   in1=qdec[:, h0 * 256:(h0 + 2) * 256].rearrange('p (h a t) -> p h a t', h=2, a=2), op=ALU.mult),
                                  ['QT', 'qdec'], ['QTD'])
                        elif g in (4, 5):
                            A(lambda e, ps=ps, g=g: e.copy(out=Vb[:, (g - 4) * 512:(g - 3) * 512], in_=ps[:, :]), [P(pi_)], ['Vb'])
                        elif g in (6, 7):
                            A(lambda e, ps=ps, g=g: e.activation(out=SG[:, (g - 6) * 512:(g - 5) * 512], in_=ps[:, :], func=AF.Silu), [P(pi_)], ['SG'])
                        elif g in (8, 9):
                            A(lambda e, ps=ps, g=g: e.copy(out=UT[:, (g - 8) * 512:(g - 7) * 512], in_=ps[:, :]), [P(pi_)], ['UT'])
                        elif g in (10, 11):
                            hf = g - 10
                            A(lambda e, ps=ps: e.activation(out=sgT[:], in_=ps[:, :], func=AF.Sigmoid), [P(pi_)], ['sgT'])
                            V(lambda e, hf=hf: e.tensor_tensor(out=ybuf[:, hf * 4:(hf + 1) * 4, 30:158], in0=UT[:, hf * 512:(hf + 1) * 512].rearrange('p (c t) -> p c t', c=4),
                                                               in1=sgT[:].rearrange('p (c t) -> p c t', c=4), op=ALU.mult), ['UT', 'sgT'], ['ybuf'])
                        elif g in (12, 13):
                            A(lambda e, ps=ps, g=g: e.activation(out=SA_[:, (g - 12) * 512:(g - 11) * 512], in_=ps[:, :], func=AF.Sigmoid), [P(pi_)], ['SA'])
                        else:
                            A(lambda e, ps=ps, g=g: e.activation(out=SBg[:, (g - 14) * 512:(g - 13) * 512], in_=ps[:, :], func=AF.Sigmoid), [P(pi_)], ['SBg'])
                    pS = nextps()
                    for h in range(4):
                        for ab in range(2):
                            T(lambda e, h=h, ab=ab, pS=pS: e.matmul(PS[pS][:, h * 128:(h + 1) * 128], lhsT=KT[:, h, ab, :], rhs=QT[:, h, ab, :],
                                                                   start=(ab == 0), stop=(ab == 1)), ['KT', 'QT'], [P(pS)])
                    V(lambda e, pS=pS: e.tensor_tensor(out=SmT[:], in0=PS[pS][:, :], in1=maskT[:], op=ALU.mult), [P(pS), 'maskT'], ['SmT'])
                    pK = nextps(); psb = PS[pK][:].bitcast(BF16)
                    for h in range(4):
                        for ab in range(2):
                            c = h * 2 + ab
                            T(lambda e, h=h, ab=ab, c=c, psb=psb: e.transpose(out=psb[:, c * 128:(c + 1) * 128], in_=KT[:, h, ab, :], identity=identb[:]),
                              ['KT', 'identb'], [P(pK)])
                    V(lambda e, psb=psb: e.tensor_tensor(out=Kd[:], in0=psb.rearrange('p (h f) -> p h f', h=4),
                                                         in1=kdec[:].unsqueeze(2).to_broadcast([128, 4, 256]), op=ALU.mult), [P(pK), 'kdec'], ['Kd'])
                    for hp in range(2):
                        pO = nextps()
                        for hl in range(2):
                            h = hp * 2 + hl
                            reg = PS[pO][:, hl * 256:(hl + 1) * 256]
                            T(lambda e, reg=reg, h=h: e.matmul(reg, lhsT=SmT[:, h * 128:(h + 1) * 128], rhs=Vb[:, h * 256:(h + 1) * 256], start=True, stop=False),
                              ['SmT', 'Vb'], [P(pO)])
                            for ab in range(2):
                                T(lambda e, reg=reg, h=h, ab=ab: e.matmul(reg, lhsT=QTD[:, h, ab, :], rhs=STB[:, h, ab, :], start=False, stop=(ab == 1)),
                                  ['QTD', 'STB'], [P(pO)])
                        for hl in range(2):
                            h = hp * 2 + hl
                            reg = PS[pO][:, hl * 256:(hl + 1) * 256]
                            V(lambda e, reg=reg, h=h: e.bn_stats(out=st6h[:, h, :], in_=reg), [P(pO)], ['st6h'])
                            V(lambda e, h=h: e.bn_aggr(out=mvh[:, h, :], in_=st6h[:, h, :]), ['st6h'], ['mvh'])
                            A(lambda e, h=h: e.activation(out=rstdh[:, h:h + 1], in_=mvh[:, h, 1:2], func=AF.Sqrt, bias=epsc[:, 0:1], scale=1.0), ['mvh', 'epsc'], ['rstdh']); V(lambda e, h=h: e.reciprocal(out=rstdh[:, h:h + 1], in_=rstdh[:, h:h + 1]), ['rstdh'], ['rstdh'])
                            V(lambda e, reg=reg, h=h: e.tensor_scalar(out=retn[:, h * 256:(h + 1) * 256], in0=reg, scalar1=mvh[:, h, 0:1], scalar2=rstdh[:, h:h + 1],
                                                                       op0=ALU.subtract, op1=ALU.mult), [P(pO), 'mvh', 'rstdh'], ['retn'])
                    for h in range(4):
                        pU = nextps()
                        for ab in range(2):
                            T(lambda e, pU=pU, h=h, ab=ab: e.matmul(PS[pU][:, ab * 256:(ab + 1) * 256], lhsT=Kd[:, h, ab * 128:(ab + 1) * 128], rhs=Vb[:, h * 256:(h + 1) * 256],
                                                                   start=True, stop=True), ['Kd', 'Vb'], [P(pU)])
                        V(lambda e, pU=pU, h=h: e.scalar_tensor_tensor(out=STF[:, h, :], in0=STF[:, h, :], scalar=HC['gam128'][h], in1=PS[pU][:, :], op0=ALU.mult, op1=ALU.add),
                          [P(pU), 'STF'], ['STF'])
                        A(lambda e, h=h: e.copy(out=STB[:, h, :, :], in_=STF[:, h, :].rearrange('p (a f) -> p a f', a=2)), ['STF'], ['STB'])
                    G(lambda e: e.tensor_tensor(out=MR[:], in0=SG[:], in1=SA_[:], op=ALU.mult), ['SG', 'SA'], ['MR'])
                    V(lambda e: e.tensor_tensor(out=MR[:], in0=MR[:], in1=retn[:], op=ALU.mult), ['MR', 'retn'], ['MR'])
                    pcs = []
                    for hf in range(2):
                        pC = nextps(); pcs.append(pC)
                        for cl in range(4):
                            cc = hf * 4 + cl
                            reg = PS[pC][:, cl * 128:(cl + 1) * 128]
                            for j in range(31):
                                T(lambda e, reg=reg, cc=cc, j=j: e.matmul(reg, lhsT=ybuf[:, cc, j:j + 128], rhs=DG[:, cc, j, :], start=(j == 0), stop=False),
                                  ['ybuf', 'DG'], [P(pC)])
                            T(lambda e, reg=reg, cc=cc: e.matmul(reg, lhsT=ones[0:1, :], rhs=dwbr[0:1, cc * 128:(cc + 1) * 128], start=False, stop=True),
                              ['ones', 'dwbr'], [P(pC)])
                        V(lambda e, pC=pC, hf=hf: e.bn_stats(out=st6[:, hf, :], in_=PS[pC][:, :]), [P(pC)], ['st6'])
                    G(lambda e: e.tensor_copy(out=ybuf[:, :, 0:30], in_=ybuf[:, :, 128:158]), ['ybuf'], ['ybuf'])
                    V(lambda e: e.bn_aggr(out=mv[:], in_=st6[:]), ['st6'], ['mv'])
                    A(lambda e: e.activation(out=rstd[:], in_=mv[:, 1:2], func=AF.Sqrt, bias=epsc[:, 0:1], scale=1.0), ['mv', 'epsc'], ['rstd']); V(lambda e: e.reciprocal(out=rstd[:], in_=rstd[:]), ['rstd'], ['rstd'])
                    for hf in range(2):
                        V(lambda e, hf=hf, pcs=pcs: e.tensor_scalar(out=z[:, hf * 512:(hf + 1) * 512], in0=PS[pcs[hf]][:, :], scalar1=mv[:, 0:1], scalar2=rstd[:, 0:1],
                                                           op0=ALU.subtract, op1=ALU.mult), [P(pcs[hf]), 'mv', 'rstd'], ['z'])
                    for hf in range(2):
                        pZ = nextps()
                        for j in range(4):
                            k = hf * 4 + j
                            T(lambda e, pZ=pZ, j=j, k=k: e.transpose(out=PS[pZ][:, j * 128:(j + 1) * 128], in_=z[:, k * 128:(k + 1) * 128], identity=identf[:]),
                              ['z', 'identf'], [P(pZ)])
                        for j in range(4):
                            k = hf * 4 + j
                            V(lambda e, pZ=pZ, j=j, k=k: e.tensor_scalar(out=xn[:, k * 128:(k + 1) * 128], in0=PS[pZ][:, j * 128:(j + 1) * 128], scalar1=clng[:, k:k + 1], scalar2=clnb[:, k:k + 1],
                                                                          op0=ALU.mult, op1=ALU.add), [P(pZ), 'clng', 'clnb'], ['xn'])
                    A(lambda e: e.activation(out=sT[:], in_=xn[:].rearrange('p (k t) -> p k t', k=KD), func=AF.Silu), ['xn'], ['sT'])
                    for hf in range(2):
                        pD = nextps()
                        for cc in range(KD):
                            T(lambda e, pD=pD, cc=cc, hf=hf: e.matmul(PS[pD][:, :], lhsT=sT[:, cc, :], rhs=WCO[:, cc, hf * 512:(hf + 1) * 512], start=(cc == 0), stop=False),
                              ['sT', 'WCO'], [P(pD)])
                        T(lambda e, pD=pD, hf=hf: e.matmul(PS[pD][:, :], lhsT=ones[0:1, :], rhs=bcor[0:1, hf * 512:(hf + 1) * 512], start=False, stop=True),
                          ['ones', 'bcor'], [P(pD)])
                        V(lambda e, pD=pD, hf=hf: e.tensor_tensor(out=tmpw[:, hf * 512:(hf + 1) * 512], in0=PS[pD][:, :], in1=SBg[:, hf * 512:(hf + 1) * 512], op=ALU.mult),
                          [P(pD), 'SBg'], ['SG'])
                    G(lambda e: e.tensor_tensor(out=MG[:], in0=tmpw[:], in1=MR[:], op=ALU.add), ['SG', 'MR'], ['z'])
                    for hf in range(2):
                        pM = nextps()
                        for j in range(4):
                            k = hf * 4 + j
                            T(lambda e, pM=pM, j=j, k=k: e.transpose(out=PS[pM][:, j * 128:(j + 1) * 128], in_=MG[:, k * 128:(k + 1) * 128], identity=identf[:]),
                              ['z', 'identf'], [P(pM)])
                        A(lambda e, pM=pM, hf=hf: e.copy(out=MT[:, hf * 4:(hf + 1) * 4, :], in_=PS[pM][:, :].rearrange('p (c t) -> p c t', c=4)), [P(pM)], ['MT'])
                    for hf in range(2):
                        pX = nextps()
                        for k in range(KD):
                            T(lambda e, pX=pX, k=k, hf=hf: e.matmul(PS[pX][:, :], lhsT=MT[:, k, :], rhs=WO[:, k, hf * 512:(hf + 1) * 512], start=(k == 0), stop=(k == KD - 1)),
                              ['MT', 'WO'], [P(pX)])
                        V(lambda e, pX=pX, hf=hf: e.tensor_tensor(out=tmpw[:, hf * 512:(hf + 1) * 512], in0=PS[pX][:, :], in1=g1B[:, hf * 512:(hf + 1) * 512], op=ALU.mult),
                          [P(pX), 'g1B'], ['SG'])
                    V(lambda e: e.scalar_tensor_tensor(out=x1p[:], in0=xt[:], scalar=ALPHA, in1=tmpw[:], op0=ALU.mult, op1=ALU.add), [XK, 'SG'], ['retn'])
                    for hf in range(2):
                        V(lambda e, hf=hf: e.bn_stats(out=st6[:, hf, :], in_=x1p[:, hf * 512:(hf + 1) * 512]), ['retn'], ['st6'])
                    V(lambda e: e.bn_aggr(out=mv[:], in_=st6[:]), ['st6'], ['mv'])
                    A(lambda e: e.activation(out=rstd[:], in_=mv[:, 1:2], func=AF.Sqrt, bias=epsc[:, 0:1], scale=1.0), ['mv', 'epsc'], ['rstd']); V(lambda e: e.reciprocal(out=rstd[:], in_=rstd[:]), ['rstd'], ['rstd'])
                    V(lambda e: e.tensor_scalar(out=x1p[:], in0=x1p[:], scalar1=mv[:, 0:1], scalar2=rstd[:, 0:1], op0=ALU.subtract, op1=ALU.mult),
                      ['retn', 'mv', 'rstd'], ['retn'])
                    dma(x1_d[g0:g0 + 128, :], x1p[:], reads=['retn'], semkey='x1o')
            R.barrier()
        def tt(eng, out, in0, in1, op, reads, writes):
            return R.op(eng, lambda e: e.tensor_tensor(out=out, in0=in0, in1=in1, op=op), reads, writes)

        def ts(eng, out, in0, s1, s2, op0, op1, reads, writes):
            if op1 is None:
                return R.op(eng, lambda e: e.tensor_scalar(out=out, in0=in0, scalar1=s1, scalar2=None, op0=op0), reads, writes)
            return R.op(eng, lambda e: e.tensor_scalar(out=out, in0=in0, scalar1=s1, scalar2=s2, op0=op0, op1=op1), reads, writes)

        def stt(out, in0, sc, in1, op0, op1, reads, writes):
            return R.op('vector', lambda e: e.scalar_tensor_tensor(out=out, in0=in0, scalar=sc, in1=in1, op0=op0, op1=op1), reads, writes)

        def act(out, in_, func, reads, writes, bias=None):
            if bias is None:
                return R.op('scalar', lambda e: e.activation(out=out, in_=in_, func=func), reads, writes)
            return R.op('scalar', lambda e: e.activation(out=out, in_=in_, func=func, bias=bias, scale=1.0), reads, writes)

        def cp(eng, out, in_, reads, writes):
            if eng == 'scalar':
                return R.op('scalar', lambda e: e.copy(out=out, in_=in_), reads, writes)
            return R.op(eng, lambda e: e.tensor_copy(out=out, in_=in_), reads, writes)

        def mm(out, lhsT, rhs, start, stop, reads, writes):
            return R.op('tensor', lambda e: e.matmul(out, lhsT=lhsT, rhs=rhs, start=start, stop=stop), reads, writes)

        def tr(out, in_, ident, reads, writes):
            return R.op('tensor', lambda e: e.transpose(out=out, in_=in_, identity=ident), reads, writes)

        def ln_stats(src, skey, st6_, mv_, rstd_, pfx):
            for hf in range(2):
                R.op('vector', (lambda hf: lambda e: e.bn_stats(out=st6_[:, hf, :], in_=src[:, hf * 512:(hf + 1) * 512]))(hf), [skey], [pfx + 'st6'])
            R.op('vector', lambda e: e.bn_aggr(out=mv_[:], in_=st6_[:]), [pfx + 'st6'], [pfx + 'mv'])
            act(rstd_[:], mv_[:, 1:2], AF.Sqrt, [pfx + 'mv', 'epsc'], [pfx + 'rstd'], bias=epsc[:, 0:1])
            R.op('vector', lambda e: e.reciprocal(out=rstd_[:], in_=rstd_[:]), [pfx + 'rstd'], [pfx + 'rstd'])

        with ExitStack() as pb:
            def sB(name, shape, dt): return sbt(pb, 'b1_' + name, shape, dt)
            WQ = sB('WQ', [128, KD, 2048], BF16); SKT = sB('SKT', [128, 16, 128], BF16)
            ln1gB = sB('ln1gB', [128, D], F32); ln1bB = sB('ln1bB', [128, D], F32)
            x1t = sB('x1t', [128, D], F32); h2T = sB('h2T', [128, KD, 128], BF16)
            sc_ = sB('sc', [128, 16, 128], F32)
            top_ = sB('top', [128, 16, 16], F32)
            cand = sB('cand', [128, 8, 16, 16], F32); best = sB('best', [128, 8, 16], F32)
            db = sB('db', [128, 8, 16], F32); Zs = sB('Zs', [128, 8], F32); rZ = sB('rZ', [128, 8], F32); nb0 = sB('nb0', [128, 8], F32)
            Lb = [sB('Lb%d' % i, [128, 16, 128], F32) for i in range(2)]; Eb1 = sB('Eb1', [128, 16, 128], F32)
            xn2 = Eb1[:].rearrange('p a b -> p (a b)')[:, 0:D]
            qTs = Lb[1][:].rearrange('p a b -> p (a b)').bitcast(BF16)[:, 0:2048].rearrange('p (c t) -> p c t', c=16)
            Ebs = [cand[:].rearrange('p h a b -> p (h a) b').rearrange('p (x y) b -> p x (y b)', x=16), Eb1[:]]
            Wa = sB('Wa', [128, 8, 16, 128], BF16); Wb = sB('Wb', [128, 8, 16, 128], BF16)
            AT = sB('AT', [128, 128, 64], BF16); BT = sB('BT', [128, 128, 64], BF16)
            Gs = sB('Gs', [128, 128, 64], BF16)
            st6b = sB('st6', [128, 2, 6], F32); mvb = sB('mv', [128, 2], F32); rstdb = sB('rstd', [128, 1], F32)
            stgq = Lb[0][:].rearrange('p a b -> p (a b)')
            for k in range(KD):
                dma(stgq, wq_d[k * 128:(k + 1) * 128, :], writes=[('Lb', 0)], semkey='stgq')
                cp('vector', WQ[:, k, :], stgq, [('Lb', 0)], ['WQ'])
            for c2 in range(16):
                dma(stgq[:, 0:128], skT_d[c2, :, :], writes=[('Lb', 0)], semkey='stgq')
                cp('vector', SKT[:, c2, :], stgq[:, 0:128], [('Lb', 0)], ['SKT'])
            dma(ln1gB[:], ln1g_d.partition_broadcast(128), writes=['ln1gB'], semkey='b1c1')
            dma(ln1bB[:], ln1b_d.partition_broadcast(128), writes=['ln1bB'], semkey='b1c2')
            def b1_front(b, t):
                g0 = b * S + t * 128; tile_i = b * NT + t
                dma(x1t[:], x1_d[g0:g0 + 128, :], writes=['x1t'], semkey='x1t')
                tt('gpsimd', x1t[:], x1t[:], ln1gB[:], ALU.mult, ['x1t', 'ln1gB'], ['x1t'])
                tt('gpsimd', x1t[:], x1t[:], ln1bB[:], ALU.add, ['x1t', 'ln1bB'], ['x1t'])
                dma(x1_d[g0:g0 + 128, :], x1t[:], reads=['x1t'], semkey='x1tw')
                ln_stats(x1t, 'x1t', st6b, mvb, rstdb, 'b1')
                ts('vector', xn2, x1t[:], mvb[:, 0:1], rstdb[:, 0:1], ALU.subtract, ALU.mult, ['x1t', 'b1mv', 'b1rstd'], [('Eb', 1)])
                for hf in range(2):
                    pi_ = nextps()
                    for j in range(4):
                        k = hf * 4 + j
                        tr(PS[pi_][:, j * 128:(j + 1) * 128], xn2[:, k * 128:(k + 1) * 128], identf[:], [('Eb', 1), 'identf'], [P(pi_)])
                    for j in range(4):
                        k = hf * 4 + j
                        ts('vector', h2T[:, k, :], PS[pi_][:, j * 128:(j + 1) * 128], sc2p[:, k, b:b + 1], modT[:, 24 + k, b:b + 1], ALU.mult, ALU.add,
                           [P(pi_), 'sc2p', 'modT'], ['h2T'])
                dma(h2T_d[tile_i], h2T[:], reads=['h2T'], semkey='h2Tw')
                for c0 in range(0, 16, 4):
                    pi_ = nextps()
                    for cl in range(4):
                        c = c0 + cl
                        for k in range(KD):
                            mm(PS[pi_][:, cl * 128:(cl + 1) * 128], WQ[:, k, c * 128:(c + 1) * 128], h2T[:, k, :], k == 0, k == KD - 1, ['WQ', 'h2T'], [P(pi_)])
                    cp('scalar', qTs[:, c0:c0 + 4, :], PS[pi_][:, :].rearrange('p (c t) -> p c t', c=4), [P(pi_)], [('Lb', 1)])
                for c0 in range(0, 16, 4):
                    pi_ = nextps()
                    for cl in range(4):
                        c = c0 + cl
                        mm(PS[pi_][:, cl * 128:(cl + 1) * 128], qTs[:, c, :], SKT[:, c, :], True, True, [('Lb', 1), 'SKT'], [P(pi_)])
                    cp('scalar', sc_[:, c0:c0 + 4, :], PS[pi_][:, :].rearrange('p (c t) -> p c t', c=4), [P(pi_)], ['sc'])
                th = []
                for c in range(16):
                    th.append((lambda c: lambda: R.op('vector', lambda e: e.max(out=top_[:, c, 0:8], in_=sc_[:, c, :]), ['sc'], [('top', c)]))(c))
                for c in range(16):
                    th.append((lambda c: lambda: R.op('vector', lambda e: e.match_replace(out=Lb[0][:, c, :], in_to_replace=top_[:, c, 0:8], in_values=sc_[:, c, :], imm_value=NEG),
                                                     ['sc', ('top', c)], [('wk', c), ('Lb', 0)]))(c))
                for c in range(16):
                    th.append((lambda c: lambda: R.op('vector', lambda e: e.max(out=top_[:, c, 8:16], in_=Lb[0][:, c, :]), [('wk', c)], [('top', c), 'top']))(c))
                topv = top_[:].rearrange('p (h s) i -> p h s i', s=2)
                th.append(lambda: tt('vector', cand[:], topv[:, :, 0, :].unsqueeze(3).to_broadcast([128, 8, 16, 16]),
                                     topv[:, :, 1, :].unsqueeze(2).to_broadcast([128, 8, 16, 16]), ALU.add, ['top'], ['cand', ('Eb', 0)]))
                Lw = Lb[1][:].rearrange('p a b -> p (a b)').rearrange('p (h x) -> p h x', h=8)
                for h in range(8):
                    th.append((lambda h: lambda: R.op('vector', lambda e: e.max(out=best[:, h, 0:8], in_=cand[:, h, :, :].rearrange('p a b -> p (a b)')), ['cand'], [('best', h)]))(h))
                for h in range(8):
                    th.append((lambda h: lambda: R.op('vector', lambda e: e.match_replace(out=Lw[:, h, :], in_to_replace=best[:, h, 0:8],
                                                                                           in_values=cand[:, h, :, :].rearrange('p a b -> p (a b)'), imm_value=NEG),
                                                     ['cand', ('best', h)], [('wk2', h), ('Lb', 1)]))(h))
                for h in range(8):
                    th.append((lambda h: lambda: R.op('vector', lambda e: e.max(out=best[:, h, 8:16], in_=Lw[:, h, :]), [('wk2', h)], [('best', h), 'best']))(h))
                return th

            def b1_s4(b, t):
                topv = top_[:].rearrange('p (h s) i -> p h s i', s=2)
                tt('vector', db[:], best[:], best[:, :, 0:1].to_broadcast([128, 8, 16]), ALU.subtract, ['best'], ['db'])
                act(db[:], db[:], AF.Exp, ['db'], ['db'])
                R.op('vector', lambda e: e.tensor_reduce(out=Zs[:], in_=db[:], axis=mybir.AxisListType.X, op=ALU.add), ['db'], ['Zs'])
                act(rZ[:], Zs[:], AF.Ln, ['Zs'], ['rZ'])
                stt(nb0[:], best[:, :, 0], -1.0, rZ[:], ALU.mult, ALU.subtract, ['best', 'rZ'], ['nb0'])
                sv = sc_[:].rearrange('p (h s) k -> p h s k', s=2)
                tt('vector', Wb[:], sv[:, :, 1, :].unsqueeze(2).to_broadcast([128, 8, 16, 128]),
                   topv[:, :, 1, :].unsqueeze(3).to_broadcast([128, 8, 16, 128]), ALU.is_equal, ['sc', 'top'], ['Wb'])
                def addL(h):
                    q_ = h % 2
                    tt('vector', Lb[q_][:], sc_[:, 2 * h, :].unsqueeze(1).to_broadcast([128, 16, 128]),
                       top_[:, 2 * h + 1, :].unsqueeze(2).to_broadcast([128, 16, 128]), ALU.add, ['sc', 'top'], [('Lb', q_)])
                addL(0)
                for h in range(8):
                    q_ = h % 2
                    if h + 1 < 8:
                        addL(h + 1)
                    act(Ebs[q_], Lb[q_][:], AF.Exp, [('Lb', q_), 'nb0'], [('Eb', q_)] + (['cand'] if q_ == 0 else []), bias=nb0[:, h:h + 1])
                    stt(Wa[:, h, :, :], Lb[q_][:], best[:, h, 15:16], Ebs[q_], ALU.is_ge, ALU.mult, [('Lb', q_), 'best', ('Eb', q_)], ['Wa'])
            def b1_s5(b, t, filler):
                tile_i = b * NT + t

                def fill(n):
                    for _ in range(n):
                        if filler:
                            filler.pop(0)()
                Wav = Wa[:].rearrange('p h i k -> p (h i) k'); Wbv = Wb[:].rearrange('p h i k -> p (h i) k')
                for half in range(2):
                    p0_, p1_ = half * 64, (half + 1) * 64
                    for (Wv, dstT, wk, dk) in ((Wbv, BT, 'Wb', 'BT'), (Wav, AT, 'Wa', 'AT')):
                        for kb in range(8):
                            pi_ = nextps(); psb = PS[pi_][:].bitcast(BF16)
                            for kl in range(16):
                                kk_ = kb * 16 + kl
                                tr(psb[:, kl * 64:(kl + 1) * 64], Wv[p0_:p1_, :, kk_], identb[p0_:p1_, p0_:p1_], [wk, 'identb'], [P(pi_)])
                            if kb % 2 == 0:
                                cp('scalar', dstT[:, kb * 16:(kb + 1) * 16, :], psb.rearrange('p (k t) -> p k t', k=16), [P(pi_)], [dk])
                            else:
                                cp('vector', dstT[:, kb * 16:(kb + 1) * 16, :].rearrange('p k t -> p (k t)').bitcast(I32), PS[pi_][:].bitcast(I32), [P(pi_)], [dk])
                            fill(2)
                    for t0 in range(0, 64, 4):
                        pi_ = nextps()
                        for tl in range(4):
                            tk = t0 + tl
                            mm(PS[pi_][:, tl * 128:(tl + 1) * 128], BT[:, :, tk], AT[:, :, tk], True, True, ['BT', 'AT'], [P(pi_)])
                        cp('scalar', Gs[:, :, t0:t0 + 4], PS[pi_][:, :].rearrange('p (t k) -> p k t', t=4), [P(pi_)], ['Gs'])
                        fill(1)
                    dma(G_d[tile_i * 2 + half], Gs[:], reads=['Gs'], semkey='Gw')
            tiles_ = [(b, t) for b in range(NB) for t in range(NT)]
            th0 = b1_front(*tiles_[0])
            for f_ in th0:
                f_()
            for ii, bt in enumerate(tiles_):
                b1_s4(*bt)
                filler = b1_front(*tiles_[ii + 1]) if ii + 1 < len(tiles_) else []
                b1_s5(*bt, filler)
                while filler:
                    filler.pop(0)()
            R.barrier()

        finals = []
        GT = min(8, NT)
        with ExitStack() as pc:
            def sC(name, shape, dt): return sbt(pc, 'b2_' + name, shape, dt)
            Ub = [sC('Ub%d' % i, [128, KD, 1024], BF16) for i in range(2)]
            Vbk = [sC('Vb%d' % i, [128, 8, 1024], BF16) for i in range(2)]
            Gb = [sC('Gb%d' % i, [128, 2 * GT, 8, 64], BF16) for i in range(2)]
            H2s = [sC('H2_%d' % i, [128, KD, GT * 128], BF16) for i in range(2)]
            acc = [sC('acc%d' % i, [128, D], F32) for i in range(GT)]
            ln2gB = sC('ln2gB', [128, D], F32); ln2bB = sC('ln2bB', [128, D], F32)
            g2Bs = [sC('g2B%d' % i, [128, D], F32) for i in range(2)]
            gl = [sC('gl%d' % i, [128, 128], BF16) for i in range(4)]; PT = [sC('PT%d' % i, [128, 128], BF16) for i in range(4)]
            xr = sC('xr', [128, D], F32); yp = sC('yp', [128, D], F32)
            st6c = sC('st6', [128, 2, 6], F32); mvc = sC('mv', [128, 2], F32); rstdc = sC('rstd', [128, 1], F32)
            dma(ln2gB[:], ln2g_d.partition_broadcast(128), writes=['ln2gB'], semkey='b2c1')
            dma(ln2bB[:], ln2b_d.partition_broadcast(128), writes=['ln2bB'], semkey='b2c2')
            pslo[0] = 4; psc[0] = 0
            bc = [0]; cc_ = [0]; pc_ = [0]
            ylast = [None]

            def final_thunks(tile0, tl, gq):
                g0 = (tile0 + tl) * 128
                g2 = g2Bs[gq]; gk = ('g2B', gq)
                L = []
                L.append(lambda: dma(xr[:], x1_d[g0:g0 + 128, :], writes=['xr'], semkey='xr'))
                L.append(lambda: tt('gpsimd', acc[tl][:], acc[tl][:], g2[:], ALU.mult, [('acc', tl), gk], [('acc', tl)]))
                L.append(lambda: stt(yp[:], xr[:], ALPHA, acc[tl][:], ALU.mult, ALU.add, ['xr', ('acc', tl)], ['yp']))
                L.append(lambda: ln_stats(yp, 'yp', st6c, mvc, rstdc, 'b2'))
                L.append(lambda: ts('vector', yp[:], yp[:], mvc[:, 0:1], rstdc[:, 0:1], ALU.subtract, ALU.mult, ['yp', 'b2mv', 'b2rstd'], ['yp']))
                L.append(lambda: tt('gpsimd', yp[:], yp[:], ln2gB[:], ALU.mult, ['yp', 'ln2gB'], ['yp']))
                L.append(lambda: tt('gpsimd', yp[:], yp[:], ln2bB[:], ALU.add, ['yp', 'ln2bB'], ['yp']))

                def st_():
                    ylast[0] = dma(y_d[g0:g0 + 128, :], yp[:], reads=['yp'], semkey='yout')
                L.append(st_)
                return [(tl, f_) for f_ in L]

            def load_H2(gidx_):
                b_, gi_ = groups[gidx_]
                t0_ = b_ * NT + gi_ * GT
                for tl in range(GT):
                    dma(H2s[gidx_ % 2][:, :, tl * 128:(tl + 1) * 128], h2T_d[t0_ + tl], writes=[('H2', gidx_ % 2, tl)], semkey=('H2', gidx_ % 2, tl))
            groups = [(b, gi) for b in range(NB) for gi in range(NT // GT)]
            pfin = []

            def flush_upto(tl):
                while pfin and pfin[0][0] <= tl:
                    pfin.pop(0)[1]()
            load_H2(0)
            for gidx, (b, gi) in enumerate(groups):
                tile0 = b * NT + gi * GT
                H2 = H2s[gidx % 2]; hb = gidx % 2
                if gi == 0:
                    dma(g2Bs[b % 2][:], mod_d[b, 5120:6144].partition_broadcast(128), writes=[('g2B', b % 2)], semkey=('g2B', b % 2))
                if gidx + 1 < len(groups):
                    load_H2(gidx + 1)
                pend = []

                def vstage(ci, bi, j, pp, tl, blk):
                    for dh in range(2):
                        mm(PS[pp * 2 + dh][:, :], PT[ci][:, :], Vbk[bi][:, j, dh * 512:(dh + 1) * 512], j == 0, j == 7,
                           [('PT', ci), ('Vbk', bi)], [P(pp * 2 + dh)])
                    if j == 7:
                        if blk == 0:
                            flush_upto(tl)
                        for dh in range(2):
                            if blk == 0:
                                cp('vector', acc[tl][:, dh * 512:(dh + 1) * 512], PS[pp * 2 + dh][:, :], [P(pp * 2 + dh)], [('acc', tl)])
                            else:
                                tt('vector', acc[tl][:, dh * 512:(dh + 1) * 512], PS[pp * 2 + dh][:, :], acc[tl][:, dh * 512:(dh + 1) * 512], ALU.add,
                                   [P(pp * 2 + dh), ('acc', tl)], [('acc', tl)])
                for blk in range(16):
                    bi = bc[0] % 2; bc[0] += 1
                    dma(Ub[bi][:], ubf_d[:, :, blk * 1024:(blk + 1) * 1024], writes=[('Ub', bi)], semkey=('Ub', bi))
                    dma(Vbk[bi][:], vbf_d[:, blk * 8:(blk + 1) * 8, :], writes=[('Vbk', bi)], semkey=('Vbk', bi))
                    dma(Gb[bi][:].rearrange('p h k t -> p h (k t)'),
                        G_d[tile0 * 2:(tile0 + GT) * 2, :, blk * 8:(blk + 1) * 8, :].rearrange('h p k t -> p h (k t)'),
                        writes=[('Gb', bi)], semkey=('Gb', bi))
                    for tl in range(GT):
                        pp = pc_[0] % 2; pc_[0] += 1
                        for j in range(8):
                            ci = cc_[0] % 4; cc_[0] += 1
                            pa = nextps()
                            for k in range(KD):
                                mm(PS[pa][:, 0:128], Ub[bi][:, k, j * 128:(j + 1) * 128], H2[:, k, tl * 128:(tl + 1) * 128], k == 0, k == KD - 1,
                                   [('Ub', bi), ('H2', hb, tl)], [P(pa)])
                            act(gl[ci][:], PS[pa][:, 0:128], AF.Gelu, [P(pa)], [('gl', ci)])
                            tt('vector', PT[ci][:].rearrange('p (q t) -> p q t', q=2), gl[ci][:].rearrange('p (q t) -> p q t', q=2),
                               Gb[bi][:, 2 * tl:2 * tl + 2, j, :], ALU.mult, [('gl', ci), ('Gb', bi)], [('PT', ci)])
                            pend.append((ci, bi, j, pp, tl, blk))
                            if len(pend) > 2:
                                vstage(*pend.pop(0))
                            if pfin:
                                pfin.pop(0)[1]()
                while pend:
                    vstage(*pend.pop(0))
                flush_upto(GT)
                for tl in range(GT):
                    pfin.extend(final_thunks(tile0, tl, b % 2))
            flush_upto(GT)
            finals = [ylast[0]]
        R.emit(top, finals)
    return nc


def make_in_maps(inputs, n_cores, NB):
    HC = host_consts()
    f = lambda a: np.ascontiguousarray(np.asarray(a))
    shared = {
        'w_ada': f(inputs['w_ada'][0]), 'b_ada': f(np.asarray(inputs['b_ada'][0]).reshape(48, 128).T), 'b_ada_row': f(np.asarray(inputs['b_ada'][0])[None, :]), 'w_in': f(inputs['w_in'][0]),
        'conv_dwT': f(np.asarray(inputs['conv_dw'][0]).T), 'conv_dw_b': f(np.asarray(inputs['conv_dw_b'][0])[None, :]),
        'conv_ln_g': f(np.asarray(inputs['conv_ln_g'][0]).reshape(KD, 128).T), 'conv_ln_b': f(np.asarray(inputs['conv_ln_b'][0]).reshape(KD, 128).T),
        'w_conv_out': f(inputs['w_conv_out'][0]), 'b_conv_out': f(np.asarray(inputs['b_conv_out'][0])[None, :]),
        'w_out': f(inputs['w_out'][0]), 'ln1_g': f(inputs['ln1_g'][0]), 'ln1_b': f(inputs['ln1_b'][0]),
        'ln2_g': f(inputs['ln2_g'][0]), 'ln2_b': f(inputs['ln2_b'][0]), 'peer_wq': f(inputs['peer_wq'][0]),
        'skT': f(np.asarray(inputs['peer_subkeys'][0]).reshape(16, 128, 128).transpose(0, 2, 1)),
        'peer_uT': f(np.asarray(inputs['peer_u'][0]).T), 'peer_v': f(inputs['peer_v'][0]),
        'ident': HC['ident'], 'iota': HC['iota'], 'invf': HC['invf'], 'maskT': HC['maskT'], 'qdec': HC['qdec'], 'kdec': HC['kdec'],
    }
    x = np.asarray(inputs['x']); c = np.asarray(inputs['c']); pos = np.asarray(inputs['positions'])
    S = x.shape[1]
    maps = []
    for i in range(n_cores):
        m = dict(shared)
        m['x'] = f(x[i * NB:(i + 1) * NB].reshape(NB * S, D))
        m['cT'] = f(c[i * NB:(i + 1) * NB].T)
        m['pos'] = f(pos[i * NB:(i + 1) * NB].astype(np.int32))
        maps.append(m)
    return maps


def kernel(**inputs):
    n_cores = 8; NB = 2
    S = np.asarray(inputs['x']).shape[1]
    nc = build(NB, S)
    maps = make_in_maps(inputs, n_cores, NB)
    res = run_bass_kernel_spmd(nc, maps, core_ids=list(range(n_cores)))
    out = np.concatenate([np.asarray(r['y']).reshape(NB, S, D) for r in res.results], axis=0)
    return out.astype(np.float32)
```

```python
import numpy as np
from contextlib import ExitStack
import concourse.bass as bass
import concourse.mybir as mybir
from concourse.bass_utils import run_bass_kernel_spmd

F32 = mybir.dt.float32; BF16 = mybir.dt.bfloat16; I32 = mybir.dt.int32
ALU = mybir.AluOpType; AF = mybir.ActivationFunctionType
ENGS = ['sync', 'scalar', 'vector', 'gpsimd', 'tensor']
D = 1024; KD = 8; NCOL = 8192
ALPHA = float(2.0 ** 0.25); EPS = 1e-5
NEG = -1e30


class Item:
    __slots__ = ('eng', 'fn', 'deps', 'needed', 'dma', 'semkey', 'count', 'sem')

    def __init__(s, eng, fn, dma, semkey):
        s.eng = eng; s.fn = fn; s.deps = []; s.needed = False; s.dma = dma
        s.semkey = semkey; s.count = 0; s.sem = None


class Rec:
    def __init__(s, nc):
        s.nc = nc; s.items = {e: [] for e in ENGS}; s.lastw = {}; s.readers = {}; s.all = []

    def op(s, eng, fn, reads=(), writes=(), dma=False, semkey=None):
        it = Item(eng, fn, dma, semkey)
        deps = {}
        for k in reads:
            w = s.lastw.get(k)
            if w is not None: deps[id(w)] = w
        for k in writes:
            w = s.lastw.get(k)
            if w is not None: deps[id(w)] = w
            for r in s.readers.get(k, ()): deps[id(r)] = r
        for d in deps.values():
            if d is it: continue
            if d.eng == eng and eng == 'tensor' and not d.dma: continue
            it.deps.append(d); d.needed = True
        for k in reads: s.readers.setdefault(k, []).append(it)
        for k in writes: s.lastw[k] = it; s.readers[k] = []
        s.items[eng].append(it); s.all.append(it)
        return it

    def barrier(s):
        lasts = [s.items[e][-1] for e in ENGS if s.items[e]]
        dmas = {}
        for it in s.all:
            if it.dma: dmas[it.semkey] = it
        for e in ENGS:
            it = Item(e, None, False, None)
            for d in lasts + list(dmas.values()):
                if d.fn is None: continue
                if d.eng == e and not d.dma: continue
                it.deps.append(d); d.needed = True
            s.items[e].append(it); s.all.append(it)
        s.lastw = {}; s.readers = {}

    def emit(s, stack, finals=()):
        nc = s.nc
        esem = {e: stack.enter_context(nc.semaphore('sem_' + e)) for e in ENGS}
        dsem = {}; dcnt = {}; cnt = {e: 0 for e in ENGS}
        for it in s.all:
            if it.dma:
                k = it.semkey
                if k not in dsem:
                    dsem[k] = stack.enter_context(nc.semaphore('dsem%d' % len(dsem))); dcnt[k] = 0
                dcnt[k] += 16; it.sem = dsem[k]; it.count = dcnt[k]; it.needed = True
            elif it.needed and it.fn is not None:
                cnt[it.eng] += 1; it.sem = esem[it.eng]; it.count = cnt[it.eng]
        block = stack.enter_context(nc.Block())

        def body(e):
            def run(eng):
                waited = {}
                for it in s.items[e]:
                    for d in it.deps:
                        if d.sem is None: continue
                        if waited.get(id(d.sem), 0) < d.count:
                            eng.wait_ge(d.sem, d.count); waited[id(d.sem)] = d.count
                    if it.fn is None: continue
                    ins = it.fn(eng)
                    if it.dma: ins.then_inc(it.sem, 16)
                    elif it.needed: ins.then_inc(it.sem, 1)
                if e == 'sync':
                    for d in finals:
                        if waited.get(id(d.sem), 0) < d.count:
                            eng.wait_ge(d.sem, d.count); waited[id(d.sem)] = d.count
            return run
        for e in ENGS:
            getattr(block, e)(body(e))


def host_consts():
    c = {}
    c['ident'] = np.eye(128, dtype=np.float32)
    c['iota'] = np.tile(np.arange(128, dtype=np.float32)[None, :], (128, 1))
    c['invf'] = (np.float32(10000.0) ** (-(np.arange(128, dtype=np.float32)) / np.float32(128))).astype(np.float32)[:, None]
    gam = [1.0 - 2.0 ** (-5.0 - h) for h in range(4)]
    idx = np.arange(128)
    mask = np.zeros((128, 4, 128), np.float32)
    qdec = np.zeros((128, 4, 2, 128), np.float32)
    kdec = np.zeros((128, 4), np.float32)
    for h in range(4):
        g = gam[h]
        m = (g ** np.abs(idx[:, None] - idx[None, :]).astype(np.float64)) * ((idx[:, None] // 64) <= (idx[None, :] // 64))
        mask[:, h, :] = (m / 16.0).astype(np.float32)
        qdec[:, h, :, :] = (g ** (idx + 1.0))[None, None, :]
        kdec[:, h] = (g ** (127.0 - idx)) / 16.0
    c['maskT'] = mask.reshape(128, 512)
    c['qdec'] = qdec.reshape(128, 1024)
    c['kdec'] = kdec
    c['gam128'] = [float(g ** 128) for g in gam]
    return c


def build(NB, S, debug=False):
    NT = S // 128
    NTOK = NB * S
    HC = host_consts()
    nc = bass.Bass('TRN2', target_bir_lowering=False)

    def din(name, shape, dt=F32): return nc.dram_tensor(name, shape, dt, kind='ExternalInput').ap()
    x_d = din('x', [NTOK, D]); cT_d = din('cT', [D, NB]); pos_d = din('pos', [NB, S], I32)
    wada_d = din('w_ada', [D, 6 * D]); bada_d = din('b_ada', [128, 48]); badar_d = din('b_ada_row', [1, 6 * D])
    win_d = din('w_in', [D, NCOL]); dwT_d = din('conv_dwT', [D, 31]); dwb_d = din('conv_dw_b', [1, D])
    clng_d = din('conv_ln_g', [128, KD]); clnb_d = din('conv_ln_b', [128, KD])
    wco_d = din('w_conv_out', [D, D]); bco_d = din('b_conv_out', [1, D]); wo_d = din('w_out', [D, D])
    ln1g_d = din('ln1_g', [D]); ln1b_d = din('ln1_b', [D]); ln2g_d = din('ln2_g', [D]); ln2b_d = din('ln2_b', [D])
    wq_d = din('peer_wq', [D, 2048]); skT_d = din('skT', [16, 128, 128])
    uT_d = din('peer_uT', [D, 16384]); v_d = din('peer_v', [16384, D])
    ident_d = din('ident', [128, 128]); iota_d = din('iota', [128, 128]); invf_d = din('invf', [128, 1])
    maskT_d = din('maskT', [128, 512]); qdec_d = din('qdec', [128, 1024]); kdec_d = din('kdec', [128, 4])
    y_d = nc.dram_tensor('y', [NTOK, D], F32, kind='ExternalOutput').ap()
    winbf_d = nc.dram_tensor('winbf', [128, KD, NCOL], BF16, kind='Internal').ap()
    ubf_d = nc.dram_tensor('ubf', [128, KD, 16384], BF16, kind='Internal').ap()
    vbf_d = nc.dram_tensor('vbf', [128, 128, D], BF16, kind='Internal').ap()
    mod_d = nc.dram_tensor('modd', [NB, 6 * D], F32, kind='Internal').ap()
    NTILE = NTOK // 128
    G_d = nc.dram_tensor('Gd', [NTILE * 2, 128, 128, 64], BF16, kind='Internal').ap()
    h2T_d = nc.dram_tensor('h2Td', [NTILE, 128, KD, 128], BF16, kind='Internal').ap()
    x1_d = nc.dram_tensor('x1d', [NTOK, D], F32, kind='ExternalOutput' if debug else 'Internal').ap()

    with ExitStack() as top:
        R = Rec(nc)
        PS = [top.enter_context(nc.psum_tensor('ps%d' % i, [128, 512], F32)) for i in range(8)]
        psc = [0]; pslo = [0]

        def nextps():
            i = pslo[0] + psc[0] % (8 - pslo[0]); psc[0] += 1
            return i

        def P(i): return ('ps', i)

        def dma(out, in_, reads=(), writes=(), semkey=None, eng='sync', **kw):
            return R.op(eng, lambda e: e.dma_start(out=out, in_=in_, **kw), reads=reads, writes=writes, dma=True, semkey=semkey)

        def V(fn, reads=(), writes=()): return R.op('vector', fn, reads, writes)
        def A(fn, reads=(), writes=()): return R.op('scalar', fn, reads, writes)
        def G(fn, reads=(), writes=()): return R.op('gpsimd', fn, reads, writes)
        def T(fn, reads=(), writes=()): return R.op('tensor', fn, reads, writes)

        def sbt(st, name, shape, dt): return st.enter_context(nc.sbuf_tensor('s_' + name, shape, dt))
        identf = sbt(top, 'identf', [128, 128], F32); identb = sbt(top, 'identb', [128, 128], BF16)
        ones = sbt(top, 'ones', [1, 128], BF16)
        modT = sbt(top, 'modT', [128, 48, NB], F32)
        sc1p = sbt(top, 'sc1p', [128, 8, NB], F32); sc2p = sbt(top, 'sc2p', [128, 8, NB], F32)
        dma(identf[:], ident_d, writes=['identf'], semkey='identf')
        V(lambda e: e.tensor_copy(out=identb[:], in_=identf[:]), ['identf'], ['identb'])
        V(lambda e: e.memset(ones[:], 1.0), [], ['ones'])
        epsc = sbt(top, 'epsc', [128, 1], F32)
        V(lambda e: e.memset(epsc[:], EPS), [], ['epsc'])

        with ExitStack() as p0:
            stgf = [sbt(p0, 'stgf%d' % i, [128, 4096], F32) for i in range(2)]
            stgb = [sbt(p0, 'stgb%d' % i, [128, 4096], BF16) for i in range(2)]
            cnt = [0]

            def conv_block(src_ap, dst_ap, shape3=None):
                i = cnt[0] % 2; cnt[0] += 1
                sf = stgf[i][:]; sbv = stgb[i][:]
                if shape3 is not None:
                    sf = sf.rearrange('p (a b) -> p a b', a=shape3); sbv = sbv.rearrange('p (a b) -> p a b', a=shape3)
                dma(sf, src_ap, writes=[('stgf', i)], semkey=('stgf', i))
                if i == 0:
                    V(lambda e: e.tensor_copy(out=stgb[i][:], in_=stgf[i][:]), [('stgf', i)], [('stgb', i)])
                else:
                    A(lambda e: e.copy(out=stgb[i][:], in_=stgf[i][:]), [('stgf', i)], [('stgb', i)])
                dma(dst_ap, sbv, reads=[('stgb', i)], semkey=('stgbo', i))
            for k in range(KD):
                for cb in range(4):
                    dma(winbf_d[:, k, cb * 2048:(cb + 1) * 2048], win_d[k * 128:(k + 1) * 128, cb * 2048:(cb + 1) * 2048], semkey='wincast', eng='gpsimd')

            cTs = sbt(p0, 'cTs', [128, KD, NB], F32); siluT = sbt(p0, 'siluT', [128, KD, NB], F32)
            badaT = sbt(p0, 'badaT', [128, 48], F32)
            wab = [sbt(p0, 'wab%d' % i, [128, KD, 512], F32) for i in range(2)]
            dma(cTs[:], cT_d.rearrange('(k p) b -> p k b', p=128), writes=['cTs'], semkey='cTs')
            dma(badaT[:], bada_d, writes=['badaT'], semkey='badaT')
            A(lambda e: e.activation(out=siluT[:], in_=cTs[:], func=AF.Silu), ['cTs'], ['siluT'])
            pm = nextps(); pm2 = [nextps(), nextps()]
            onesf = sbt(p0, 'onesf', [1, 8], F32); badar = sbt(p0, 'badar', [1, 6 * D], F32); modrow = sbt(p0, 'modrow', [NB, 6 * D], F32)
            V(lambda e: e.memset(onesf[:], 1.0), [], ['onesf'])
            dma(badar[:], badar_d, writes=['badar'], semkey='badar')
            for blk in range(12):
                wb_ = wab[blk % 2]
                dma(wb_[:], wada_d.rearrange('(k p) c -> p k c', p=128)[:, :, blk * 512:(blk + 1) * 512],
                    writes=[('wab', blk % 2)], semkey=('wab', blk % 2))
                q2 = pm2[blk % 2]
                for k in range(KD):
                    T(lambda e, wb_=wb_, k=k, q2=q2: e.matmul(PS[q2][0:NB, :], lhsT=siluT[:, k, :], rhs=wb_[:, k, :], start=(k == 0), stop=False),
                      [('wab', blk % 2), 'siluT'], [P(q2)])
                T(lambda e, q2=q2, blk=blk: e.matmul(PS[q2][0:NB, :], lhsT=onesf[0:1, 0:NB], rhs=badar[0:1, blk * 512:(blk + 1) * 512], start=False, stop=True),
                  ['onesf', 'badar'], [P(q2)])
                V(lambda e, q2=q2, blk=blk: e.tensor_copy(out=modrow[:, blk * 512:(blk + 1) * 512], in_=PS[q2][0:NB, :]), [P(q2)], ['modrow'])
                for c4 in range(4):
                    kk = blk * 4 + c4
                    for k in range(KD):
                        T(lambda e, wb_=wb_, k=k, c4=c4, kk=kk: e.matmul(
                            PS[pm][:, kk * NB:(kk + 1) * NB], lhsT=wb_[:, k, c4 * 128:(c4 + 1) * 128], rhs=siluT[:, k, :],
                            start=(k == 0), stop=(k == KD - 1)), [('wab', blk % 2), 'siluT'], [P(pm)])
            V(lambda e: e.tensor_tensor(out=modT[:], in0=PS[pm][:, 0:48 * NB].rearrange('p (k b) -> p k b', b=NB),
                                        in1=badaT[:].unsqueeze(2).to_broadcast([128, 48, NB]), op=ALU.add),
              [P(pm), 'badaT'], ['modT'])
            V(lambda e: e.tensor_scalar(out=sc1p[:], in0=modT[:, 8:16, :], scalar1=1.0, scalar2=None, op0=ALU.add), ['modT'], ['sc1p'])
            V(lambda e: e.tensor_scalar(out=sc2p[:], in0=modT[:, 32:40, :], scalar1=1.0, scalar2=None, op0=ALU.add), ['modT'], ['sc2p'])
            dma(mod_d, modrow[:], reads=['modrow'], semkey='modout')
            R.barrier()

        with ExitStack() as pa:
            def sa(name, shape, dt): return sbt(pa, name, shape, dt)
            XT = [sa('xt0', [128, D], F32)] * 2
            WB = [sa('wblk%d' % i, [128, KD, 512], BF16) for i in range(2)]
            DG = sa('dg', [128, KD, 31, 128], BF16)
            dwT = sa('dwT', [128, KD, 31], F32)
            WCO = sa('wco', [128, KD, D], BF16); WO = sa('wo', [128, KD, D], BF16)
            dwbr_f = sa('dwbr_f', [1, D], F32); dwbr = sa('dwbr', [1, D], BF16)
            bcor_f = sa('bcor_f', [1, D], F32); bcor = sa('bcor', [1, D], BF16)
            clng = sa('clng', [128, KD], F32); clnb = sa('clnb', [128, KD], F32)
            g1B = sa('g1B', [128, D], F32)
            invf = sa('invf', [128, 1], F32); maskT = sa('maskT', [128, 512], F32)
            qdecf = sa('qdecf', [128, 1024], F32) if False else None; qdec = sa('qdec', [128, 1024], BF16); kdec = sa('kdec', [128, 4], F32)
            st6 = sa('st6', [128, 2, 6], F32); mv = sa('mv', [128, 2], F32); rstd = sa('rstd', [128, 1], F32)
            st6h = sa('st6h', [128, 4, 6], F32); mvh = sa('mvh', [128, 4, 2], F32); rstdh = sa('rstdh', [128, 4], F32)
            xn = sa('xn', [128, D], F32); h1T = sa('h1T', [128, KD, 128], BF16)
            posi = sa('posi', [128, 128], I32); posf = sa('posf', [128, 128], F32); ang = sa('ang', [128, 128], F32)
            ki = sa('ki', [128, 128], I32); kf = sa('kf', [128, 128], F32); rr = sa('rr', [128, 128], F32)
            rc = sa('rc', [128, 128], F32); tmpa = sa('tmpa', [128, 128], F32)
            sinT = sa('sinT', [128, 128], F32); cosT = sa('cosT', [128, 128], F32)
            t1 = sa('t1', [128, 2, 128], F32); t2 = sa('t2', [128, 2, 128], F32)
            QT = sa('QT', [128, 4, 2, 128], BF16); KT = sa('KT', [128, 4, 2, 128], BF16); QTD = sa('QTD', [128, 4, 2, 128], BF16)
            Vb = sa('Vb', [128, D], BF16); SG = sa('SG', [128, D], F32); SA_ = sa('SA', [128, D], BF16); SBg = sa('SBg', [128, D], BF16)
            UT = sa('UT', [128, 1024], BF16); sgT = sa('sgT', [128, 512], BF16)
            ybuf = sa('ybuf', [128, KD, 158], BF16)
            SmT = sa('SmT', [128, 512], BF16); Kd = sa('Kd', [128, 4, 256], BF16)
            STF = sa('STF', [128, 4, 512], F32); STB = sa('STB', [128, 4, 2, 256], BF16)
            retn = sa('retn', [128, D], F32); MR = sa('MR', [128, D], F32)
            z = sa('z', [128, D], F32); stg = z; sT = sa('sT', [128, KD, 128], BF16)
            MG = z; MT = sa('MT', [128, KD, 128], BF16)
            x1p = retn
            tmpw = SG

            dma(invf[:], invf_d, writes=['invf'], semkey='c1'); dma(maskT[:], maskT_d, writes=['maskT'], semkey='c2')
            dma(z[:], qdec_d, writes=['z'], semkey='c3'); V(lambda e: e.tensor_copy(out=qdec[:], in_=z[:]), ['z'], ['qdec']); dma(kdec[:], kdec_d, writes=['kdec'], semkey='c4')
            dma(clng[:], clng_d, writes=['clng'], semkey='c5')
            dma(clnb[:], clnb_d, writes=['clnb'], semkey='c6')
            dma(dwbr_f[:], dwb_d, writes=['dwbr_f'], semkey='c9'); dma(bcor_f[:], bco_d, writes=['bcor_f'], semkey='c10')
            V(lambda e: e.tensor_copy(out=dwbr[:], in_=dwbr_f[:]), ['dwbr_f'], ['dwbr'])
            V(lambda e: e.tensor_copy(out=bcor[:], in_=bcor_f[:]), ['bcor_f'], ['bcor'])
            dma(dwT[:], dwT_d.rearrange('(k p) j -> p k j', p=128), writes=['dwT'], semkey='c11')
            for cc in range(KD):
                G(lambda e, cc=cc: e.tensor_tensor(out=DG[:, cc, :, :], in0=identb[:].unsqueeze(1).to_broadcast([128, 31, 128]),
                                                   in1=dwT[:, cc, :].unsqueeze(2).to_broadcast([128, 31, 128]), op=ALU.mult),
                  ['identb', 'dwT'], ['DG'])
            for k in range(KD):
                dma(stg[:], wco_d[k * 128:(k + 1) * 128, :], writes=['z'], semkey='stg')
                V(lambda e, k=k: e.tensor_copy(out=WCO[:, k, :], in_=stg[:]), ['z'], ['WCO'])
            for k in range(KD):
                dma(stg[:], wo_d[k * 128:(k + 1) * 128, :], writes=['z'], semkey='stg')
                V(lambda e, k=k: e.tensor_copy(out=WO[:, k, :], in_=stg[:]), ['z'], ['WO'])

            C1 = float(np.float32(6.28125)); C2 = float(np.float32(2 * np.pi - 6.28125)); PI = float(np.pi); TWO_PI = float(2 * np.pi)
            wcnt = [0]; itc = [0]
            for b in range(NB):
                dma(g1B[:], mod_d[b, 2048:3072].partition_broadcast(128), writes=['g1B'], semkey='g1B')
                G(lambda e: e.memset(STF[:], 0.0), [], ['STF']); G(lambda e: e.memset(STB[:], 0.0), [], ['STB'])
                G(lambda e: e.memset(ybuf[:, :, 0:30], 0.0), [], ['ybuf'])
                for t in range(NT):
                    g0 = b * S + t * 128
                    xi = 0; itc[0] += 1
                    xt = XT[xi]; XK = ('xt', xi)
                    dma(xt[:], x_d[g0:g0 + 128, :], writes=[XK], semkey=XK)
                    for hf in range(2):
                        V(lambda e, hf=hf: e.bn_stats(out=st6[:, hf, :], in_=xt[:, hf * 512:(hf + 1) * 512]), [XK], ['st6'])
                    V(lambda e: e.bn_aggr(out=mv[:], in_=st6[:]), ['st6'], ['mv'])
                    A(lambda e: e.activation(out=rstd[:], in_=mv[:, 1:2], func=AF.Sqrt, bias=epsc[:, 0:1], scale=1.0), ['mv', 'epsc'], ['rstd']); V(lambda e: e.reciprocal(out=rstd[:], in_=rstd[:]), ['rstd'], ['rstd'])
                    V(lambda e: e.tensor_scalar(out=xn[:], in0=xt[:], scalar1=mv[:, 0:1], scalar2=rstd[:, 0:1], op0=ALU.subtract, op1=ALU.mult),
                      [XK, 'mv', 'rstd'], ['xn'])
                    for hf in range(2):
                        pi_ = nextps()
                        for j in range(4):
                            k = hf * 4 + j
                            T(lambda e, pi_=pi_, j=j, k=k: e.transpose(out=PS[pi_][:, j * 128:(j + 1) * 128], in_=xn[:, k * 128:(k + 1) * 128], identity=identf[:]),
                              ['xn', 'identf'], [P(pi_)])
                        for j in range(4):
                            k = hf * 4 + j
                            V(lambda e, pi_=pi_, j=j, k=k, b=b: e.tensor_scalar(out=h1T[:, k, :], in0=PS[pi_][:, j * 128:(j + 1) * 128],
                                                                            scalar1=sc1p[:, k, b:b + 1], scalar2=modT[:, k, b:b + 1], op0=ALU.mult, op1=ALU.add),
                              [P(pi_), 'sc1p', 'modT'], ['h1T'])
                    dma(posi[:], pos_d[b, t * 128:(t + 1) * 128].partition_broadcast(128), writes=['posi'], semkey='posi')
                    V(lambda e: e.tensor_copy(out=posf[:], in_=posi[:]), ['posi'], ['posf'])
                    V(lambda e: e.tensor_scalar(out=ang[:], in0=posf[:], scalar1=invf[:, 0:1], scalar2=None, op0=ALU.mult), ['posf', 'invf'], ['ang'])
                    V(lambda e: e.tensor_scalar(out=ki[:], in0=ang[:], scalar1=float(1 / (2 * np.pi)), scalar2=None, op0=ALU.mult), ['ang'], ['ki'])
                    V(lambda e: e.tensor_copy(out=kf[:], in_=ki[:]), ['ki'], ['kf'])
                    V(lambda e: e.scalar_tensor_tensor(out=rr[:], in0=kf[:], scalar=-C1, in1=ang[:], op0=ALU.mult, op1=ALU.add), ['ang', 'kf'], ['rr'])
                    V(lambda e: e.scalar_tensor_tensor(out=rr[:], in0=kf[:], scalar=-C2, in1=rr[:], op0=ALU.mult, op1=ALU.add), ['rr', 'kf'], ['rr'])
                    V(lambda e: e.tensor_scalar(out=tmpa[:], in0=rr[:], scalar1=PI, scalar2=-TWO_PI, op0=ALU.is_gt, op1=ALU.mult), ['rr'], ['tmpa'])
                    V(lambda e: e.tensor_tensor(out=rr[:], in0=rr[:], in1=tmpa[:], op=ALU.add), ['rr', 'tmpa'], ['rr'])
                    V(lambda e: e.tensor_scalar(out=rc[:], in0=rr[:], scalar1=PI / 2, scalar2=None, op0=ALU.add), ['rr'], ['rc'])
                    V(lambda e: e.tensor_scalar(out=tmpa[:], in0=rc[:], scalar1=PI, scalar2=-TWO_PI, op0=ALU.is_gt, op1=ALU.mult), ['rc'], ['tmpa'])
                    V(lambda e: e.tensor_tensor(out=rc[:], in0=rc[:], in1=tmpa[:], op=ALU.add), ['rc', 'tmpa'], ['rc'])
                    V(lambda e: e.tensor_scalar(out=rr[:], in0=rr[:], scalar1=PI, scalar2=-PI, op0=ALU.min, op1=ALU.max), ['rr'], ['rr'])
                    V(lambda e: e.tensor_scalar(out=rc[:], in0=rc[:], scalar1=PI, scalar2=-PI, op0=ALU.min, op1=ALU.max), ['rc'], ['rc'])
                    A(lambda e: e.activation(out=sinT[:], in_=rr[:], func=AF.Sin), ['rr'], ['sinT'])
                    A(lambda e: e.activation(out=cosT[:], in_=rc[:], func=AF.Sin), ['rc'], ['cosT'])
                    for g in range(16):
                        wi = wcnt[0] % 2; wcnt[0] += 1
                        wb_ = WB[wi]; WK = ('wblk', wi)
                        dma(wb_[:], winbf_d[:, :, g * 512:(g + 1) * 512], writes=[WK], semkey=WK)
                        pi_ = nextps(); ps = PS[pi_]
                        if g in (0, 1, 2, 3, 8, 9, 10, 11):
                            for c4 in range(4):
                                for k in range(KD):
                                    T(lambda e, ps=ps, wb_=wb_, c4=c4, k=k: e.matmul(ps[:, c4 * 128:(c4 + 1) * 128], lhsT=wb_[:, k, c4 * 128:(c4 + 1) * 128],
                                                                                  rhs=h1T[:, k, :], start=(k == 0), stop=(k == KD - 1)), [WK, 'h1T'], [P(pi_)])
                        else:
                            for k in range(KD):
                                T(lambda e, ps=ps, wb_=wb_, k=k: e.matmul(ps[:, :], lhsT=h1T[:, k, :], rhs=wb_[:, k, :], start=(k == 0), stop=(k == KD - 1)),
                                  [WK, 'h1T'], [P(pi_)])
                        if g < 4:
                            dst = QT if g < 2 else KT; dk = 'QT' if g < 2 else 'KT'
                            h0 = (g % 2) * 2
                            psv = ps[:].rearrange('p (h a t) -> p h a t', h=2, a=2)
                            Av = psv[:, :, 0, :]; Bv = psv[:, :, 1, :]
                            cb = cosT[:].unsqueeze(1).to_broadcast([128, 2, 128]); sb_ = sinT[:].unsqueeze(1).to_broadcast([128, 2, 128])
                            V(lambda e, Av=Av, cb=cb: e.tensor_tensor(out=t1[:], in0=Av, in1=cb, op=ALU.mult), [P(pi_), 'cosT'], ['t1'])
                            V(lambda e, Bv=Bv, sb_=sb_: e.tensor_tensor(out=t2[:], in0=Bv, in1=sb_, op=ALU.mult), [P(pi_), 'sinT'], ['t2'])
                            V(lambda e, dst=dst, h0=h0: e.tensor_tensor(out=dst[:, h0:h0 + 2, 0, :], in0=t1[:], in1=t2[:], op=ALU.subtract), ['t1', 't2'], [dk])
                            V(lambda e, Av=Av, sb_=sb_: e.tensor_tensor(out=t1[:], in0=Av, in1=sb_, op=ALU.mult), [P(pi_), 'sinT'], ['t1'])
                            V(lambda e, Bv=Bv, cb=cb: e.tensor_tensor(out=t2[:], in0=Bv, in1=cb, op=ALU.mult), [P(pi_), 'cosT'], ['t2'])
                            V(lambda e, dst=dst, h0=h0: e.tensor_tensor(out=dst[:, h0:h0 + 2, 1, :], in0=t1[:], in1=t2[:], op=ALU.add), ['t1', 't2'], [dk])
                            if g < 2:
                                V(lambda e, h0=h0: e.tensor_tensor(out=QTD[:, h0:h0 + 2, :, :], in0=QT[:, h0:h0 + 2, :, :],
                                                                   in1=qdec[:, h0 * 256:(h0 + 2) * 256].rearrange('p (h a t) -> p h a t', h=2, a=2), op=ALU.mult),
                                  ['QT', 'qdec'], ['QTD'])
                        elif g in (4, 5):
                            A(lambda e, ps=ps, g=g: e.copy(out=Vb[:, (g - 4) * 512:(g - 3) * 512], in_=ps[:, :]), [P(pi_)], ['Vb'])
                        elif g in (6, 7):
                            A(lambda e, ps=ps, g=g: e.activation(out=SG[:, (g - 6) * 512:(g - 5) * 512], in_=ps[:, :], func=AF.Silu), [P(pi_)], ['SG'])
                        elif g in (8, 9):
                            A(lambda e, ps=ps, g=g: e.copy(out=UT[:, (g - 8) * 512:(g - 7) * 512], in_=ps[:, :]), [P(pi_)], ['UT'])
                        elif g in (10, 11):
                            hf = g - 10
                            A(lambda e, ps=ps: e.activation(out=sgT[:], in_=ps[:, :], func=AF.Sigmoid), [P(pi_)], ['sgT'])
                            V(lambda e, hf=hf: e.tensor_tensor(out=ybuf[:, hf * 4:(hf + 1) * 4, 30:158], in0=UT[:, hf * 512:(hf + 1) * 512].rearrange('p (c t) -> p c t', c=4),
                                                               in1=sgT[:].rearrange('p (c t) -> p c t', c=4), op=ALU.mult), ['UT', 'sgT'], ['ybuf'])
                        elif g in (12, 13):
                            A(lambda e, ps=ps, g=g: e.activation(out=SA_[:, (g - 12) * 512:(g - 11) * 512], in_=ps[:, :], func=AF.Sigmoid), [P(pi_)], ['SA'])
                        else:
                            A(lambda e, ps=ps, g=g: e.activation(out=SBg[:, (g - 14) * 512:(g - 13) * 512], in_=ps[:, :], func=AF.Sigmoid), [P(pi_)], ['SBg'])
                    pS = nextps()
                    for h in range(4):
                        for ab in range(2):
                            T(lambda e, h=h, ab=ab, pS=pS: e.matmul(PS[pS][:, h * 128:(h + 1) * 128], lhsT=KT[:, h, ab, :], rhs=QT[:, h, ab, :],
                                                                   start=(ab == 0), stop=(ab == 1)), ['KT', 'QT'], [P(pS)])
                    V(lambda e, pS=pS: e.tensor_tensor(out=SmT[:], in0=PS[pS][:, :], in1=maskT[:], op=ALU.mult), [P(pS), 'maskT'], ['SmT'])
                    pK = nextps(); psb = PS[pK][:].bitcast(BF16)
                    for h in range(4):
                        for ab in range(2):
                            c = h * 2 + ab
                            T(lambda e, h=h, ab=ab, c=c, psb=psb: e.transpose(out=psb[:, c * 128:(c + 1) * 128], in_=KT[:, h, ab, :], identity=identb[:]),
                              ['KT', 'identb'], [P(pK)])
                    V(lambda e, psb=psb: e.tensor_tensor(out=Kd[:], in0=psb.rearrange('p (h f) -> p h f', h=4),
                                                         in1=kdec[:].unsqueeze(2).to_broadcast([128, 4, 256]), op=ALU.mult), [P(pK), 'kdec'], ['Kd'])
                    for hp in range(2):
                        pO = nextps()
                        for hl in range(2):
                            h = hp * 2 + hl
                            reg = PS[pO][:, hl * 256:(hl + 1) * 256]
                            T(lambda e, reg=reg, h=h: e.matmul(reg, lhsT=SmT[:, h * 128:(h + 1) * 128], rhs=Vb[:, h * 256:(h + 1) * 256], start=True, stop=False),
                              ['SmT', 'Vb'], [P(pO)])
                            for ab in range(2):
                                T(lambda e, reg=reg, h=h, ab=ab: e.matmul(reg, lhsT=QTD[:, h, ab, :], rhs=STB[:, h, ab, :], start=False, stop=(ab == 1)),
                                  ['QTD', 'STB'], [P(pO)])
                        for hl in range(2):
                            h = hp * 2 + hl
                            reg = PS[pO][:, hl * 256:(hl + 1) * 256]
                            V(lambda e, reg=reg, h=h: e.bn_stats(out=st6h[:, h, :], in_=reg), [P(pO)], ['st6h'])
                            V(lambda e, h=h: e.bn_aggr(out=mvh[:, h, :], in_=st6h[:, h, :]), ['st6h'], ['mvh'])
                            A(lambda e, h=h: e.activation(out=rstdh[:, h:h + 1], in_=mvh[:, h, 1:2], func=AF.Sqrt, bias=epsc[:, 0:1], scale=1.0), ['mvh', 'epsc'], ['rstdh']); V(lambda e, h=h: e.reciprocal(out=rstdh[:, h:h + 1], in_=rstdh[:, h:h + 1]), ['rstdh'], ['rstdh'])
                            V(lambda e, reg=reg, h=h: e.tensor_scalar(out=retn[:, h * 256:(h + 1) * 256], in0=reg, scalar1=mvh[:, h, 0:1], scalar2=rstdh[:, h:h + 1],
                                                                       op0=ALU.subtract, op1=ALU.mult), [P(pO), 'mvh', 'rstdh'], ['retn'])
                    for h in range(4):
                        pU = nextps()
                        for ab in range(2):
                            T(lambda e, pU=pU, h=h, ab=ab: e.matmul(PS[pU][:, ab * 256:(ab + 1) * 256], lhsT=Kd[:, h, ab * 128:(ab + 1) * 128], rhs=Vb[:, h * 256:(h + 1) * 256],
                                                                   start=True, stop=True), ['Kd', 'Vb'], [P(pU)])
                        V(lambda e, pU=pU, h=h: e.scalar_tensor_tensor(out=STF[:, h, :], in0=STF[:, h, :], scalar=HC['gam128'][h], in1=PS[pU][:, :], op0=ALU.mult, op1=ALU.add),
                          [P(pU), 'STF'], ['STF'])
                        A(lambda e, h=h: e.copy(out=STB[:, h, :, :], in_=STF[:, h, :].rearrange('p (a f) -> p a f', a=2)), ['STF'], ['STB'])
                    G(lambda e: e.tensor_tensor(out=MR[:], in0=SG[:], in1=SA_[:], op=ALU.mult), ['SG', 'SA'], ['MR'])
                    V(lambda e: e.tensor_tensor(out=MR[:], in0=MR[:], in1=retn[:], op=ALU.mult), ['MR', 'retn'], ['MR'])
                    pcs = []
                    for hf in range(2):
                        pC = nextps(); pcs.append(pC)
                        for cl in range(4):
                            cc = hf * 4 + cl
                            reg = PS[pC][:, cl * 128:(cl + 1) * 128]
                            for j in range(31):
                                T(lambda e, reg=reg, cc=cc, j=j: e.matmul(reg, lhsT=ybuf[:, cc, j:j + 128], rhs=DG[:, cc, j, :], start=(j == 0), stop=False),
                                  ['ybuf', 'DG'], [P(pC)])
                            T(lambda e, reg=reg, cc=cc: e.matmul(reg, lhsT=ones[0:1, :], rhs=dwbr[0:1, cc * 128:(cc + 1) * 128], start=False, stop=True),
                              ['ones', 'dwbr'], [P(pC)])
                        V(lambda e, pC=pC, hf=hf: e.bn_stats(out=st6[:, hf, :], in_=PS[pC][:, :]), [P(pC)], ['st6'])
                    G(lambda e: e.tensor_copy(out=ybuf[:, :, 0:30], in_=ybuf[:, :, 128:158]), ['ybuf'], ['ybuf'])
                    V(lambda e: e.bn_aggr(out=mv[:], in_=st6[:]), ['st6'], ['mv'])
                    A(lambda e: e.activation(out=rstd[:], in_=mv[:, 1:2], func=AF.Sqrt, bias=epsc[:, 0:1], scale=1.0), ['mv', 'epsc'], ['rstd']); V(lambda e: e.reciprocal(out=rstd[:], in_=rstd[:]), ['rstd'], ['rstd'])
                    for hf in range(2):
                        V(lambda e, hf=hf, pcs=pcs: e.tensor_scalar(out=z[:, hf * 512:(hf + 1) * 512], in0=PS[pcs[hf]][:, :], scalar1=mv[:, 0:1], scalar2=rstd[:, 0:1],
                                                           op0=ALU.subtract, op1=ALU.mult), [P(pcs[hf]), 'mv', 'rstd'], ['z'])
                    for hf in range(2):
                        pZ = nextps()
                        for j in range(4):
                            k = hf * 4 + j
                            T(lambda e, pZ=pZ, j=j, k=k: e.transpose(out=PS[pZ][:, j * 128:(j + 1) * 128], in_=z[:, k * 128:(k + 1) * 128], identity=identf[:]),
                              ['z', 'identf'], [P(pZ)])
                        for j in range(4):
                            k = hf * 4 + j
                            V(lambda e, pZ=pZ, j=j, k=k: e.tensor_scalar(out=xn[:, k * 128:(k + 1) * 128], in0=PS[pZ][:, j * 128:(j + 1) * 128], scalar1=clng[:, k:k + 1], scalar2=clnb[:, k:k + 1],
                                                                          op0=ALU.mult, op1=ALU.add), [P(pZ), 'clng', 'clnb'], ['xn'])
                    A(lambda e: e.activation(out=sT[:], in_=xn[:].rearrange('p (k t) -> p k t', k=KD), func=AF.Silu), ['xn'], ['sT'])
                    for hf in range(2):
                        pD = nextps()
                        for cc in range(KD):
                            T(lambda e, pD=pD, cc=cc, hf=hf: e.matmul(PS[pD][:, :], lhsT=sT[:, cc, :], rhs=WCO[:, cc, hf * 512:(hf + 1) * 512], start=(cc == 0), stop=False),
                              ['sT', 'WCO'], [P(pD)])
                        T(lambda e, pD=pD, hf=hf: e.matmul(PS[pD][:, :], lhsT=ones[0:1, :], rhs=bcor[0:1, hf * 512:(hf + 1) * 512], start=False, stop=True),
                          ['ones', 'bcor'], [P(pD)])
                        V(lambda e, pD=pD, hf=hf: e.tensor_tensor(out=tmpw[:, hf * 512:(hf + 1) * 512], in0=PS[pD][:, :], in1=SBg[:, hf * 512:(hf + 1) * 512], op=ALU.mult),
                          [P(pD), 'SBg'], ['SG'])
                    G(lambda e: e.tensor_tensor(out=MG[:], in0=tmpw[:], in1=MR[:], op=ALU.add), ['SG', 'MR'], ['z'])
                    for hf in range(2):
                        pM = nextps()
                        for j in range(4):
                            k = hf * 4 + j
                            T(lambda e, pM=pM, j=j, k=k: e.transpose(out=PS[pM][:, j * 128:(j + 1) * 128], in_=MG[:, k * 128:(k + 1) * 128], identity=identf[:]),
                              ['z', 'identf'], [P(pM)])
                        A(lambda e, pM=pM, hf=hf: e.copy(out=MT[:, hf * 4:(hf + 1) * 4, :], in_=PS[pM][:, :].rearrange('p (c t) -> p c t', c=4)), [P(pM)], ['MT'])
                    for hf in range(2):
                        pX = nextps()
                        for k in range(KD):
                            T(lambda e, pX=pX, k=k, hf=hf: e.matmul(PS[pX][:, :], lhsT=MT[:, k, :], rhs=WO[:, k, hf * 512:(hf + 1) * 512], start=(k == 0), stop=(k == KD - 1)),
                              ['MT', 'WO'], [P(pX)])
                        V(lambda e, pX=pX, hf=hf: e.tensor_tensor(out=tmpw[:, hf * 512:(hf + 1) * 512], in0=PS[pX][:, :], in1=g1B[:, hf * 512:(hf + 1) * 512], op=ALU.mult),
                          [P(pX), 'g1B'], ['SG'])
                    V(lambda e: e.scalar_tensor_tensor(out=x1p[:], in0=xt[:], scalar=ALPHA, in1=tmpw[:], op0=ALU.mult, op1=ALU.add), [XK, 'SG'], ['retn'])
                    for hf in range(2):
                        V(lambda e, hf=hf: e.bn_stats(out=st6[:, hf, :], in_=x1p[:, hf * 512:(hf + 1) * 512]), ['retn'], ['st6'])
                    V(lambda e: e.bn_aggr(out=mv[:], in_=st6[:]), ['st6'], ['mv'])
                    A(lambda e: e.activation(out=rstd[:], in_=mv[:, 1:2], func=AF.Sqrt, bias=epsc[:, 0:1], scale=1.0), ['mv', 'epsc'], ['rstd']); V(lambda e: e.reciprocal(out=rstd[:], in_=rstd[:]), ['rstd'], ['rstd'])
                    V(lambda e: e.tensor_scalar(out=x1p[:], in0=x1p[:], scalar1=mv[:, 0:1], scalar2=rstd[:, 0:1], op0=ALU.subtract, op1=ALU.mult),
                      ['retn', 'mv', 'rstd'], ['retn'])
                    dma(x1_d[g0:g0 + 128, :], x1p[:], reads=['retn'], semkey='x1o')
            R.barrier()
        def tt(eng, out, in0, in1, op, reads, writes):
            return R.op(eng, lambda e: e.tensor_tensor(out=out, in0=in0, in1=in1, op=op), reads, writes)

        def ts(eng, out, in0, s1, s2, op0, op1, reads, writes):
            if op1 is None:
                return R.op(eng, lambda e: e.tensor_scalar(out=out, in0=in0, scalar1=s1, scalar2=None, op0=op0), reads, writes)
            return R.op(eng, lambda e: e.tensor_scalar(out=out, in0=in0, scalar1=s1, scalar2=s2, op0=op0, op1=op1), reads, writes)

        def stt(out, in0, sc, in1, op0, op1, reads, writes):
            return R.op('vector', lambda e: e.scalar_tensor_tensor(out=out, in0=in0, scalar=sc, in1=in1, op0=op0, op1=op1), reads, writes)

        def act(out, in_, func, reads, writes, bias=None):
            if bias is None:
                return R.op('scalar', lambda e: e.activation(out=out, in_=in_, func=func), reads, writes)
            return R.op('scalar', lambda e: e.activation(out=out, in_=in_, func=func, bias=bias, scale=1.0), reads, writes)

        def cp(eng, out, in_, reads, writes):
            if eng == 'scalar':
                return R.op('scalar', lambda e: e.copy(out=out, in_=in_), reads, writes)
            return R.op(eng, lambda e: e.tensor_copy(out=out, in_=in_), reads, writes)

        def mm(out, lhsT, rhs, start, stop, reads, writes):
            return R.op('tensor', lambda e: e.matmul(out, lhsT=lhsT, rhs=rhs, start=start, stop=stop), reads, writes)

        def tr(out, in_, ident, reads, writes):
            return R.op('tensor', lambda e: e.transpose(out=out, in_=in_, identity=ident), reads, writes)

        def ln_stats(src, skey, st6_, mv_, rstd_, pfx):
            for hf in range(2):
                R.op('vector', (lambda hf: lambda e: e.bn_stats(out=st6_[:, hf, :], in_=src[:, hf * 512:(hf + 1) * 512]))(hf), [skey], [pfx + 'st6'])
            R.op('vector', lambda e: e.bn_aggr(out=mv_[:], in_=st6_[:]), [pfx + 'st6'], [pfx + 'mv'])
            act(rstd_[:], mv_[:, 1:2], AF.Sqrt, [pfx + 'mv', 'epsc'], [pfx + 'rstd'], bias=epsc[:, 0:1])
            R.op('vector', lambda e: e.reciprocal(out=rstd_[:], in_=rstd_[:]), [pfx + 'rstd'], [pfx + 'rstd'])

        with ExitStack() as pb:
            def sB(name, shape, dt): return sbt(pb, 'b1_' + name, shape, dt)
            WQ = sB('WQ', [128, KD, 2048], BF16); SKT = sB('SKT', [128, 16, 128], BF16)
            ln1gB = sB('ln1gB', [128, D], F32); ln1bB = sB('ln1bB', [128, D], F32)
            x1t = sB('x1t', [128, D], F32); h2T = sB('h2T', [128, KD, 128], BF16)
            sc_ = sB('sc', [128, 16, 128], F32)
            top_ = sB('top', [128, 16, 16], F32)
            cand = sB('cand', [128, 8, 16, 16], F32); best = sB('best', [128, 8, 16], F32)
            db = sB('db', [128, 8, 16], F32); Zs = sB('Zs', [128, 8], F32); rZ = sB('rZ', [128, 8], F32); nb0 = sB('nb0', [128, 8], F32)
            Lb = [sB('Lb%d' % i, [128, 16, 128], F32) for i in range(2)]; Eb1 = sB('Eb1', [128, 16, 128], F32)
            xn2 = Eb1[:].rearrange('p a b -> p (a b)')[:, 0:D]
            qTs = Lb[1][:].rearrange('p a b -> p (a b)').bitcast(BF16)[:, 0:2048].rearrange('p (c t) -> p c t', c=16)
            Ebs = [cand[:].rearrange('p h a b -> p (h a) b').rearrange('p (x y) b -> p x (y b)', x=16), Eb1[:]]
            Wa = sB('Wa', [128, 8, 16, 128], BF16); Wb = sB('Wb', [128, 8, 16, 128], BF16)
            AT = sB('AT', [128, 128, 64], BF16); BT = sB('BT', [128, 128, 64], BF16)
            Gs = sB('Gs', [128, 128, 64], BF16)
            st6b = sB('st6', [128, 2, 6], F32); mvb = sB('mv', [128, 2], F32); rstdb = sB('rstd', [128, 1], F32)
            stgq = Lb[0][:].rearrange('p a b -> p (a b)')
            for k in range(KD):
                dma(stgq, wq_d[k * 128:(k + 1) * 128, :], writes=[('Lb', 0)], semkey='stgq')
                cp('vector', WQ[:, k, :], stgq, [('Lb', 0)], ['WQ'])
            for c2 in range(16):
                dma(stgq[:, 0:128], skT_d[c2, :, :], writes=[('Lb', 0)], semkey='stgq')
                cp('vector', SKT[:, c2, :], stgq[:, 0:128], [('Lb', 0)], ['SKT'])
            dma(ln1gB[:], ln1g_d.partition_broadcast(128), writes=['ln1gB'], semkey='b1c1')
            dma(ln1bB[:], ln1b_d.partition_broadcast(128), writes=['ln1bB'], semkey='b1c2')
            def b1_front(b, t):
                g0 = b * S + t * 128; tile_i = b * NT + t
                dma(x1t[:], x1_d[g0:g0 + 128, :], writes=['x1t'], semkey='x1t')
                tt('gpsimd', x1t[:], x1t[:], ln1gB[:], ALU.mult, ['x1t', 'ln1gB'], ['x1t'])
                tt('gpsimd', x1t[:], x1t[:], ln1bB[:], ALU.add, ['x1t', 'ln1bB'], ['x1t'])
                dma(x1_d[g0:g0 + 128, :], x1t[:], reads=['x1t'], semkey='x1tw')
                ln_stats(x1t, 'x1t', st6b, mvb, rstdb, 'b1')
                ts('vector', xn2, x1t[:], mvb[:, 0:1], rstdb[:, 0:1], ALU.subtract, ALU.mult, ['x1t', 'b1mv', 'b1rstd'], [('Eb', 1)])
                for hf in range(2):
                    pi_ = nextps()
                    for j in range(4):
                        k = hf * 4 + j
                        tr(PS[pi_][:, j * 128:(j + 1) * 128], xn2[:, k * 128:(k + 1) * 128], identf[:], [('Eb', 1), 'identf'], [P(pi_)])
                    for j in range(4):
                        k = hf * 4 + j
                        ts('vector', h2T[:, k, :], PS[pi_][:, j * 128:(j + 1) * 128], sc2p[:, k, b:b + 1], modT[:, 24 + k, b:b + 1], ALU.mult, ALU.add,
                           [P(pi_), 'sc2p', 'modT'], ['h2T'])
                dma(h2T_d[tile_i], h2T[:], reads=['h2T'], semkey='h2Tw')
                for c0 in range(0, 16, 4):
                    pi_ = nextps()
                    for cl in range(4):
                        c = c0 + cl
                        for k in range(KD):
                            mm(PS[pi_][:, cl * 128:(cl + 1) * 128], WQ[:, k, c * 128:(c + 1) * 128], h2T[:, k, :], k == 0, k == KD - 1, ['WQ', 'h2T'], [P(pi_)])
                    cp('scalar', qTs[:, c0:c0 + 4, :], PS[pi_][:, :].rearrange('p (c t) -> p c t', c=4), [P(pi_)], [('Lb', 1)])
                for c0 in range(0, 16, 4):
                    pi_ = nextps()
                    for cl in range(4):
                        c = c0 + cl
                        mm(PS[pi_][:, cl * 128:(cl + 1) * 128], qTs[:, c, :], SKT[:, c, :], True, True, [('Lb', 1), 'SKT'], [P(pi_)])
                    cp('scalar', sc_[:, c0:c0 + 4, :], PS[pi_][:, :].rearrange('p (c t) -> p c t', c=4), [P(pi_)], ['sc'])
                th = []
                for c in range(16):
                    th.append((lambda c: lambda: R.op('vector', lambda e: e.max(out=top_[:, c, 0:8], in_=sc_[:, c, :]), ['sc'], [('top', c)]))(c))
                for c in range(16):
                    th.append((lambda c: lambda: R.op('vector', lambda e: e.match_replace(out=Lb[0][:, c, :], in_to_replace=top_[:, c, 0:8], in_values=sc_[:, c, :], imm_value=NEG),
                                                     ['sc', ('top', c)], [('wk', c), ('Lb', 0)]))(c))
                for c in range(16):
                    th.append((lambda c: lambda: R.op('vector', lambda e: e.max(out=top_[:, c, 8:16], in_=Lb[0][:, c, :]), [('wk', c)], [('top', c), 'top']))(c))
                topv = top_[:].rearrange('p (h s) i -> p h s i', s=2)
                th.append(lambda: tt('vector', cand[:], topv[:, :, 0, :].unsqueeze(3).to_broadcast([128, 8, 16, 16]),
                                     topv[:, :, 1, :].unsqueeze(2).to_broadcast([128, 8, 16, 16]), ALU.add, ['top'], ['cand', ('Eb', 0)]))
                Lw = Lb[1][:].rearrange('p a b -> p (a b)').rearrange('p (h x) -> p h x', h=8)
                for h in range(8):
                    th.append((lambda h: lambda: R.op('vector', lambda e: e.max(out=best[:, h, 0:8], in_=cand[:, h, :, :].rearrange('p a b -> p (a b)')), ['cand'], [('best', h)]))(h))
                for h in range(8):
                    th.append((lambda h: lambda: R.op('vector', lambda e: e.match_replace(out=Lw[:, h, :], in_to_replace=best[:, h, 0:8],
                                                                                           in_values=cand[:, h, :, :].rearrange('p a b -> p (a b)'), imm_value=NEG),
                                                     ['cand', ('best', h)], [('wk2', h), ('Lb', 1)]))(h))
                for h in range(8):
                    th.append((lambda h: lambda: R.op('vector', lambda e: e.max(out=best[:, h, 8:16], in_=Lw[:, h, :]), [('wk2', h)], [('best', h), 'best']))(h))
                return th

            def b1_s4(b, t):
                topv = top_[:].rearrange('p (h s) i -> p h s i', s=2)
                tt('vector', db[:], best[:], best[:, :, 0:1].to_broadcast([128, 8, 16]), ALU.subtract, ['best'], ['db'])
                act(db[:], db[:], AF.Exp, ['db'], ['db'])
                R.op('vector', lambda e: e.tensor_reduce(out=Zs[:], in_=db[:], axis=mybir.AxisListType.X, op=ALU.add), ['db'], ['Zs'])
                act(rZ[:], Zs[:], AF.Ln, ['Zs'], ['rZ'])
                stt(nb0[:], best[:, :, 0], -1.0, rZ[:], ALU.mult, ALU.subtract, ['best', 'rZ'], ['nb0'])
                sv = sc_[:].rearrange('p (h s) k -> p h s k', s=2)
                tt('vector', Wb[:], sv[:, :, 1, :].unsqueeze(2).to_broadcast([128, 8, 16, 128]),
                   topv[:, :, 1, :].unsqueeze(3).to_broadcast([128, 8, 16, 128]), ALU.is_equal, ['sc', 'top'], ['Wb'])
                def addL(h):
                    q_ = h % 2
                    tt('vector', Lb[q_][:], sc_[:, 2 * h, :].unsqueeze(1).to_broadcast([128, 16, 128]),
                       top_[:, 2 * h + 1, :].unsqueeze(2).to_broadcast([128, 16, 128]), ALU.add, ['sc', 'top'], [('Lb', q_)])
                addL(0)
                for h in range(8):
                    q_ = h % 2
                    if h + 1 < 8:
                        addL(h + 1)
                    act(Ebs[q_], Lb[q_][:], AF.Exp, [('Lb', q_), 'nb0'], [('Eb', q_)] + (['cand'] if q_ == 0 else []), bias=nb0[:, h:h + 1])
                    stt(Wa[:, h, :, :], Lb[q_][:], best[:, h, 15:16], Ebs[q_], ALU.is_ge, ALU.mult, [('Lb', q_), 'best', ('Eb', q_)], ['Wa'])
            def b1_s5(b, t, filler):
                tile_i = b * NT + t

                def fill(n):
                    for _ in range(n):
                        if filler:
                            filler.pop(0)()
                Wav = Wa[:].rearrange('p h i k -> p (h i) k'); Wbv = Wb[:].rearrange('p h i k -> p (h i) k')
                for half in range(2):
                    p0_, p1_ = half * 64, (half + 1) * 64
                    for (Wv, dstT, wk, dk) in ((Wbv, BT, 'Wb', 'BT'), (Wav, AT, 'Wa', 'AT')):
                        for kb in range(8):
                            pi_ = nextps(); psb = PS[pi_][:].bitcast(BF16)
                            for kl in range(16):
                                kk_ = kb * 16 + kl
                                tr(psb[:, kl * 64:(kl + 1) * 64], Wv[p0_:p1_, :, kk_], identb[p0_:p1_, p0_:p1_], [wk, 'identb'], [P(pi_)])
                            if kb % 2 == 0:
                                cp('scalar', dstT[:, kb * 16:(kb + 1) * 16, :], psb.rearrange('p (k t) -> p k t', k=16), [P(pi_)], [dk])
                            else:
                                cp('vector', dstT[:, kb * 16:(kb + 1) * 16, :].rearrange('p k t -> p (k t)').bitcast(I32), PS[pi_][:].bitcast(I32), [P(pi_)], [dk])
                            fill(2)
                    for t0 in range(0, 64, 4):
                        pi_ = nextps()
                        for tl in range(4):
                            tk = t0 + tl
                            mm(PS[pi_][:, tl * 128:(tl + 1) * 128], BT[:, :, tk], AT[:, :, tk], True, True, ['BT', 'AT'], [P(pi_)])
                        cp('scalar', Gs[:, :, t0:t0 + 4], PS[pi_][:, :].rearrange('p (t k) -> p k t', t=4), [P(pi_)], ['Gs'])
                        fill(1)
                    dma(G_d[tile_i * 2 + half], Gs[:], reads=['Gs'], semkey='Gw')
            vv2 = v_d.rearrange('(j p) d -> p j d', p=128)
            cvl = []
            for k in range(KD):
                for e8 in range(8):
                    cvl.append((uT_d[k * 128:(k + 1) * 128, e8 * 2048:(e8 + 1) * 2048], ubf_d[:, k, e8 * 2048:(e8 + 1) * 2048]))
            for j1 in range(128):
                cvl.append((vv2[:, j1, :], vbf_d[:, j1, :]))
            cvi = [0]

            def cv_issue(n):
                for _ in range(n):
                    if cvi[0] < len(cvl):
                        src, dst = cvl[cvi[0]]; cvi[0] += 1
                        dma(dst, src, semkey='cvcast', eng='gpsimd')
            tiles_ = [(b, t) for b in range(NB) for t in range(NT)]
            th0 = b1_front(*tiles_[0])
            for f_ in th0:
                f_()
            for ii, bt in enumerate(tiles_):
                b1_s4(*bt)
                filler = b1_front(*tiles_[ii + 1]) if ii + 1 < len(tiles_) else []
                b1_s5(*bt, filler)
                cv_issue(3)
                while filler:
                    filler.pop(0)()
            cv_issue(len(cvl))
            R.barrier()

        finals = []
        GT = min(8, NT)
        with ExitStack() as pc:
            def sC(name, shape, dt): return sbt(pc, 'b2_' + name, shape, dt)
            Ub = [sC('Ub%d' % i, [128, KD, 1024], BF16) for i in range(2)]
            Vbk = [sC('Vb%d' % i, [128, 8, 1024], BF16) for i in range(2)]
            Gb = [sC('Gb%d' % i, [128, 2 * GT, 8, 64], BF16) for i in range(2)]
            H2s = [sC('H2_%d' % i, [128, KD, GT * 128], BF16) for i in range(2)]
            acc = [sC('acc%d' % i, [128, D], F32) for i in range(GT)]
            ln2gB = sC('ln2gB', [128, D], F32); ln2bB = sC('ln2bB', [128, D], F32)
            g2Bs = [sC('g2B%d' % i, [128, D], F32) for i in range(2)]
            gl = [sC('gl%d' % i, [128, 128], BF16) for i in range(4)]; PT = [sC('PT%d' % i, [128, 128], BF16) for i in range(4)]
            xr = sC('xr', [128, D], F32); yp = sC('yp', [128, D], F32)
            st6c = sC('st6', [128, 2, 6], F32); mvc = sC('mv', [128, 2], F32); rstdc = sC('rstd', [128, 1], F32)
            dma(ln2gB[:], ln2g_d.partition_broadcast(128), writes=['ln2gB'], semkey='b2c1')
            dma(ln2bB[:], ln2b_d.partition_broadcast(128), writes=['ln2bB'], semkey='b2c2')
            pslo[0] = 4; psc[0] = 0
            bc = [0]; cc_ = [0]; pc_ = [0]
            ylast = [None]

            def final_thunks(tile0, tl, gq):
                g0 = (tile0 + tl) * 128
                g2 = g2Bs[gq]; gk = ('g2B', gq)
                L = []
                L.append(lambda: dma(xr[:], x1_d[g0:g0 + 128, :], writes=['xr'], semkey='xr'))
                L.append(lambda: tt('gpsimd', acc[tl][:], acc[tl][:], g2[:], ALU.mult, [('acc', tl), gk], [('acc', tl)]))
                L.append(lambda: stt(yp[:], xr[:], ALPHA, acc[tl][:], ALU.mult, ALU.add, ['xr', ('acc', tl)], ['yp']))
                L.append(lambda: ln_stats(yp, 'yp', st6c, mvc, rstdc, 'b2'))
                L.append(lambda: ts('vector', yp[:], yp[:], mvc[:, 0:1], rstdc[:, 0:1], ALU.subtract, ALU.mult, ['yp', 'b2mv', 'b2rstd'], ['yp']))
                L.append(lambda: tt('gpsimd', yp[:], yp[:], ln2gB[:], ALU.mult, ['yp', 'ln2gB'], ['yp']))
                L.append(lambda: tt('gpsimd', yp[:], yp[:], ln2bB[:], ALU.add, ['yp', 'ln2bB'], ['yp']))

                def st_():
                    ylast[0] = dma(y_d[g0:g0 + 128, :], yp[:], reads=['yp'], semkey='yout')
                L.append(st_)
                return [(tl, f_) for f_ in L]

            def load_H2(gidx_):
                b_, gi_ = groups[gidx_]
                t0_ = b_ * NT + gi_ * GT
                for tl in range(GT):
                    dma(H2s[gidx_ % 2][:, :, tl * 128:(tl + 1) * 128], h2T_d[t0_ + tl], writes=[('H2', gidx_ % 2, tl)], semkey=('H2', gidx_ % 2, tl))
            groups = [(b, gi) for b in range(NB) for gi in range(NT // GT)]
            pfin = []

            def flush_upto(tl):
                while pfin and pfin[0][0] <= tl:
                    pfin.pop(0)[1]()
            load_H2(0)
            for gidx, (b, gi) in enumerate(groups):
                tile0 = b * NT + gi * GT
                H2 = H2s[gidx % 2]; hb = gidx % 2
                if gi == 0:
                    dma(g2Bs[b % 2][:], mod_d[b, 5120:6144].partition_broadcast(128), writes=[('g2B', b % 2)], semkey=('g2B', b % 2))
                if gidx + 1 < len(groups):
                    load_H2(gidx + 1)
                pend = []

                def vstage(ci, bi, j, pp, tl, blk):
                    for dh in range(2):
                        mm(PS[pp * 2 + dh][:, :], PT[ci][:, :], Vbk[bi][:, j, dh * 512:(dh + 1) * 512], j == 0, j == 7,
                           [('PT', ci), ('Vbk', bi)], [P(pp * 2 + dh)])
                    if j == 7:
                        if blk == 0:
                            flush_upto(tl)
                        for dh in range(2):
                            if blk == 0:
                                cp('vector', acc[tl][:, dh * 512:(dh + 1) * 512], PS[pp * 2 + dh][:, :], [P(pp * 2 + dh)], [('acc', tl)])
                            else:
                                tt('vector', acc[tl][:, dh * 512:(dh + 1) * 512], PS[pp * 2 + dh][:, :], acc[tl][:, dh * 512:(dh + 1) * 512], ALU.add,
                                   [P(pp * 2 + dh), ('acc', tl)], [('acc', tl)])
                for blk in range(16):
                    bi = bc[0] % 2; bc[0] += 1
                    dma(Ub[bi][:], ubf_d[:, :, blk * 1024:(blk + 1) * 1024], writes=[('Ub', bi)], semkey=('Ub', bi))
                    dma(Vbk[bi][:], vbf_d[:, blk * 8:(blk + 1) * 8, :], writes=[('Vbk', bi)], semkey=('Vbk', bi))
                    dma(Gb[bi][:].rearrange('p h k t -> p h (k t)'),
                        G_d[tile0 * 2:(tile0 + GT) * 2, :, blk * 8:(blk + 1) * 8, :].rearrange('h p k t -> p h (k t)'),
                        writes=[('Gb', bi)], semkey=('Gb', bi))
                    for tl in range(GT):
                        pp = pc_[0] % 2; pc_[0] += 1
                        for j in range(8):
                            ci = cc_[0] % 4; cc_[0] += 1
                            pa = nextps()
                            for k in range(KD):
                                mm(PS[pa][:, 0:128], Ub[bi][:, k, j * 128:(j + 1) * 128], H2[:, k, tl * 128:(tl + 1) * 128], k == 0, k == KD - 1,
                                   [('Ub', bi), ('H2', hb, tl)], [P(pa)])
                            act(gl[ci][:], PS[pa][:, 0:128], AF.Gelu, [P(pa)], [('gl', ci)])
                            tt('vector', PT[ci][:].rearrange('p (q t) -> p q t', q=2), gl[ci][:].rearrange('p (q t) -> p q t', q=2),
                               Gb[bi][:, 2 * tl:2 * tl + 2, j, :], ALU.mult, [('gl', ci), ('Gb', bi)], [('PT', ci)])
                            pend.append((ci, bi, j, pp, tl, blk))
                            if len(pend) > 2:
                                vstage(*pend.pop(0))
                            if pfin:
                                pfin.pop(0)[1]()
                while pend:
                    vstage(*pend.pop(0))
                flush_upto(GT)
                for tl in range(GT):
                    pfin.extend(final_thunks(tile0, tl, b % 2))
            flush_upto(GT)
            finals = [ylast[0]]
        R.emit(top, finals)
    return nc


def make_in_maps(inputs, n_cores, NB):
    HC = host_consts()
    f = lambda a: np.ascontiguousarray(np.asarray(a))
    shared = {
        'w_ada': f(inputs['w_ada'][0]), 'b_ada': f(np.asarray(inputs['b_ada'][0]).reshape(48, 128).T), 'b_ada_row': f(np.asarray(inputs['b_ada'][0])[None, :]), 'w_in': f(inputs['w_in'][0]),
        'conv_dwT': f(np.asarray(inputs['conv_dw'][0]).T), 'conv_dw_b': f(np.asarray(inputs['conv_dw_b'][0])[None, :]),
        'conv_ln_g': f(np.asarray(inputs['conv_ln_g'][0]).reshape(KD, 128).T), 'conv_ln_b': f(np.asarray(inputs['conv_ln_b'][0]).reshape(KD, 128).T),
        'w_conv_out': f(inputs['w_conv_out'][0]), 'b_conv_out': f(np.asarray(inputs['b_conv_out'][0])[None, :]),
        'w_out': f(inputs['w_out'][0]), 'ln1_g': f(inputs['ln1_g'][0]), 'ln1_b': f(inputs['ln1_b'][0]),
        'ln2_g': f(inputs['ln2_g'][0]), 'ln2_b': f(inputs['ln2_b'][0]), 'peer_wq': f(inputs['peer_wq'][0]),
        'skT': f(np.asarray(inputs['peer_subkeys'][0]).reshape(16, 128, 128).transpose(0, 2, 1)),
        'peer_uT': f(np.asarray(inputs['peer_u'][0]).T), 'peer_v': f(inputs['peer_v'][0]),
        'ident': HC['ident'], 'iota': HC['iota'], 'invf': HC['invf'], 'maskT': HC['maskT'], 'qdec': HC['qdec'], 'kdec': HC['kdec'],
    }
    x = np.asarray(inputs['x']); c = np.asarray(inputs['c']); pos = np.asarray(inputs['positions'])
    S = x.shape[1]
    maps = []
    for i in range(n_cores):
        m = dict(shared)
        m['x'] = f(x[i * NB:(i + 1) * NB].reshape(NB * S, D))
        m['cT'] = f(c[i * NB:(i + 1) * NB].T)
        m['pos'] = f(pos[i * NB:(i + 1) * NB].astype(np.int32))
        maps.append(m)
    return maps


def kernel(**inputs):
    n_cores = 8; NB = 2
    S = np.asarray(inputs['x']).shape[1]
    nc = build(NB, S)
    maps = make_in_maps(inputs, n_cores, NB)
    res = run_bass_kernel_spmd(nc, maps, core_ids=list(range(n_cores)))
    out = np.concatenate([np.asarray(r['y']).reshape(NB, S, D) for r in res.results], axis=0)
    return out.astype(np.float32)
```

```python
import numpy as np
from contextlib import ExitStack
import concourse.bass as bass
import concourse.mybir as mybir
from concourse.bass_utils import run_bass_kernel_spmd

F32 = mybir.dt.float32; BF16 = mybir.dt.bfloat16; I32 = mybir.dt.int32
ALU = mybir.AluOpType; AF = mybir.ActivationFunctionType
ENGS = ['sync', 'scalar', 'vector', 'gpsimd', 'tensor']
D = 1024; KD = 8; NCOL = 8192
ALPHA = float(2.0 ** 0.25); EPS = 1e-5
NEG = -1e30


class Item:
    __slots__ = ('eng', 'fn', 'deps', 'needed', 'dma', 'semkey', 'count', 'sem')

    def __init__(s, eng, fn, dma, semkey):
        s.eng = eng; s.fn = fn; s.deps = []; s.needed = False; s.dma = dma
        s.semkey = semkey; s.count = 0; s.sem = None


class Rec:
    def __init__(s, nc):
        s.nc = nc; s.items = {e: [] for e in ENGS}; s.lastw = {}; s.readers = {}; s.all = []

    def op(s, eng, fn, reads=(), writes=(), dma=False, semkey=None):
        it = Item(eng, fn, dma, semkey)
        deps = {}
        for k in reads:
            w = s.lastw.get(k)
            if w is not None: deps[id(w)] = w
        for k in writes:
            w = s.lastw.get(k)
            if w is not None: deps[id(w)] = w
            for r in s.readers.get(k, ()): deps[id(r)] = r
        for d in deps.values():
            if d is it: continue
            if d.eng == eng and eng == 'tensor' and not d.dma: continue
            it.deps.append(d); d.needed = True
        for k in reads: s.readers.setdefault(k, []).append(it)
        for k in writes: s.lastw[k] = it; s.readers[k] = []
        s.items[eng].append(it); s.all.append(it)
        return it

    def barrier(s):
        lasts = [s.items[e][-1] for e in ENGS if s.items[e]]
        dmas = {}
        for it in s.all:
            if it.dma: dmas[it.semkey] = it
        for e in ENGS:
            it = Item(e, None, False, None)
            for d in lasts + list(dmas.values()):
                if d.fn is None: continue
                if d.eng == e and not d.dma: continue
                it.deps.append(d); d.needed = True
            s.items[e].append(it); s.all.append(it)
        s.lastw = {}; s.readers = {}

    def emit(s, stack, finals=()):
        nc = s.nc
        esem = {e: stack.enter_context(nc.semaphore('sem_' + e)) for e in ENGS}
        dsem = {}; dcnt = {}; cnt = {e: 0 for e in ENGS}
        for it in s.all:
            if it.dma:
                k = it.semkey
                if k not in dsem:
                    dsem[k] = stack.enter_context(nc.semaphore('dsem%d' % len(dsem))); dcnt[k] = 0
                dcnt[k] += 16; it.sem = dsem[k]; it.count = dcnt[k]; it.needed = True
            elif it.needed and it.fn is not None:
                cnt[it.eng] += 1; it.sem = esem[it.eng]; it.count = cnt[it.eng]
        block = stack.enter_context(nc.Block())

        def body(e):
            def run(eng):
                waited = {}
                for it in s.items[e]:
                    for d in it.deps:
                        if d.sem is None: continue
                        if waited.get(id(d.sem), 0) < d.count:
                            eng.wait_ge(d.sem, d.count); waited[id(d.sem)] = d.count
                    if it.fn is None: continue
                    ins = it.fn(eng)
                    if it.dma: ins.then_inc(it.sem, 16)
                    elif it.needed: ins.then_inc(it.sem, 1)
                if e == 'sync':
                    for d in finals:
                        if waited.get(id(d.sem), 0) < d.count:
                            eng.wait_ge(d.sem, d.count); waited[id(d.sem)] = d.count
            return run
        for e in ENGS:
            getattr(block, e)(body(e))


def host_consts():
    c = {}
    c['ident'] = np.eye(128, dtype=np.float32)
    c['iota'] = np.tile(np.arange(128, dtype=np.float32)[None, :], (128, 1))
    c['invf'] = (np.float32(10000.0) ** (-(np.arange(128, dtype=np.float32)) / np.float32(128))).astype(np.float32)[:, None]
    gam = [1.0 - 2.0 ** (-5.0 - h) for h in range(4)]
    idx = np.arange(128)
    mask = np.zeros((128, 4, 128), np.float32)
    qdec = np.zeros((128, 4, 2, 128), np.float32)
    kdec = np.zeros((128, 4), np.float32)
    for h in range(4):
        g = gam[h]
        m = (g ** np.abs(idx[:, None] - idx[None, :]).astype(np.float64)) * ((idx[:, None] // 64) <= (idx[None, :] // 64))
        mask[:, h, :] = (m / 16.0).astype(np.float32)
        qdec[:, h, :, :] = (g ** (idx + 1.0))[None, None, :]
        kdec[:, h] = (g ** (127.0 - idx)) / 16.0
    c['maskT'] = mask.reshape(128, 512)
    c['qdec'] = qdec.reshape(128, 1024)
    c['kdec'] = kdec
    c['gam128'] = [float(g ** 128) for g in gam]
    return c


def build(NB, S, debug=False):
    NT = S // 128
    NTOK = NB * S
    HC = host_consts()
    nc = bass.Bass('TRN2', target_bir_lowering=False)

    def din(name, shape, dt=F32): return nc.dram_tensor(name, shape, dt, kind='ExternalInput').ap()
    x_d = din('x', [NTOK, D]); cT_d = din('cT', [D, NB]); pos_d = din('pos', [NB, S], I32)
    wada_d = din('w_ada', [D, 6 * D]); bada_d = din('b_ada', [128, 48]); badar_d = din('b_ada_row', [1, 6 * D])
    win_d = din('w_in', [D, NCOL]); dwT_d = din('conv_dwT', [D, 31]); dwb_d = din('conv_dw_b', [1, D])
    clng_d = din('conv_ln_g', [128, KD]); clnb_d = din('conv_ln_b', [128, KD])
    wco_d = din('w_conv_out', [D, D]); bco_d = din('b_conv_out', [1, D]); wo_d = din('w_out', [D, D])
    ln1g_d = din('ln1_g', [D]); ln1b_d = din('ln1_b', [D]); ln2g_d = din('ln2_g', [D]); ln2b_d = din('ln2_b', [D])
    wq_d = din('peer_wq', [D, 2048]); skT_d = din('skT', [16, 128, 128])
    uT_d = din('peer_uT', [D, 16384]); v_d = din('peer_v', [16384, D])
    ident_d = din('ident', [128, 128]); iota_d = din('iota', [128, 128]); invf_d = din('invf', [128, 1])
    maskT_d = din('maskT', [128, 512]); qdec_d = din('qdec', [128, 1024]); kdec_d = din('kdec', [128, 4])
    y_d = nc.dram_tensor('y', [NTOK, D], F32, kind='ExternalOutput').ap()
    winbf_d = nc.dram_tensor('winbf', [128, KD, NCOL], BF16, kind='Internal').ap()
    ubf_d = nc.dram_tensor('ubf', [128, KD, 16384], BF16, kind='Internal').ap()
    vbf_d = nc.dram_tensor('vbf', [128, 128, D], BF16, kind='Internal').ap()
    mod_d = nc.dram_tensor('modd', [NB, 6 * D], F32, kind='Internal').ap()
    NTILE = NTOK // 128
    G_d = nc.dram_tensor('Gd', [NTILE * 2, 128, 128, 64], BF16, kind='Internal').ap()
    h2T_d = nc.dram_tensor('h2Td', [NTILE, 128, KD, 128], BF16, kind='Internal').ap()
    x1_d = nc.dram_tensor('x1d', [NTOK, D], F32, kind='ExternalOutput' if debug else 'Internal').ap()

    with ExitStack() as top:
        R = Rec(nc)
        PS = [top.enter_context(nc.psum_tensor('ps%d' % i, [128, 512], F32)) for i in range(8)]
        psc = [0]; pslo = [0]

        def nextps():
            i = pslo[0] + psc[0] % (8 - pslo[0]); psc[0] += 1
            return i

        def P(i): return ('ps', i)

        def dma(out, in_, reads=(), writes=(), semkey=None, eng='sync', **kw):
            return R.op(eng, lambda e: e.dma_start(out=out, in_=in_, **kw), reads=reads, writes=writes, dma=True, semkey=semkey)

        def V(fn, reads=(), writes=()): return R.op('vector', fn, reads, writes)
        def A(fn, reads=(), writes=()): return R.op('scalar', fn, reads, writes)
        def G(fn, reads=(), writes=()): return R.op('gpsimd', fn, reads, writes)
        def T(fn, reads=(), writes=()): return R.op('tensor', fn, reads, writes)

        def sbt(st, name, shape, dt): return st.enter_context(nc.sbuf_tensor('s_' + name, shape, dt))
        identf = sbt(top, 'identf', [128, 128], F32); identb = sbt(top, 'identb', [128, 128], BF16)
        ones = sbt(top, 'ones', [1, 128], BF16)
        modT = sbt(top, 'modT', [128, 48, NB], F32)
        sc1p = sbt(top, 'sc1p', [128, 8, NB], F32); sc2p = sbt(top, 'sc2p', [128, 8, NB], F32)
        dma(identf[:], ident_d, writes=['identf'], semkey='identf')
        V(lambda e: e.tensor_copy(out=identb[:], in_=identf[:]), ['identf'], ['identb'])
        V(lambda e: e.memset(ones[:], 1.0), [], ['ones'])
        epsc = sbt(top, 'epsc', [128, 1], F32)
        V(lambda e: e.memset(epsc[:], EPS), [], ['epsc'])

        with ExitStack() as p0:
            stgf = [sbt(p0, 'stgf%d' % i, [128, 4096], F32) for i in range(2)]
            stgb = [sbt(p0, 'stgb%d' % i, [128, 4096], BF16) for i in range(2)]
            cnt = [0]

            def conv_block(src_ap, dst_ap, shape3=None):
                i = cnt[0] % 2; cnt[0] += 1
                sf = stgf[i][:]; sbv = stgb[i][:]
                if shape3 is not None:
                    sf = sf.rearrange('p (a b) -> p a b', a=shape3); sbv = sbv.rearrange('p (a b) -> p a b', a=shape3)
                dma(sf, src_ap, writes=[('stgf', i)], semkey=('stgf', i))
                if i == 0:
                    V(lambda e: e.tensor_copy(out=stgb[i][:], in_=stgf[i][:]), [('stgf', i)], [('stgb', i)])
                else:
                    A(lambda e: e.copy(out=stgb[i][:], in_=stgf[i][:]), [('stgf', i)], [('stgb', i)])
                dma(dst_ap, sbv, reads=[('stgb', i)], semkey=('stgbo', i))
            for k in range(KD):
                for cb in range(4):
                    dma(winbf_d[:, k, cb * 2048:(cb + 1) * 2048], win_d[k * 128:(k + 1) * 128, cb * 2048:(cb + 1) * 2048], semkey='wincast', eng='gpsimd')

            cTs = sbt(p0, 'cTs', [128, KD, NB], F32); siluT = sbt(p0, 'siluT', [128, KD, NB], F32)
            badaT = sbt(p0, 'badaT', [128, 48], F32)
            wab = [sbt(p0, 'wab%d' % i, [128, KD, 512], F32) for i in range(2)]
            dma(cTs[:], cT_d.rearrange('(k p) b -> p k b', p=128), writes=['cTs'], semkey='cTs')
            dma(badaT[:], bada_d, writes=['badaT'], semkey='badaT')
            A(lambda e: e.activation(out=siluT[:], in_=cTs[:], func=AF.Silu), ['cTs'], ['siluT'])
            pm = nextps(); pm2 = [nextps(), nextps()]
            onesf = sbt(p0, 'onesf', [1, 8], F32); badar = sbt(p0, 'badar', [1, 6 * D], F32); modrow = sbt(p0, 'modrow', [NB, 6 * D], F32)
            V(lambda e: e.memset(onesf[:], 1.0), [], ['onesf'])
            dma(badar[:], badar_d, writes=['badar'], semkey='badar')
            for blk in range(12):
                wb_ = wab[blk % 2]
                dma(wb_[:], wada_d.rearrange('(k p) c -> p k c', p=128)[:, :, blk * 512:(blk + 1) * 512],
                    writes=[('wab', blk % 2)], semkey=('wab', blk % 2))
                q2 = pm2[blk % 2]
                for k in range(KD):
                    T(lambda e, wb_=wb_, k=k, q2=q2: e.matmul(PS[q2][0:NB, :], lhsT=siluT[:, k, :], rhs=wb_[:, k, :], start=(k == 0), stop=False),
                      [('wab', blk % 2), 'siluT'], [P(q2)])
                T(lambda e, q2=q2, blk=blk: e.matmul(PS[q2][0:NB, :], lhsT=onesf[0:1, 0:NB], rhs=badar[0:1, blk * 512:(blk + 1) * 512], start=False, stop=True),
                  ['onesf', 'badar'], [P(q2)])
                V(lambda e, q2=q2, blk=blk: e.tensor_copy(out=modrow[:, blk * 512:(blk + 1) * 512], in_=PS[q2][0:NB, :]), [P(q2)], ['modrow'])
                for c4 in range(4):
                    kk = blk * 4 + c4
                    for k in range(KD):
                        T(lambda e, wb_=wb_, k=k, c4=c4, kk=kk: e.matmul(
                            PS[pm][:, kk * NB:(kk + 1) * NB], lhsT=wb_[:, k, c4 * 128:(c4 + 1) * 128], rhs=siluT[:, k, :],
                            start=(k == 0), stop=(k == KD - 1)), [('wab', blk % 2), 'siluT'], [P(pm)])
            V(lambda e: e.tensor_tensor(out=modT[:], in0=PS[pm][:, 0:48 * NB].rearrange('p (k b) -> p k b', b=NB),
                                        in1=badaT[:].unsqueeze(2).to_broadcast([128, 48, NB]), op=ALU.add),
              [P(pm), 'badaT'], ['modT'])
            V(lambda e: e.tensor_scalar(out=sc1p[:], in0=modT[:, 8:16, :], scalar1=1.0, scalar2=None, op0=ALU.add), ['modT'], ['sc1p'])
            V(lambda e: e.tensor_scalar(out=sc2p[:], in0=modT[:, 32:40, :], scalar1=1.0, scalar2=None, op0=ALU.add), ['modT'], ['sc2p'])
            dma(mod_d, modrow[:], reads=['modrow'], semkey='modout')
            R.barrier()

        with ExitStack() as pa:
            def sa(name, shape, dt): return sbt(pa, name, shape, dt)
            XT = [sa('xt0', [128, D], F32)] * 2
            WB = [sa('wblk%d' % i, [128, KD, 512], BF16) for i in range(2)]
            DG = sa('dg', [128, KD, 31, 128], BF16)
            dwT = sa('dwT', [128, KD, 31], F32)
            WCO = sa('wco', [128, KD, D], BF16); WO = sa('wo', [128, KD, D], BF16)
            dwbr_f = sa('dwbr_f', [1, D], F32); dwbr = sa('dwbr', [1, D], BF16)
            bcor_f = sa('bcor_f', [1, D], F32); bcor = sa('bcor', [1, D], BF16)
            clng = sa('clng', [128, KD], F32); clnb = sa('clnb', [128, KD], F32)
            g1B = sa('g1B', [128, D], F32)
            invf = sa('invf', [128, 1], F32); maskT = sa('maskT', [128, 512], F32)
            qdecf = sa('qdecf', [128, 1024], F32) if False else None; qdec = sa('qdec', [128, 1024], BF16); kdec = sa('kdec', [128, 4], F32)
            st6 = sa('st6', [128, 2, 6], F32); mv = sa('mv', [128, 2], F32); rstd = sa('rstd', [128, 1], F32)
            st6h = sa('st6h', [128, 4, 6], F32); mvh = sa('mvh', [128, 4, 2], F32); rstdh = sa('rstdh', [128, 4], F32)
            xn = sa('xn', [128, D], F32); h1T = sa('h1T', [128, KD, 128], BF16)
            posi = sa('posi', [128, 128], I32); posf = sa('posf', [128, 128], F32); ang = sa('ang', [128, 128], F32)
            ki = sa('ki', [128, 128], I32); kf = sa('kf', [128, 128], F32); rr = sa('rr', [128, 128], F32)
            rc = sa('rc', [128, 128], F32); tmpa = sa('tmpa', [128, 128], F32)
            sinT = sa('sinT', [128, 128], F32); cosT = sa('cosT', [128, 128], F32)
            t1 = sa('t1', [128, 2, 128], F32); t2 = sa('t2', [128, 2, 128], F32)
            QT = sa('QT', [128, 4, 2, 128], BF16); KT = sa('KT', [128, 4, 2, 128], BF16); QTD = sa('QTD', [128, 4, 2, 128], BF16)
            Vb = sa('Vb', [128, D], BF16); SG = sa('SG', [128, D], F32); SA_ = sa('SA', [128, D], BF16); SBg = sa('SBg', [128, D], BF16)
            UT = sa('UT', [128, 1024], BF16); sgT = sa('sgT', [128, 512], BF16)
            ybuf = sa('ybuf', [128, KD, 158], BF16)
            SmT = sa('SmT', [128, 512], BF16); Kd = sa('Kd', [128, 4, 256], BF16)
            STF = sa('STF', [128, 4, 512], F32); STB = sa('STB', [128, 4, 2, 256], BF16)
            retn = sa('retn', [128, D], F32); MR = sa('MR', [128, D], F32)
            z = sa('z', [128, D], F32); stg = z; sT = sa('sT', [128, KD, 128], BF16)
            MG = z; MT = sa('MT', [128, KD, 128], BF16)
            x1p = retn
            tmpw = SG

            dma(invf[:], invf_d, writes=['invf'], semkey='c1'); dma(maskT[:], maskT_d, writes=['maskT'], semkey='c2')
            dma(z[:], qdec_d, writes=['z'], semkey='c3'); V(lambda e: e.tensor_copy(out=qdec[:], in_=z[:]), ['z'], ['qdec']); dma(kdec[:], kdec_d, writes=['kdec'], semkey='c4')
            dma(clng[:], clng_d, writes=['clng'], semkey='c5')
            dma(clnb[:], clnb_d, writes=['clnb'], semkey='c6')
            dma(dwbr_f[:], dwb_d, writes=['dwbr_f'], semkey='c9'); dma(bcor_f[:], bco_d, writes=['bcor_f'], semkey='c10')
            V(lambda e: e.tensor_copy(out=dwbr[:], in_=dwbr_f[:]), ['dwbr_f'], ['dwbr'])
            V(lambda e: e.tensor_copy(out=bcor[:], in_=bcor_f[:]), ['bcor_f'], ['bcor'])
            dma(dwT[:], dwT_d.rearrange('(k p) j -> p k j', p=128), writes=['dwT'], semkey='c11')
            for cc in range(KD):
                G(lambda e, cc=cc: e.tensor_tensor(out=DG[:, cc, :, :], in0=identb[:].unsqueeze(1).to_broadcast([128, 31, 128]),
                                                   in1=dwT[:, cc, :].unsqueeze(2).to_broadcast([128, 31, 128]), op=ALU.mult),
                  ['identb', 'dwT'], ['DG'])
            for k in range(KD):
                dma(stg[:], wco_d[k * 128:(k + 1) * 128, :], writes=['z'], semkey='stg')
                V(lambda e, k=k: e.tensor_copy(out=WCO[:, k, :], in_=stg[:]), ['z'], ['WCO'])
            for k in range(KD):
                dma(stg[:], wo_d[k * 128:(k + 1) * 128, :], writes=['z'], semkey='stg')
                V(lambda e, k=k: e.tensor_copy(out=WO[:, k, :], in_=stg[:]), ['z'], ['WO'])

            C1 = float(np.float32(6.28125)); C2 = float(np.float32(2 * np.pi - 6.28125)); PI = float(np.pi); TWO_PI = float(2 * np.pi)
            wcnt = [0]; itc = [0]
            for b in range(NB):
                dma(g1B[:], mod_d[b, 2048:3072].partition_broadcast(128), writes=['g1B'], semkey='g1B')
                G(lambda e: e.memset(STF[:], 0.0), [], ['STF']); G(lambda e: e.memset(STB[:], 0.0), [], ['STB'])
                G(lambda e: e.memset(ybuf[:, :, 0:30], 0.0), [], ['ybuf'])
                for t in range(NT):
                    g0 = b * S + t * 128
                    xi = 0; itc[0] += 1
                    xt = XT[xi]; XK = ('xt', xi)
                    dma(xt[:], x_d[g0:g0 + 128, :], writes=[XK], semkey=XK)
                    for hf in range(2):
                        V(lambda e, hf=hf: e.bn_stats(out=st6[:, hf, :], in_=xt[:, hf * 512:(hf + 1) * 512]), [XK], ['st6'])
                    V(lambda e: e.bn_aggr(out=mv[:], in_=st6[:]), ['st6'], ['mv'])
                    A(lambda e: e.activation(out=rstd[:], in_=mv[:, 1:2], func=AF.Sqrt, bias=epsc[:, 0:1], scale=1.0), ['mv', 'epsc'], ['rstd']); V(lambda e: e.reciprocal(out=rstd[:], in_=rstd[:]), ['rstd'], ['rstd'])
                    V(lambda e: e.tensor_scalar(out=xn[:], in0=xt[:], scalar1=mv[:, 0:1], scalar2=rstd[:, 0:1], op0=ALU.subtract, op1=ALU.mult),
                      [XK, 'mv', 'rstd'], ['xn'])
                    for hf in range(2):
                        pi_ = nextps()
                        for j in range(4):
                            k = hf * 4 + j
                            T(lambda e, pi_=pi_, j=j, k=k: e.transpose(out=PS[pi_][:, j * 128:(j + 1) * 128], in_=xn[:, k * 128:(k + 1) * 128], identity=identf[:]),
                              ['xn', 'identf'], [P(pi_)])
                        for j in range(4):
                            k = hf * 4 + j
                            V(lambda e, pi_=pi_, j=j, k=k, b=b: e.tensor_scalar(out=h1T[:, k, :], in0=PS[pi_][:, j * 128:(j + 1) * 128],
                                                                            scalar1=sc1p[:, k, b:b + 1], scalar2=modT[:, k, b:b + 1], op0=ALU.mult, op1=ALU.add),
                              [P(pi_), 'sc1p', 'modT'], ['h1T'])
                    dma(posi[:], pos_d[b, t * 128:(t + 1) * 128].partition_broadcast(128), writes=['posi'], semkey='posi')
                    V(lambda e: e.tensor_copy(out=posf[:], in_=posi[:]), ['posi'], ['posf'])
                    V(lambda e: e.tensor_scalar(out=ang[:], in0=posf[:], scalar1=invf[:, 0:1], scalar2=None, op0=ALU.mult), ['posf', 'invf'], ['ang'])
                    V(lambda e: e.tensor_scalar(out=ki[:], in0=ang[:], scalar1=float(1 / (2 * np.pi)), scalar2=None, op0=ALU.mult), ['ang'], ['ki'])
                    V(lambda e: e.tensor_copy(out=kf[:], in_=ki[:]), ['ki'], ['kf'])
                    V(lambda e: e.scalar_tensor_tensor(out=rr[:], in0=kf[:], scalar=-C1, in1=ang[:], op0=ALU.mult, op1=ALU.add), ['ang', 'kf'], ['rr'])
                    V(lambda e: e.scalar_tensor_tensor(out=rr[:], in0=kf[:], scalar=-C2, in1=rr[:], op0=ALU.mult, op1=ALU.add), ['rr', 'kf'], ['rr'])
                    V(lambda e: e.tensor_scalar(out=tmpa[:], in0=rr[:], scalar1=PI, scalar2=-TWO_PI, op0=ALU.is_gt, op1=ALU.mult), ['rr'], ['tmpa'])
                    V(lambda e: e.tensor_tensor(out=rr[:], in0=rr[:], in1=tmpa[:], op=ALU.add), ['rr', 'tmpa'], ['rr'])
                    V(lambda e: e.tensor_scalar(out=rc[:], in0=rr[:], scalar1=PI / 2, scalar2=None, op0=ALU.add), ['rr'], ['rc'])
                    V(lambda e: e.tensor_scalar(out=tmpa[:], in0=rc[:], scalar1=PI, scalar2=-TWO_PI, op0=ALU.is_gt, op1=ALU.mult), ['rc'], ['tmpa'])
                    V(lambda e: e.tensor_tensor(out=rc[:], in0=rc[:], in1=tmpa[:], op=ALU.add), ['rc', 'tmpa'], ['rc'])
                    V(lambda e: e.tensor_scalar(out=rr[:], in0=rr[:], scalar1=PI, scalar2=-PI, op0=ALU.min, op1=ALU.max), ['rr'], ['rr'])
                    V(lambda e: e.tensor_scalar(out=rc[:], in0=rc[:], scalar1=PI, scalar2=-PI, op0=ALU.min, op1=ALU.max), ['rc'], ['rc'])
                    A(lambda e: e.activation(out=sinT[:], in_=rr[:], func=AF.Sin), ['rr'], ['sinT'])
                    A(lambda e: e.activation(out=cosT[:], in_=rc[:], func=AF.Sin), ['rc'], ['cosT'])
                    for g in range(16):
                        wi = wcnt[0] % 2; wcnt[0] += 1
                        wb_ = WB[wi]; WK = ('wblk', wi)
                        dma(wb_[:], winbf_d[:, :, g * 512:(g + 1) * 512], writes=[WK], semkey=WK)
                        pi_ = nextps(); ps = PS[pi_]
                        if g in (0, 1, 2, 3, 8, 9, 10, 11):
                            for c4 in range(4):
                                for k in range(KD):
                                    T(lambda e, ps=ps, wb_=wb_, c4=c4, k=k: e.matmul(ps[:, c4 * 128:(c4 + 1) * 128], lhsT=wb_[:, k, c4 * 128:(c4 + 1) * 128],
                                                                                  rhs=h1T[:, k, :], start=(k == 0), stop=(k == KD - 1)), [WK, 'h1T'], [P(pi_)])
                        else:
                            for k in range(KD):
                                T(lambda e, ps=ps, wb_=wb_, k=k: e.matmul(ps[:, :], lhsT=h1T[:, k, :], rhs=wb_[:, k, :], start=(k == 0), stop=(k == KD - 1)),
                                  [WK, 'h1T'], [P(pi_)])
                        if g < 4:
                            dst = QT if g < 2 else KT; dk = 'QT' if g < 2 else 'KT'
                            h0 = (g % 2) * 2
                            psv = ps[:].rearrange('p (h a t) -> p h a t', h=2, a=2)
                            Av = psv[:, :, 0, :]; Bv = psv[:, :, 1, :]
                            cb = cosT[:].unsqueeze(1).to_broadcast([128, 2, 128]); sb_ = sinT[:].unsqueeze(1).to_broadcast([128, 2, 128])
                            V(lambda e, Av=Av, cb=cb: e.tensor_tensor(out=t1[:], in0=Av, in1=cb, op=ALU.mult), [P(pi_), 'cosT'], ['t1'])
                            V(lambda e, Bv=Bv, sb_=sb_: e.tensor_tensor(out=t2[:], in0=Bv, in1=sb_, op=ALU.mult), [P(pi_), 'sinT'], ['t2'])
                            V(lambda e, dst=dst, h0=h0: e.tensor_tensor(out=dst[:, h0:h0 + 2, 0, :], in0=t1[:], in1=t2[:], op=ALU.subtract), ['t1', 't2'], [dk])
                            V(lambda e, Av=Av, sb_=sb_: e.tensor_tensor(out=t1[:], in0=Av, in1=sb_, op=ALU.mult), [P(pi_), 'sinT'], ['t1'])
                            V(lambda e, Bv=Bv, cb=cb: e.tensor_tensor(out=t2[:], in0=Bv, in1=cb, op=ALU.mult), [P(pi_), 'cosT'], ['t2'])
                            V(lambda e, dst=dst, h0=h0: e.tensor_tensor(out=dst[:, h0:h0 + 2, 1, :], in0=t1[:], in1=t2[:], op=ALU.add), ['t1', 't2'], [dk])
                            if g < 2:
                                V(lambda e, h0=h0: e.tensor_tensor(out=QTD[:, h0:h0 + 2, :, :], in0=QT[:, h0:h0 + 2, :, :],
                                                                   in1=qdec[:, h0 * 256:(h0 + 2) * 256].rearrange('p (h a t) -> p h a t', h=2, a=2), op=ALU.mult),
                                  ['QT', 'qdec'], ['QTD'])
                        elif g in (4, 5):
                            A(lambda e, ps=ps, g=g: e.copy(out=Vb[:, (g - 4) * 512:(g - 3) * 512], in_=ps[:, :]), [P(pi_)], ['Vb'])
                        elif g in (6, 7):
                            A(lambda e, ps=ps, g=g: e.activation(out=SG[:, (g - 6) * 512:(g - 5) * 512], in_=ps[:, :], func=AF.Silu), [P(pi_)], ['SG'])
                        elif g in (8, 9):
                            A(lambda e, ps=ps, g=g: e.copy(out=UT[:, (g - 8) * 512:(g - 7) * 512], in_=ps[:, :]), [P(pi_)], ['UT'])
                        elif g in (10, 11):
                            hf = g - 10
                            A(lambda e, ps=ps: e.activation(out=sgT[:], in_=ps[:, :], func=AF.Sigmoid), [P(pi_)], ['sgT'])
                            V(lambda e, hf=hf: e.tensor_tensor(out=ybuf[:, hf * 4:(hf + 1) * 4, 30:158], in0=UT[:, hf * 512:(hf + 1) * 512].rearrange('p (c t) -> p c t', c=4),
                                                               in1=sgT[:].rearrange('p (c t) -> p c t', c=4), op=ALU.mult), ['UT', 'sgT'], ['ybuf'])
                        elif g in (12, 13):
                            A(lambda e, ps=ps, g=g: e.activation(out=SA_[:, (g - 12) * 512:(g - 11) * 512], in_=ps[:, :], func=AF.Sigmoid), [P(pi_)], ['SA'])
                        else:
                            A(lambda e, ps=ps, g=g: e.activation(out=SBg[:, (g - 14) * 512:(g - 13) * 512], in_=ps[:, :], func=AF.Sigmoid), [P(pi_)], ['SBg'])
                    pS = nextps()
                    for h in range(4):
                        for ab in range(2):
                            T(lambda e, h=h, ab=ab, pS=pS: e.matmul(PS[pS][:, h * 128:(h + 1) * 128], lhsT=KT[:, h, ab, :], rhs=QT[:, h, ab, :],
                                                                   start=(ab == 0), stop=(ab == 1)), ['KT', 'QT'], [P(pS)])
                    V(lambda e, pS=pS: e.tensor_tensor(out=SmT[:], in0=PS[pS][:, :], in1=maskT[:], op=ALU.mult), [P(pS), 'maskT'], ['SmT'])
                    pK = nextps(); psb = PS[pK][:].bitcast(BF16)
                    for h in range(4):
                        for ab in range(2):
                            c = h * 2 + ab
                            T(lambda e, h=h, ab=ab, c=c, psb=psb: e.transpose(out=psb[:, c * 128:(c + 1) * 128], in_=KT[:, h, ab, :], identity=identb[:]),
                              ['KT', 'identb'], [P(pK)])
                    V(lambda e, psb=psb: e.tensor_tensor(out=Kd[:], in0=psb.rearrange('p (h f) -> p h f', h=4),
                                                         in1=kdec[:].unsqueeze(2).to_broadcast([128, 4, 256]), op=ALU.mult), [P(pK), 'kdec'], ['Kd'])
                    for hp in range(2):
                        pO = nextps()
                        for hl in range(2):
                            h = hp * 2 + hl
                            reg = PS[pO][:, hl * 256:(hl + 1) * 256]
                            T(lambda e, reg=reg, h=h: e.matmul(reg, lhsT=SmT[:, h * 128:(h + 1) * 128], rhs=Vb[:, h * 256:(h + 1) * 256], start=True, stop=False),
                              ['SmT', 'Vb'], [P(pO)])
                            for ab in range(2):
                                T(lambda e, reg=reg, h=h, ab=ab: e.matmul(reg, lhsT=QTD[:, h, ab, :], rhs=STB[:, h, ab, :], start=False, stop=(ab == 1)),
                                  ['QTD', 'STB'], [P(pO)])
                        for hl in range(2):
                            h = hp * 2 + hl
                            reg = PS[pO][:, hl * 256:(hl + 1) * 256]
                            V(lambda e, reg=reg, h=h: e.bn_stats(out=st6h[:, h, :], in_=reg), [P(pO)], ['st6h'])
                            V(lambda e, h=h: e.bn_aggr(out=mvh[:, h, :], in_=st6h[:, h, :]), ['st6h'], ['mvh'])
                            A(lambda e, h=h: e.activation(out=rstdh[:, h:h + 1], in_=mvh[:, h, 1:2], func=AF.Sqrt, bias=epsc[:, 0:1], scale=1.0), ['mvh', 'epsc'], ['rstdh']); V(lambda e, h=h: e.reciprocal(out=rstdh[:, h:h + 1], in_=rstdh[:, h:h + 1]), ['rstdh'], ['rstdh'])
                            V(lambda e, reg=reg, h=h: e.tensor_scalar(out=retn[:, h * 256:(h + 1) * 256], in0=reg, scalar1=mvh[:, h, 0:1], scalar2=rstdh[:, h:h + 1],
                                                                       op0=ALU.subtract, op1=ALU.mult), [P(pO), 'mvh', 'rstdh'], ['retn'])
                    for h in range(4):
                        pU = nextps()
                        for ab in range(2):
                            T(lambda e, pU=pU, h=h, ab=ab: e.matmul(PS[pU][:, ab * 256:(ab + 1) * 256], lhsT=Kd[:, h, ab * 128:(ab + 1) * 128], rhs=Vb[:, h * 256:(h + 1) * 256],
                                                                   start=True, stop=True), ['Kd', 'Vb'], [P(pU)])
                        V(lambda e, pU=pU, h=h: e.scalar_tensor_tensor(out=STF[:, h, :], in0=STF[:, h, :], scalar=HC['gam128'][h], in1=PS[pU][:, :], op0=ALU.mult, op1=ALU.add),
                          [P(pU), 'STF'], ['STF'])
                        A(lambda e, h=h: e.copy(out=STB[:, h, :, :], in_=STF[:, h, :].rearrange('p (a f) -> p a f', a=2)), ['STF'], ['STB'])
                    G(lambda e: e.tensor_tensor(out=MR[:], in0=SG[:], in1=SA_[:], op=ALU.mult), ['SG', 'SA'], ['MR'])
                    V(lambda e: e.tensor_tensor(out=MR[:], in0=MR[:], in1=retn[:], op=ALU.mult), ['MR', 'retn'], ['MR'])
                    pcs = []
                    for hf in range(2):
                        pC = nextps(); pcs.append(pC)
                        for cl in range(4):
                            cc = hf * 4 + cl
                            reg = PS[pC][:, cl * 128:(cl + 1) * 128]
                            for j in range(31):
                                T(lambda e, reg=reg, cc=cc, j=j: e.matmul(reg, lhsT=ybuf[:, cc, j:j + 128], rhs=DG[:, cc, j, :], start=(j == 0), stop=False),
                                  ['ybuf', 'DG'], [P(pC)])
                            T(lambda e, reg=reg, cc=cc: e.matmul(reg, lhsT=ones[0:1, :], rhs=dwbr[0:1, cc * 128:(cc + 1) * 128], start=False, stop=True),
                              ['ones', 'dwbr'], [P(pC)])
                        V(lambda e, pC=pC, hf=hf: e.bn_stats(out=st6[:, hf, :], in_=PS[pC][:, :]), [P(pC)], ['st6'])
                    G(lambda e: e.tensor_copy(out=ybuf[:, :, 0:30], in_=ybuf[:, :, 128:158]), ['ybuf'], ['ybuf'])
                    V(lambda e: e.bn_aggr(out=mv[:], in_=st6[:]), ['st6'], ['mv'])
                    A(lambda e: e.activation(out=rstd[:], in_=mv[:, 1:2], func=AF.Sqrt, bias=epsc[:, 0:1], scale=1.0), ['mv', 'epsc'], ['rstd']); V(lambda e: e.reciprocal(out=rstd[:], in_=rstd[:]), ['rstd'], ['rstd'])
                    for hf in range(2):
                        V(lambda e, hf=hf, pcs=pcs: e.tensor_scalar(out=z[:, hf * 512:(hf + 1) * 512], in0=PS[pcs[hf]][:, :], scalar1=mv[:, 0:1], scalar2=rstd[:, 0:1],
                                                           op0=ALU.subtract, op1=ALU.mult), [P(pcs[hf]), 'mv', 'rstd'], ['z'])
                    for hf in range(2):
                        pZ = nextps()
                        for j in range(4):
                            k = hf * 4 + j
                            T(lambda e, pZ=pZ, j=j, k=k: e.transpose(out=PS[pZ][:, j * 128:(j + 1) * 128], in_=z[:, k * 128:(k + 1) * 128], identity=identf[:]),
                              ['z', 'identf'], [P(pZ)])
                        for j in range(4):
                            k = hf * 4 + j
                            V(lambda e, pZ=pZ, j=j, k=k: e.tensor_scalar(out=xn[:, k * 128:(k + 1) * 128], in0=PS[pZ][:, j * 128:(j + 1) * 128], scalar1=clng[:, k:k + 1], scalar2=clnb[:, k:k + 1],
                                                                          op0=ALU.mult, op1=ALU.add), [P(pZ), 'clng', 'clnb'], ['xn'])
                    A(lambda e: e.activation(out=sT[:], in_=xn[:].rearrange('p (k t) -> p k t', k=KD), func=AF.Silu), ['xn'], ['sT'])
                    for hf in range(2):
                        pD = nextps()
                        for cc in range(KD):
                            T(lambda e, pD=pD, cc=cc, hf=hf: e.matmul(PS[pD][:, :], lhsT=sT[:, cc, :], rhs=WCO[:, cc, hf * 512:(hf + 1) * 512], start=(cc == 0), stop=False),
                              ['sT', 'WCO'], [P(pD)])
                        T(lambda e, pD=pD, hf=hf: e.matmul(PS[pD][:, :], lhsT=ones[0:1, :], rhs=bcor[0:1, hf * 512:(hf + 1) * 512], start=False, stop=True),
                          ['ones', 'bcor'], [P(pD)])
                        V(lambda e, pD=pD, hf=hf: e.tensor_tensor(out=tmpw[:, hf * 512:(hf + 1) * 512], in0=PS[pD][:, :], in1=SBg[:, hf * 512:(hf + 1) * 512], op=ALU.mult),
                          [P(pD), 'SBg'], ['SG'])
                    G(lambda e: e.tensor_tensor(out=MG[:], in0=tmpw[:], in1=MR[:], op=ALU.add), ['SG', 'MR'], ['z'])
                    for hf in range(2):
                        pM = nextps()
                        for j in range(4):
                            k = hf * 4 + j
                            T(lambda e, pM=pM, j=j, k=k: e.transpose(out=PS[pM][:, j * 128:(j + 1) * 128], in_=MG[:, k * 128:(k + 1) * 128], identity=identf[:]),
                              ['z', 'identf'], [P(pM)])
                        A(lambda e, pM=pM, hf=hf: e.copy(out=MT[:, hf * 4:(hf + 1) * 4, :], in_=PS[pM][:, :].rearrange('p (c t) -> p c t', c=4)), [P(pM)], ['MT'])
                    for hf in range(2):
                        pX = nextps()
                        for k in range(KD):
                            T(lambda e, pX=pX, k=k, hf=hf: e.matmul(PS[pX][:, :], lhsT=MT[:, k, :], rhs=WO[:, k, hf * 512:(hf + 1) * 512], start=(k == 0), stop=(k == KD - 1)),
                              ['MT', 'WO'], [P(pX)])
                        V(lambda e, pX=pX, hf=hf: e.tensor_tensor(out=tmpw[:, hf * 512:(hf + 1) * 512], in0=PS[pX][:, :], in1=g1B[:, hf * 512:(hf + 1) * 512], op=ALU.mult),
                          [P(pX), 'g1B'], ['SG'])
                    V(lambda e: e.scalar_tensor_tensor(out=x1p[:], in0=xt[:], scalar=ALPHA, in1=tmpw[:], op0=ALU.mult, op1=ALU.add), [XK, 'SG'], ['retn'])
                    for hf in range(2):
                        V(lambda e, hf=hf: e.bn_stats(out=st6[:, hf, :], in_=x1p[:, hf * 512:(hf + 1) * 512]), ['retn'], ['st6'])
                    V(lambda e: e.bn_aggr(out=mv[:], in_=st6[:]), ['st6'], ['mv'])
                    A(lambda e: e.activation(out=rstd[:], in_=mv[:, 1:2], func=AF.Sqrt, bias=epsc[:, 0:1], scale=1.0), ['mv', 'epsc'], ['rstd']); V(lambda e: e.reciprocal(out=rstd[:], in_=rstd[:]), ['rstd'], ['rstd'])
                    V(lambda e: e.tensor_scalar(out=x1p[:], in0=x1p[:], scalar1=mv[:, 0:1], scalar2=rstd[:, 0:1], op0=ALU.subtract, op1=ALU.mult),
                      ['retn', 'mv', 'rstd'], ['retn'])
                    dma(x1_d[g0:g0 + 128, :], x1p[:], reads=['retn'], semkey='x1o')
            R.barrier()
        def tt(eng, out, in0, in1, op, reads, writes):
            return R.op(eng, lambda e: e.tensor_tensor(out=out, in0=in0, in1=in1, op=op), reads, writes)

        def ts(eng, out, in0, s1, s2, op0, op1, reads, writes):
            if op1 is None:
                return R.op(eng, lambda e: e.tensor_scalar(out=out, in0=in0, scalar1=s1, scalar2=None, op0=op0), reads, writes)
            return R.op(eng, lambda e: e.tensor_scalar(out=out, in0=in0, scalar1=s1, scalar2=s2, op0=op0, op1=op1), reads, writes)

        def stt(out, in0, sc, in1, op0, op1, reads, writes):
            return R.op('vector', lambda e: e.scalar_tensor_tensor(out=out, in0=in0, scalar=sc, in1=in1, op0=op0, op1=op1), reads, writes)

        def act(out, in_, func, reads, writes, bias=None):
            if bias is None:
                return R.op('scalar', lambda e: e.activation(out=out, in_=in_, func=func), reads, writes)
            return R.op('scalar', lambda e: e.activation(out=out, in_=in_, func=func, bias=bias, scale=1.0), reads, writes)

        def cp(eng, out, in_, reads, writes):
            if eng == 'scalar':
                return R.op('scalar', lambda e: e.copy(out=out, in_=in_), reads, writes)
            return R.op(eng, lambda e: e.tensor_copy(out=out, in_=in_), reads, writes)

        def mm(out, lhsT, rhs, start, stop, reads, writes):
            return R.op('tensor', lambda e: e.matmul(out, lhsT=lhsT, rhs=rhs, start=start, stop=stop), reads, writes)

        def tr(out, in_, ident, reads, writes):
            return R.op('tensor', lambda e: e.transpose(out=out, in_=in_, identity=ident), reads, writes)

        def ln_stats(src, skey, st6_, mv_, rstd_, pfx):
            for hf in range(2):
                R.op('vector', (lambda hf: lambda e: e.bn_stats(out=st6_[:, hf, :], in_=src[:, hf * 512:(hf + 1) * 512]))(hf), [skey], [pfx + 'st6'])
            R.op('vector', lambda e: e.bn_aggr(out=mv_[:], in_=st6_[:]), [pfx + 'st6'], [pfx + 'mv'])
            act(rstd_[:], mv_[:, 1:2], AF.Sqrt, [pfx + 'mv', 'epsc'], [pfx + 'rstd'], bias=epsc[:, 0:1])
            R.op('vector', lambda e: e.reciprocal(out=rstd_[:], in_=rstd_[:]), [pfx + 'rstd'], [pfx + 'rstd'])

        with ExitStack() as pb:
            def sB(name, shape, dt): return sbt(pb, 'b1_' + name, shape, dt)
            WQ = sB('WQ', [128, KD, 2048], BF16); SKT = sB('SKT', [128, 16, 128], BF16)
            ln1gB = sB('ln1gB', [128, D], F32); ln1bB = sB('ln1bB', [128, D], F32)
            x1t = sB('x1t', [128, D], F32); h2T = sB('h2T', [128, KD, 128], BF16)
            sc_ = sB('sc', [128, 16, 128], F32)
            top_ = sB('top', [128, 16, 16], F32)
            cand = sB('cand', [128, 8, 16, 16], F32); best = sB('best', [128, 8, 16], F32)
            db = sB('db', [128, 8, 16], F32); Zs = sB('Zs', [128, 8], F32); rZ = sB('rZ', [128, 8], F32); nb0 = sB('nb0', [128, 8], F32)
            Lb = [sB('Lb%d' % i, [128, 16, 128], F32) for i in range(2)]; Eb1 = sB('Eb1', [128, 16, 128], F32)
            xn2 = Eb1[:].rearrange('p a b -> p (a b)')[:, 0:D]
            qTs = Lb[1][:].rearrange('p a b -> p (a b)').bitcast(BF16)[:, 0:2048].rearrange('p (c t) -> p c t', c=16)
            Ebs = [cand[:].rearrange('p h a b -> p (h a) b').rearrange('p (x y) b -> p x (y b)', x=16), Eb1[:]]
            Wa = sB('Wa', [128, 8, 16, 128], BF16); Wb = sB('Wb', [128, 8, 16, 128], BF16)
            AT = sB('AT', [128, 128, 64], BF16); BT = sB('BT', [128, 128, 64], BF16)
            Gs = sB('Gs', [128, 128, 64], BF16)
            st6b = sB('st6', [128, 2, 6], F32); mvb = sB('mv', [128, 2], F32); rstdb = sB('rstd', [128, 1], F32)
            stgq = Lb[0][:].rearrange('p a b -> p (a b)')
            for k in range(KD):
                dma(stgq, wq_d[k * 128:(k + 1) * 128, :], writes=[('Lb', 0)], semkey='stgq')
                cp('vector', WQ[:, k, :], stgq, [('Lb', 0)], ['WQ'])
            for c2 in range(16):
                dma(stgq[:, 0:128], skT_d[c2, :, :], writes=[('Lb', 0)], semkey='stgq')
                cp('vector', SKT[:, c2, :], stgq[:, 0:128], [('Lb', 0)], ['SKT'])
            dma(ln1gB[:], ln1g_d.partition_broadcast(128), writes=['ln1gB'], semkey='b1c1')
            dma(ln1bB[:], ln1b_d.partition_broadcast(128), writes=['ln1bB'], semkey='b1c2')
            def b1_front(b, t):
                g0 = b * S + t * 128; tile_i = b * NT + t
                dma(x1t[:], x1_d[g0:g0 + 128, :], writes=['x1t'], semkey='x1t')
                tt('gpsimd', x1t[:], x1t[:], ln1gB[:], ALU.mult, ['x1t', 'ln1gB'], ['x1t'])
                tt('gpsimd', x1t[:], x1t[:], ln1bB[:], ALU.add, ['x1t', 'ln1bB'], ['x1t'])
                dma(x1_d[g0:g0 + 128, :], x1t[:], reads=['x1t'], semkey='x1tw')
                ln_stats(x1t, 'x1t', st6b, mvb, rstdb, 'b1')
                ts('vector', xn2, x1t[:], mvb[:, 0:1], rstdb[:, 0:1], ALU.subtract, ALU.mult, ['x1t', 'b1mv', 'b1rstd'], [('Eb', 1)])
                for hf in range(2):
                    pi_ = nextps()
                    for j in range(4):
                        k = hf * 4 + j
                        tr(PS[pi_][:, j * 128:(j + 1) * 128], xn2[:, k * 128:(k + 1) * 128], identf[:], [('Eb', 1), 'identf'], [P(pi_)])
                    for j in range(4):
                        k = hf * 4 + j
                        ts('vector', h2T[:, k, :], PS[pi_][:, j * 128:(j + 1) * 128], sc2p[:, k, b:b + 1], modT[:, 24 + k, b:b + 1], ALU.mult, ALU.add,
                           [P(pi_), 'sc2p', 'modT'], ['h2T'])
                dma(h2T_d[tile_i], h2T[:], reads=['h2T'], semkey='h2Tw')
                for c0 in range(0, 16, 4):
                    pi_ = nextps()
                    for cl in range(4):
                        c = c0 + cl
                        for k in range(KD):
                            mm(PS[pi_][:, cl * 128:(cl + 1) * 128], WQ[:, k, c * 128:(c + 1) * 128], h2T[:, k, :], k == 0, k == KD - 1, ['WQ', 'h2T'], [P(pi_)])
                    cp('scalar', qTs[:, c0:c0 + 4, :], PS[pi_][:, :].rearrange('p (c t) -> p c t', c=4), [P(pi_)], [('Lb', 1)])
                for c0 in range(0, 16, 4):
                    pi_ = nextps()
                    for cl in range(4):
                        c = c0 + cl
                        mm(PS[pi_][:, cl * 128:(cl + 1) * 128], qTs[:, c, :], SKT[:, c, :], True, True, [('Lb', 1), 'SKT'], [P(pi_)])
                    cp('scalar', sc_[:, c0:c0 + 4, :], PS[pi_][:, :].rearrange('p (c t) -> p c t', c=4), [P(pi_)], ['sc'])
                th = []
                for c in range(16):
                    th.append((lambda c: lambda: R.op('vector', lambda e: e.max(out=top_[:, c, 0:8], in_=sc_[:, c, :]), ['sc'], [('top', c)]))(c))
                for c in range(16):
                    th.append((lambda c: lambda: R.op('vector', lambda e: e.match_replace(out=Lb[0][:, c, :], in_to_replace=top_[:, c, 0:8], in_values=sc_[:, c, :], imm_value=NEG),
                                                     ['sc', ('top', c)], [('wk', c), ('Lb', 0)]))(c))
                for c in range(16):
                    th.append((lambda c: lambda: R.op('vector', lambda e: e.max(out=top_[:, c, 8:16], in_=Lb[0][:, c, :]), [('wk', c)], [('top', c), 'top']))(c))
                topv = top_[:].rearrange('p (h s) i -> p h s i', s=2)
                th.append(lambda: tt('vector', cand[:], topv[:, :, 0, :].unsqueeze(3).to_broadcast([128, 8, 16, 16]),
                                     topv[:, :, 1, :].unsqueeze(2).to_broadcast([128, 8, 16, 16]), ALU.add, ['top'], ['cand', ('Eb', 0)]))
                Lw = Lb[1][:].rearrange('p a b -> p (a b)').rearrange('p (h x) -> p h x', h=8)
                for h in range(8):
                    th.append((lambda h: lambda: R.op('vector', lambda e: e.max(out=best[:, h, 0:8], in_=cand[:, h, :, :].rearrange('p a b -> p (a b)')), ['cand'], [('best', h)]))(h))
                for h in range(8):
                    th.append((lambda h: lambda: R.op('vector', lambda e: e.match_replace(out=Lw[:, h, :], in_to_replace=best[:, h, 0:8],
                                                                                           in_values=cand[:, h, :, :].rearrange('p a b -> p (a b)'), imm_value=NEG),
                                                     ['cand', ('best', h)], [('wk2', h), ('Lb', 1)]))(h))
                for h in range(8):
                    th.append((lambda h: lambda: R.op('vector', lambda e: e.max(out=best[:, h, 8:16], in_=Lw[:, h, :]), [('wk2', h)], [('best', h), 'best']))(h))
                return th

            def b1_s4(b, t):
                topv = top_[:].rearrange('p (h s) i -> p h s i', s=2)
                tt('vector', db[:], best[:], best[:, :, 0:1].to_broadcast([128, 8, 16]), ALU.subtract, ['best'], ['db'])
                act(db[:], db[:], AF.Exp, ['db'], ['db'])
                R.op('vector', lambda e: e.tensor_reduce(out=Zs[:], in_=db[:], axis=mybir.AxisListType.X, op=ALU.add), ['db'], ['Zs'])
                act(rZ[:], Zs[:], AF.Ln, ['Zs'], ['rZ'])
                stt(nb0[:], best[:, :, 0], -1.0, rZ[:], ALU.mult, ALU.subtract, ['best', 'rZ'], ['nb0'])
                sv = sc_[:].rearrange('p (h s) k -> p h s k', s=2)
                tt('vector', Wb[:], sv[:, :, 1, :].unsqueeze(2).to_broadcast([128, 8, 16, 128]),
                   topv[:, :, 1, :].unsqueeze(3).to_broadcast([128, 8, 16, 128]), ALU.is_equal, ['sc', 'top'], ['Wb'])
                def addL(h):
                    q_ = h % 2
                    tt('vector', Lb[q_][:], sc_[:, 2 * h, :].unsqueeze(1).to_broadcast([128, 16, 128]),
                       top_[:, 2 * h + 1, :].unsqueeze(2).to_broadcast([128, 16, 128]), ALU.add, ['sc', 'top'], [('Lb', q_)])
                addL(0)
                for h in range(8):
                    q_ = h % 2
                    if h + 1 < 8:
                        addL(h + 1)
                    act(Ebs[q_], Lb[q_][:], AF.Exp, [('Lb', q_), 'nb0'], [('Eb', q_)] + (['cand'] if q_ == 0 else []), bias=nb0[:, h:h + 1])
                    stt(Wa[:, h, :, :], Lb[q_][:], best[:, h, 15:16], Ebs[q_], ALU.is_ge, ALU.mult, [('Lb', q_), 'best', ('Eb', q_)], ['Wa'])
            def b1_s5(b, t, filler):
                tile_i = b * NT + t

                def fill(n):
                    for _ in range(n):
                        if filler:
                            filler.pop(0)()
                Wav = Wa[:].rearrange('p h i k -> p (h i) k'); Wbv = Wb[:].rearrange('p h i k -> p (h i) k')
                for half in range(2):
                    p0_, p1_ = half * 64, (half + 1) * 64
                    for (Wv, dstT, wk, dk) in ((Wbv, BT, 'Wb', 'BT'), (Wav, AT, 'Wa', 'AT')):
                        for kb in range(8):
                            pi_ = nextps(); psb = PS[pi_][:].bitcast(BF16)
                            for kl in range(16):
                                kk_ = kb * 16 + kl
                                tr(psb[:, kl * 64:(kl + 1) * 64], Wv[p0_:p1_, :, kk_], identb[p0_:p1_, p0_:p1_], [wk, 'identb'], [P(pi_)])
                            if kb % 2 == 0:
                                cp('scalar', dstT[:, kb * 16:(kb + 1) * 16, :], psb.rearrange('p (k t) -> p k t', k=16), [P(pi_)], [dk])
                            else:
                                cp('vector', dstT[:, kb * 16:(kb + 1) * 16, :].rearrange('p k t -> p (k t)').bitcast(I32), PS[pi_][:].bitcast(I32), [P(pi_)], [dk])
                            fill(2)
                    for t0 in range(0, 64, 4):
                        pi_ = nextps()
                        for tl in range(4):
                            tk = t0 + tl
                            mm(PS[pi_][:, tl * 128:(tl + 1) * 128], BT[:, :, tk], AT[:, :, tk], True, True, ['BT', 'AT'], [P(pi_)])
                        cp('scalar', Gs[:, :, t0:t0 + 4], PS[pi_][:, :].rearrange('p (t k) -> p k t', t=4), [P(pi_)], ['Gs'])
                        fill(1)
                    dma(G_d[tile_i * 2 + half], Gs[:], reads=['Gs'], semkey='Gw')
            vv2 = v_d.rearrange('(j p) d -> p j d', p=128)
            cvl = []
            for k in range(KD):
                for e8 in range(8):
                    cvl.append((uT_d[k * 128:(k + 1) * 128, e8 * 2048:(e8 + 1) * 2048], ubf_d[:, k, e8 * 2048:(e8 + 1) * 2048]))
            for j1 in range(128):
                cvl.append((vv2[:, j1, :], vbf_d[:, j1, :]))
            cvi = [0]

            def cv_issue(n):
                for _ in range(n):
                    if cvi[0] < len(cvl):
                        src, dst = cvl[cvi[0]]; cvi[0] += 1
                        dma(dst, src, semkey='cvcast', eng='gpsimd')
            tiles_ = [(b, t) for b in range(NB) for t in range(NT)]
            th0 = b1_front(*tiles_[0])
            for f_ in th0:
                f_()
            for ii, bt in enumerate(tiles_):
                b1_s4(*bt)
                filler = b1_front(*tiles_[ii + 1]) if ii + 1 < len(tiles_) else []
                b1_s5(*bt, filler)
                cv_issue(3)
                while filler:
                    filler.pop(0)()
            cv_issue(len(cvl))
            R.barrier()

        finals = []
        GT = min(8, NT)
        with ExitStack() as pc:
            def sC(name, shape, dt): return sbt(pc, 'b2_' + name, shape, dt)
            Ub = [sC('Ub%d' % i, [128, KD, 1024], BF16) for i in range(2)]
            Vbk = [sC('Vb%d' % i, [128, 8, 1024], BF16) for i in range(2)]
            Gb = [sC('Gb%d' % i, [128, 2 * GT, 8, 64], BF16) for i in range(2)]
            H2s = [sC('H2_%d' % i, [128, KD, GT * 128], BF16) for i in range(2)]
            acc = [sC('acc%d' % i, [128, D], F32) for i in range(GT)]
            ln2gB = sC('ln2gB', [128, D], F32); ln2bB = sC('ln2bB', [128, D], F32)
            g2Bs = [sC('g2B%d' % i, [128, D], F32) for i in range(2)]
            gl = [sC('gl%d' % i, [128, 128], BF16) for i in range(4)]; PT = [sC('PT%d' % i, [128, 128], BF16) for i in range(4)]
            xrs = [sC('xr%d' % i, [128, D], F32) for i in range(2)]; yps = [sC('yp%d' % i, [128, D], F32) for i in range(2)]
            st6c = sC('st6', [128, 2, 6], F32); mvc = sC('mv', [128, 2], F32); rstdc = sC('rstd', [128, 1], F32)
            dma(ln2gB[:], ln2g_d.partition_broadcast(128), writes=['ln2gB'], semkey='b2c1')
            dma(ln2bB[:], ln2b_d.partition_broadcast(128), writes=['ln2bB'], semkey='b2c2')
            pslo[0] = 4; psc[0] = 0
            bc = [0]; cc_ = [0]; pc_ = [0]
            ylast = [None, None]

            def final_thunks(tile0, tl, gq):
                g0 = (tile0 + tl) * 128
                g2 = g2Bs[gq]; gk = ('g2B', gq)
                q_ = tl % 2; xr = xrs[q_]; yp = yps[q_]; xk = ('xr', q_); yk = ('yp', q_)
                L = []
                L.append(lambda: dma(xr[:], x1_d[g0:g0 + 128, :], writes=[xk], semkey=xk))
                L.append(lambda: tt('gpsimd', acc[tl][:], acc[tl][:], g2[:], ALU.mult, [('acc', tl), gk], [('acc', tl)]))
                L.append(lambda: stt(yp[:], xr[:], ALPHA, acc[tl][:], ALU.mult, ALU.add, [xk, ('acc', tl)], [yk]))
                L.append(lambda: ln_stats(yp, yk, st6c, mvc, rstdc, 'b2'))
                L.append(lambda: ts('vector', yp[:], yp[:], mvc[:, 0:1], rstdc[:, 0:1], ALU.subtract, ALU.mult, [yk, 'b2mv', 'b2rstd'], [yk]))
                L.append(lambda: tt('gpsimd', yp[:], yp[:], ln2gB[:], ALU.mult, [yk, 'ln2gB'], [yk]))
                L.append(lambda: tt('gpsimd', yp[:], yp[:], ln2bB[:], ALU.add, [yk, 'ln2bB'], [yk]))

                def st_():
                    ylast[q_] = dma(y_d[g0:g0 + 128, :], yp[:], reads=[yk], semkey=('yout', q_))
                L.append(st_)
                return [(tl, f_) for f_ in L]

            def load_H2(gidx_):
                b_, gi_ = groups[gidx_]
                t0_ = b_ * NT + gi_ * GT
                for tl in range(GT):
                    dma(H2s[gidx_ % 2][:, :, tl * 128:(tl + 1) * 128], h2T_d[t0_ + tl], writes=[('H2', gidx_ % 2, tl)], semkey=('H2', gidx_ % 2, tl))
            groups = [(b, gi) for b in range(NB) for gi in range(NT // GT)]
            pfin = []

            def flush_upto(tl):
                while pfin and pfin[0][0] <= tl:
                    pfin.pop(0)[1]()
            load_H2(0)
            for gidx, (b, gi) in enumerate(groups):
                tile0 = b * NT + gi * GT
                H2 = H2s[gidx % 2]; hb = gidx % 2
                if gi == 0:
                    dma(g2Bs[b % 2][:], mod_d[b, 5120:6144].partition_broadcast(128), writes=[('g2B', b % 2)], semkey=('g2B', b % 2))
                if gidx + 1 < len(groups):
                    load_H2(gidx + 1)
                pend = []

                def vstage(ci, bi, j, pp, tl, blk):
                    for dh in range(2):
                        mm(PS[pp * 2 + dh][:, :], PT[ci][:, :], Vbk[bi][:, j, dh * 512:(dh + 1) * 512], j == 0, j == 7,
                           [('PT', ci), ('Vbk', bi)], [P(pp * 2 + dh)])
                    if j == 7:
                        if blk == 0:
                            flush_upto(tl)
                        for dh in range(2):
                            if blk == 0:
                                cp('vector', acc[tl][:, dh * 512:(dh + 1) * 512], PS[pp * 2 + dh][:, :], [P(pp * 2 + dh)], [('acc', tl)])
                            else:
                                tt('vector', acc[tl][:, dh * 512:(dh + 1) * 512], PS[pp * 2 + dh][:, :], acc[tl][:, dh * 512:(dh + 1) * 512], ALU.add,
                                   [P(pp * 2 + dh), ('acc', tl)], [('acc', tl)])
                for blk in range(16):
                    bi = bc[0] % 2; bc[0] += 1
                    dma(Ub[bi][:], ubf_d[:, :, blk * 1024:(blk + 1) * 1024], writes=[('Ub', bi)], semkey=('Ub', bi))
                    dma(Vbk[bi][:], vbf_d[:, blk * 8:(blk + 1) * 8, :], writes=[('Vbk', bi)], semkey=('Vbk', bi))
                    dma(Gb[bi][:].rearrange('p h k t -> p h (k t)'),
                        G_d[tile0 * 2:(tile0 + GT) * 2, :, blk * 8:(blk + 1) * 8, :].rearrange('h p k t -> p h (k t)'),
                        writes=[('Gb', bi)], semkey=('Gb', bi))
                    for tl in range(GT):
                        pp = pc_[0] % 2; pc_[0] += 1
                        for j in range(8):
                            ci = cc_[0] % 4; cc_[0] += 1
                            pa = nextps()
                            for k in range(KD):
                                mm(PS[pa][:, 0:128], Ub[bi][:, k, j * 128:(j + 1) * 128], H2[:, k, tl * 128:(tl + 1) * 128], k == 0, k == KD - 1,
                                   [('Ub', bi), ('H2', hb, tl)], [P(pa)])
                            act(gl[ci][:], PS[pa][:, 0:128], AF.Gelu, [P(pa)], [('gl', ci)])
                            tt('vector', PT[ci][:].rearrange('p (q t) -> p q t', q=2), gl[ci][:].rearrange('p (q t) -> p q t', q=2),
                               Gb[bi][:, 2 * tl:2 * tl + 2, j, :], ALU.mult, [('gl', ci), ('Gb', bi)], [('PT', ci)])
                            pend.append((ci, bi, j, pp, tl, blk))
                            if len(pend) > 2:
                                vstage(*pend.pop(0))
                            if pfin:
                                pfin.pop(0)[1]()
                while pend:
                    vstage(*pend.pop(0))
                flush_upto(GT)
                for tl in range(GT):
                    pfin.extend(final_thunks(tile0, tl, b % 2))
            flush_upto(GT)
            finals = [y_ for y_ in ylast if y_ is not None]
        R.emit(top, finals)
    return nc


def make_in_maps(inputs, n_cores, NB):
    HC = host_consts()
    f = lambda a: np.ascontiguousarray(np.asarray(a))
    shared = {
        'w_ada': f(inputs['w_ada'][0]), 'b_ada': f(np.asarray(inputs['b_ada'][0]).reshape(48, 128).T), 'b_ada_row': f(np.asarray(inputs['b_ada'][0])[None, :]), 'w_in': f(inputs['w_in'][0]),
        'conv_dwT': f(np.asarray(inputs['conv_dw'][0]).T), 'conv_dw_b': f(np.asarray(inputs['conv_dw_b'][0])[None, :]),
        'conv_ln_g': f(np.asarray(inputs['conv_ln_g'][0]).reshape(KD, 128).T), 'conv_ln_b': f(np.asarray(inputs['conv_ln_b'][0]).reshape(KD, 128).T),
        'w_conv_out': f(inputs['w_conv_out'][0]), 'b_conv_out': f(np.asarray(inputs['b_conv_out'][0])[None, :]),
        'w_out': f(inputs['w_out'][0]), 'ln1_g': f(inputs['ln1_g'][0]), 'ln1_b': f(inputs['ln1_b'][0]),
        'ln2_g': f(inputs['ln2_g'][0]), 'ln2_b': f(inputs['ln2_b'][0]), 'peer_wq': f(inputs['peer_wq'][0]),
        'skT': f(np.asarray(inputs['peer_subkeys'][0]).reshape(16, 128, 128).transpose(0, 2, 1)),
        'peer_uT': f(np.asarray(inputs['peer_u'][0]).T), 'peer_v': f(inputs['peer_v'][0]),
        'ident': HC['ident'], 'iota': HC['iota'], 'invf': HC['invf'], 'maskT': HC['maskT'], 'qdec': HC['qdec'], 'kdec': HC['kdec'],
    }
    x = np.asarray(inputs['x']); c = np.asarray(inputs['c']); pos = np.asarray(inputs['positions'])
    S = x.shape[1]
    maps = []
    for i in range(n_cores):
        m = dict(shared)
        m['x'] = f(x[i * NB:(i + 1) * NB].reshape(NB * S, D))
        m['cT'] = f(c[i * NB:(i + 1) * NB].T)
        m['pos'] = f(pos[i * NB:(i + 1) * NB].astype(np.int32))
        maps.append(m)
    return maps


def kernel(**inputs):
    n_cores = 8; NB = 2
    S = np.asarray(inputs['x']).shape[1]
    nc = build(NB, S)
    maps = make_in_maps(inputs, n_cores, NB)
    res = run_bass_kernel_spmd(nc, maps, core_ids=list(range(n_cores)))
    out = np.concatenate([np.asarray(r['y']).reshape(NB, S, D) for r in res.results], axis=0)
    return out.astype(np.float32)
```
